# Optimizing a Trainium2 kernel written in Bass

```python
import math
import jax
import jax.numpy as jnp
from jax import lax
import numpy as np

D_MODEL = 2048
BATCH = 4
SEQ = 2048
DEPTH = 2

EPS = 1e-6
EXPAND = 2
MIX = EXPAND * D_MODEL
N_EVEN = (DEPTH + 1) // 2
N_ODD = DEPTH // 2

CONV_W = MIX // 2
CONV_K = 3

HEAD_DIM = 128
N_HEADS = (MIX // 2) // HEAD_DIM
N_KV = 4
GQA = N_HEADS // N_KV
ROT_DIM = HEAD_DIM // 4
ROPE_THETA = 500000.0
CMP_LEN = 32
CMP_STRIDE = 16
SLC_LEN = 64
N_SEL = 8
WINDOW = 512
Q_BLOCK = 64

L0_SIZES = (CONV_W, CONV_W, CONV_W, CONV_W,
            N_HEADS * HEAD_DIM,
            N_KV * HEAD_DIM, N_KV * HEAD_DIM, N_KV * HEAD_DIM,
            N_KV * HEAD_DIM, N_KV * HEAD_DIM, N_KV * HEAD_DIM,
            3 * N_HEADS,
            N_HEADS * HEAD_DIM)
L0_IN = 4 * CONV_W + 2 * N_HEADS * HEAD_DIM + 6 * N_KV * HEAD_DIM + 3 * N_HEADS

SGU_W = MIX
CHUNK = 128
N_GROUPS = 16
GROUP_W = SGU_W // N_GROUPS

kernel_name = "hybrid_conv_nsa_chunked_gmlp"


def rms_norm(x, g):
    x32 = x.astype(jnp.float32)
    y = x32 * lax.rsqrt(jnp.mean(x32 * x32, axis=-1, keepdims=True) + EPS)
    return (y * g.astype(jnp.float32)).astype(x.dtype)


def layer_norm(x, g, b):
    x32 = x.astype(jnp.float32)
    mu = jnp.mean(x32, axis=-1, keepdims=True)
    xc = x32 - mu
    y = xc * lax.rsqrt(jnp.mean(xc * xc, axis=-1, keepdims=True) + EPS)
    return (y * g.astype(jnp.float32) + b.astype(jnp.float32)).astype(x.dtype)


def masked_softmax(s, mask):
    s = jnp.where(mask, s.astype(jnp.float32), -jnp.inf)
    m = jnp.max(s, axis=-1, keepdims=True)
    m = jnp.where(jnp.isfinite(m), m, 0.0)
    e = jnp.exp(s - m)
    den = jnp.sum(e, axis=-1, keepdims=True)
    return e / jnp.where(den > 0, den, 1.0)


def partial_rotary(x, pos):
    half = ROT_DIM // 2
    inv_freq = jnp.power(ROPE_THETA, -jnp.arange(half, dtype=jnp.float32) * 2.0 / ROT_DIM)
    ang = pos.astype(jnp.float32)[:, None] * inv_freq[None, :]
    cos = jnp.cos(ang).astype(x.dtype)
    sin = jnp.sin(ang).astype(x.dtype)
    x1 = x[..., :half]
    x2 = x[..., half:ROT_DIM]
    return jnp.concatenate([x1 * cos - x2 * sin, x2 * cos + x1 * sin, x[..., ROT_DIM:]], axis=-1)


def split_cols(z, sizes):
    offs = []
    acc = 0
    for s in sizes[:-1]:
        acc += s
        offs.append(acc)
    return jnp.split(z, offs, axis=-1)


def short_conv(h, w):
    return lax.conv_general_dilated(
        h, w[:, None, :].astype(h.dtype), window_strides=(1,),
        padding=[(CONV_K - 1, 0)], dimension_numbers=("NWC", "WIO", "NWC"),
        feature_group_count=h.shape[-1])


def compress(k, pos_emb, w1, b1, w2):
    T = k.shape[2]
    n_cmp = (T - CMP_LEN) // CMP_STRIDE + 1
    idx = jnp.arange(n_cmp)[:, None] * CMP_STRIDE + jnp.arange(CMP_LEN)[None, :]
    blocks = k[:, :, idx] + pos_emb
    flat = blocks.reshape(blocks.shape[0], blocks.shape[1], n_cmp, CMP_LEN * HEAD_DIM)
    return jax.nn.silu(flat @ w1 + b1) @ w2


def nsa_mixer(q, k_cmp, v_cmp, k_slc, v_slc, k_win, v_win, gate_logits,
              ck_pos, ck_w1, ck_b1, ck_w2, cv_pos, cv_w1, cv_b1, cv_w2):
    Bsz, T = q.shape[0], q.shape[1]
    pos = jnp.arange(T)
    scale = HEAD_DIM ** -0.5
    q = q.reshape(Bsz, T, N_KV, GQA, HEAD_DIM).transpose(0, 2, 3, 1, 4)

    def kv(a):
        return a.reshape(Bsz, T, N_KV, HEAD_DIM).transpose(0, 2, 1, 3)

    k_cmp, v_cmp, k_slc, v_slc, k_win, v_win = (kv(a) for a in (k_cmp, v_cmp, k_slc, v_slc, k_win, v_win))
    q_rot = partial_rotary(q, pos)
    k_slc = partial_rotary(k_slc, pos)
    k_win = partial_rotary(k_win, pos)

    kc = compress(k_cmp, ck_pos, ck_w1, ck_b1, ck_w2)
    vc = compress(v_cmp, cv_pos, cv_w1, cv_b1, cv_w2)
    n_cmp = kc.shape[2]
    cmp_start = jnp.arange(n_cmp) * CMP_STRIDE
    cmp_mask = (cmp_start + CMP_LEN - 1)[None, :] <= pos[:, None]
    s_c = jnp.einsum("bgntd,bgcd->bgntc", q, kc) * scale
    p_c = masked_softmax(s_c, cmp_mask)
    o_c = jnp.einsum("bgntc,bgcd->bgntd", p_c.astype(vc.dtype), vc)

    n_blk = T // SLC_LEN
    n_sel = min(N_SEL, n_blk)
    blk = jnp.arange(n_blk)
    blk_start = blk * SLC_LEN
    overlap = ((cmp_start[:, None] < blk_start[None, :] + SLC_LEN)
               & (cmp_start[:, None] + CMP_LEN > blk_start[None, :])).astype(jnp.float32)
    imp = jnp.einsum("bgtc,cj->bgtj", jnp.sum(p_c, axis=2), overlap)
    forced = (blk[None, :] == 0) | (blk[None, :] == (pos // SLC_LEN)[:, None])
    causal_blk = blk_start[None, :] <= pos[:, None]
    imp = jnp.where(forced, jnp.inf, jnp.where(causal_blk, imp, -jnp.inf))
    sel_val, sel_idx = lax.top_k(imp, n_sel)
    sel_ok = sel_val > -jnp.inf

    n_qb = T // Q_BLOCK
    q_blocks = jnp.moveaxis(q_rot.reshape(Bsz, N_KV, GQA, n_qb, Q_BLOCK, HEAD_DIM), 3, 0)
    idx_blocks = jnp.moveaxis(sel_idx.reshape(Bsz, N_KV, n_qb, Q_BLOCK, n_sel), 2, 0)
    ok_blocks = jnp.moveaxis(sel_ok.reshape(Bsz, N_KV, n_qb, Q_BLOCK, n_sel), 2, 0)

    kb = k_slc.reshape(Bsz, N_KV, n_blk, SLC_LEN, HEAD_DIM)
    vb = v_slc.reshape(Bsz, N_KV, n_blk, SLC_LEN, HEAD_DIM)
    take = jax.vmap(jax.vmap(lambda a, i: a[i]))
    k_win_pad = jnp.pad(k_win, ((0, 0), (0, 0), (WINDOW, 0), (0, 0)))
    v_win_pad = jnp.pad(v_win, ((0, 0), (0, 0), (WINDOW, 0), (0, 0)))
    offs = jnp.arange(SLC_LEN)
    n_keys = n_sel * SLC_LEN

    def block_step(args):
        qb, ib, okb, i = args
        start = i * Q_BLOCK
        tb = start + jnp.arange(Q_BLOCK)
        kg = take(kb, ib).reshape(Bsz, N_KV, Q_BLOCK, n_keys, HEAD_DIM)
        vg = take(vb, ib).reshape(Bsz, N_KV, Q_BLOCK, n_keys, HEAD_DIM)
        kpos = (ib[..., None] * SLC_LEN + offs).reshape(Bsz, N_KV, Q_BLOCK, n_keys)
        okk = jnp.broadcast_to(okb[..., None], okb.shape + (SLC_LEN,)).reshape(Bsz, N_KV, Q_BLOCK, n_keys)
        smask = (okk & (kpos <= tb[:, None]))[:, :, None]
        s_s = jnp.einsum("bgnqd,bgqmd->bgnqm", qb, kg) * scale
        p_s = masked_softmax(s_s, smask)
        o_s = jnp.einsum("bgnqm,bgqmd->bgnqd", p_s.astype(vg.dtype), vg)
        kw = lax.dynamic_slice_in_dim(k_win_pad, start, WINDOW + Q_BLOCK, axis=2)
        vw = lax.dynamic_slice_in_dim(v_win_pad, start, WINDOW + Q_BLOCK, axis=2)
        wpos = start - WINDOW + jnp.arange(WINDOW + Q_BLOCK)
        diff = tb[:, None] - wpos[None, :]
        wmask = (wpos[None, :] >= 0) & (diff >= 0) & (diff < WINDOW)
        s_w = jnp.einsum("bgnqd,bgkd->bgnqk", qb, kw) * scale
        p_w = masked_softmax(s_w, wmask)
        o_w = jnp.einsum("bgnqk,bgkd->bgnqd", p_w.astype(vw.dtype), vw)
        return (o_s, o_w)

    o_s, o_w = lax.map(block_step, (q_blocks, idx_blocks, ok_blocks, jnp.arange(n_qb)))
    o_s = jnp.moveaxis(o_s, 0, 3).reshape(Bsz, N_KV, GQA, T, HEAD_DIM)
    o_w = jnp.moveaxis(o_w, 0, 3).reshape(Bsz, N_KV, GQA, T, HEAD_DIM)

    g = jax.nn.sigmoid(gate_logits.astype(jnp.float32)).astype(q.dtype)
    g = g.reshape(Bsz, T, 3, N_KV, GQA).transpose(2, 0, 3, 4, 1)[..., None]
    o = g[0] * o_c + g[1] * o_s + g[2] * o_w
    return o.transpose(0, 3, 1, 2, 4).reshape(Bsz, T, N_HEADS * HEAD_DIM)


def conv_nsa_layer(h, w_in, conv_w, ck_pos, ck_w1, ck_b1, ck_w2,
                   cv_pos, cv_w1, cv_b1, cv_w2, w_out):
    z = h @ w_in
    (cb, cc, ch, cg, q, kc, vc, ks, vs, kw, vw, gl, ng) = split_cols(z, L0_SIZES)
    y_conv = cb * short_conv(cc * ch, conv_w) * jax.nn.silu(cg)
    y_nsa = nsa_mixer(q, kc, vc, ks, vs, kw, vw, gl,
                      ck_pos, ck_w1, ck_b1, ck_w2, cv_pos, cv_w1, cv_b1, cv_w2) * jax.nn.silu(ng)
    return jnp.concatenate([y_conv, y_nsa], axis=-1) @ w_out


def chunked_gmlp_layer(h, w_in, ln_g, ln_b, w_s, b_s, w_out):
    Bsz, T = h.shape[0], h.shape[1]
    u, v, zg = jnp.split(h @ w_in, 3, axis=-1)
    v = layer_norm(v, ln_g, ln_b)
    v = v.reshape(Bsz, T // CHUNK, CHUNK, N_GROUPS, GROUP_W)
    tri = jnp.tril(jnp.ones((CHUNK, CHUNK), dtype=bool))
    ws = jnp.where(tri, w_s, 0.0)
    mix = jnp.einsum("hts,bcshd->bcthd", ws, v) + b_s.T[None, None, :, :, None]
    mix = mix.reshape(Bsz, T, SGU_W)
    return (u * mix * jax.nn.silu(zg)) @ w_out


def setup_inputs(seed: int = 0) -> dict:
    key = jax.random.key(seed)
    ks = jax.random.split(key, 24)
    f32 = jnp.float32
    ne, no = N_EVEN, N_ODD

    def nrm(k, shape, scale):
        return jax.random.normal(k, shape, f32) * scale

    return {
        "x": nrm(ks[0], (BATCH, SEQ, D_MODEL), 1.0),
        "norm_even": 1.0 + nrm(ks[1], (ne, D_MODEL), 0.02),
        "w_in_even": nrm(ks[2], (ne, D_MODEL, L0_IN), D_MODEL ** -0.5),
        "conv_w": nrm(ks[3], (ne, CONV_K, CONV_W), CONV_K ** -0.5),
        "cmp_k_pos": nrm(ks[4], (ne, CMP_LEN, HEAD_DIM), 0.1),
        "cmp_k_w1": nrm(ks[5], (ne, CMP_LEN * HEAD_DIM, HEAD_DIM), (CMP_LEN * HEAD_DIM) ** -0.5),
        "cmp_k_b1": nrm(ks[6], (ne, HEAD_DIM), 0.01),
        "cmp_k_w2": nrm(ks[7], (ne, HEAD_DIM, HEAD_DIM), HEAD_DIM ** -0.5),
        "cmp_v_pos": nrm(ks[8], (ne, CMP_LEN, HEAD_DIM), 0.1),
        "cmp_v_w1": nrm(ks[9], (ne, CMP_LEN * HEAD_DIM, HEAD_DIM), (CMP_LEN * HEAD_DIM) ** -0.5),
        "cmp_v_b1": nrm(ks[10], (ne, HEAD_DIM), 0.01),
        "cmp_v_w2": nrm(ks[11], (ne, HEAD_DIM, HEAD_DIM), HEAD_DIM ** -0.5),
        "w_out_even": nrm(ks[12], (ne, MIX, D_MODEL), MIX ** -0.5),
        "norm_odd": 1.0 + nrm(ks[13], (no, D_MODEL), 0.02),
        "w_in_odd": nrm(ks[14], (no, D_MODEL, 3 * SGU_W), D_MODEL ** -0.5),
        "sgu_ln_g": 1.0 + nrm(ks[15], (no, SGU_W), 0.02),
        "sgu_ln_b": nrm(ks[16], (no, SGU_W), 0.01),
        "sgu_w_s": nrm(ks[17], (no, N_GROUPS, CHUNK, CHUNK), CHUNK ** -0.5),
        "sgu_b_s": 1.0 + nrm(ks[18], (no, N_GROUPS, CHUNK), 0.02),
        "w_out_odd": nrm(ks[19], (no, SGU_W, D_MODEL), SGU_W ** -0.5),
        "norm_final": 1.0 + nrm(ks[20], (D_MODEL,), 0.02),
    }


def reference(x, norm_even, w_in_even, conv_w, cmp_k_pos, cmp_k_w1, cmp_k_b1, cmp_k_w2,
              cmp_v_pos, cmp_v_w1, cmp_v_b1, cmp_v_w2, w_out_even,
              norm_odd, w_in_odd, sgu_ln_g, sgu_ln_b, sgu_w_s, sgu_b_s, w_out_odd,
              norm_final):
    for layer in range(DEPTH):
        j = layer // 2
        if layer % 2 == 0:
            h = rms_norm(x, norm_even[j])
            x = x + conv_nsa_layer(h, w_in_even[j], conv_w[j],
                                   cmp_k_pos[j], cmp_k_w1[j], cmp_k_b1[j], cmp_k_w2[j],
                                   cmp_v_pos[j], cmp_v_w1[j], cmp_v_b1[j], cmp_v_w2[j],
                                   w_out_even[j])
        else:
            h = rms_norm(x, norm_odd[j])
            x = x + chunked_gmlp_layer(h, w_in_odd[j], sgu_ln_g[j], sgu_ln_b[j],
                                       sgu_w_s[j], sgu_b_s[j], w_out_odd[j])
    return rms_norm(x, norm_final)
```

```python
from contextlib import ExitStack
import numpy as np
import ml_dtypes
import concourse.bass as bass
import concourse.mybir as mybir
from concourse.bass_utils import run_bass_kernel_spmd

F32 = mybir.dt.float32
BF16 = mybir.dt.bfloat16
ALU = mybir.AluOpType
AF = mybir.ActivationFunctionType
AX = mybir.AxisListType

D = 2048
T = 2048
TO = 1024
L0 = 15408
OFF = dict(cb=0, cc=2048, ch=4096, cg=6144, q=8192, kc=10240, vc=10752, ks=11264, vs=11776, kw=12288,
           vw=12800, gl=13312, ng=13360)
SCALE = 128 ** -0.5
EPS = 1e-6
BIG = 1e30


class Dep:
    __slots__ = ("w", "r", "excl")

    def __init__(self, excl=False):
        self.w = {}
        self.r = {}
        self.excl = excl


class Ctx:
    DMA_POOL = 12

    def __init__(self, nc, same_engine_sync=True):
        self.nc = nc
        self.same = same_engine_sync
        self.eng = {"pe": nc.tensor, "act": nc.scalar, "dve": nc.vector, "pool": nc.gpsimd, "sp": nc.sync}
        self.sem = {}
        self.cnt = {}
        self.seen = {k: {} for k in self.eng}
        self._stack = []
        for k in self.eng:
            cm = nc.semaphore("s_" + k)
            self.sem[k] = cm.__enter__()
            self._stack.append(cm)
            self.cnt[k] = 0
        self.dpool = {}
        self.dpos = {}
        for q in ("sp", "pool", "act"):
            lst = []
            for i in range(self.DMA_POOL):
                cm = nc.semaphore("d_%s%d" % (q, i))
                lst.append([cm.__enter__(), 0])
                self._stack.append(cm)
            self.dpool[q] = lst
            self.dpos[q] = 0

    def close(self):
        for cm in reversed(self._stack):
            cm.__exit__(None, None, None)

    def _wait(self, e, tickets):
        E = self.eng[e]
        seen = self.seen[e]
        own = id(self.sem[e])
        for sid, (sem, val) in tickets.items():
            if seen.get(sid, 0) >= val:
                continue
            if sid == own and (e == "pe" or not self.same):
                continue
            E.wait_ge(sem, val)
            seen[sid] = val

    @staticmethod
    def _merge(dst, src):
        for sid, tv in src.items():
            if sid not in dst or dst[sid][1] < tv[1]:
                dst[sid] = tv

    def _collect(self, reads, writes, own=None):
        t = {}
        for d in reads:
            self._merge(t, d.w)
            if d.excl:
                self._merge(t, {k: v for k, v in d.r.items() if k != own})
        for d in writes:
            if d.r:
                self._merge(t, d.r)
        return t

    def _record(self, reads, writes, ticket):
        tk = {id(ticket[0]): ticket}
        for d in writes:
            if d.r:
                d.w = dict(tk)
                d.r = {}
            else:
                self._merge(d.w, tk)
        for d in reads:
            self._merge(d.r, tk)

    def op(self, e, fn, reads=(), writes=()):
        self._wait(e, self._collect(reads, writes, id(self.sem[e])))
        inst = fn(self.eng[e])
        self.cnt[e] += 1
        inst.then_inc(self.sem[e], 1)
        self._record(reads, writes, (self.sem[e], self.cnt[e]))
        return inst

    def dma(self, q, out, in_, reads=(), writes=(), **kw):
        self._wait(q, self._collect(reads, writes))
        slot = self.dpool[q][self.dpos[q] % self.DMA_POOL]
        self.dpos[q] += 1
        E = self.eng[q]
        if slot[1] > 0 and self.seen[q].get(id(slot[0]), 0) < slot[1]:
            E.wait_ge(slot[0], slot[1])
            self.seen[q][id(slot[0])] = slot[1]
        inst = E.dma_start(out=out, in_=in_, **kw)
        slot[1] += 16
        inst.then_inc(slot[0], 16)
        self._record(reads, writes, (slot[0], slot[1]))
        return inst

    def barrier(self):
        t = {}
        for k in self.eng:
            if self.cnt[k]:
                t[id(self.sem[k])] = (self.sem[k], self.cnt[k])
        for q in self.dpool:
            for slot in self.dpool[q]:
                if slot[1]:
                    t[id(slot[0])] = (slot[0], slot[1])
        for e in self.eng:
            E = self.eng[e]
            seen = self.seen[e]
            for sid, (sem, val) in t.items():
                if sid == id(self.sem[e]) or seen.get(sid, 0) >= val:
                    continue
                E.wait_ge(sem, val)
                seen[sid] = val


def build(debug=False, stop=99):
    nc = bass.Bass("TRN2", target_bir_lowering=False)
    c = Ctx(nc)

    def din(name, shape, dt=F32):
        return nc.dram_tensor(name, list(shape), dt, kind="ExternalInput").ap()

    def dscr(name, shape, dt):
        return nc.dram_tensor(name, list(shape), dt, kind=("ExternalOutput" if debug else "Internal")).ap()

    x_all = din("x_all", [T, D])
    x_own = din("x_own", [TO, D])
    x_halo = din("x_halo", [128, D])
    norm_even = din("norm_even", [1, D])
    w_in_even = din("w_in_even", [D, L0])
    conv_w = din("conv_w", [48, 128])
    kpos = din("cmp_k_pos", [32, 128])
    kw1 = din("cmp_k_w1", [4096, 128])
    kb1 = din("cmp_k_b1", [128, 1])
    kw2 = din("cmp_k_w2", [128, 128])
    vpos = din("cmp_v_pos", [32, 128])
    vw1 = din("cmp_v_w1", [4096, 128])
    vb1 = din("cmp_v_b1", [128, 1])
    vw2 = din("cmp_v_w2", [128, 128])
    w_out_even = din("w_out_even", [4096, D])
    norm_odd = din("norm_odd", [1, D])
    w_in_odd = din("w_in_odd", [D, 12288])
    ln_g = din("sgu_ln_g", [32, 128])
    ln_b = din("sgu_ln_b", [32, 128])
    w_s = din("sgu_w_s", [16, 128, 128])
    b_s = din("sgu_b_s", [1, 2048])
    w_out_odd = din("w_out_odd", [4096, D])
    norm_final = din("norm_final", [1, D])
    t_cos_all = din("t_cos_all", [128, T])
    t_sin_all = din("t_sin_all", [128, T])
    t_cos_own = din("t_cos_own", [128, TO])
    t_sin_own = din("t_sin_own", [128, TO])
    t_R = din("t_R", [128, 128], BF16)
    t_ident = din("t_ident", [128, 128], BF16)
    t_identf = din("t_identf", [128, 128], F32)
    t_ones = din("t_ones", [128, 128], BF16)
    t_addmask = din("t_addmask", [128, 8 * 32])
    t_ncmask = din("t_ncmask", [128, 8 * 512], BF16)
    t_overlap = din("t_overlap", [128, 32], BF16)
    t_expand = din("t_expand", [128, T], BF16)
    t_ndmask = din("t_ndmask", [128, 4 * 512], BF16)
    t_nwmask = din("t_nwmask", [128, 12 * 512], BF16)
    t_sel48 = din("t_sel48", [128, 48 * 128], BF16)
    t_tril = din("t_tril", [128, 128])

    out = nc.dram_tensor("out", [TO, D], F32, kind="ExternalOutput").ap()

    s_kc = dscr("s_kc", [4, 128, T], BF16)
    s_vc = dscr("s_vc", [4, 128, T], BF16)
    s_ks = dscr("s_ks", [4, 128, T], BF16)
    s_kw = dscr("s_kw", [4, 128, T], BF16)
    s_vs = dscr("s_vs", [128, 16, 512], BF16)
    s_vw = dscr("s_vw", [128, 16, 512], BF16)
    s_q = dscr("s_q", [16, 128, TO], BF16)
    s_qr = dscr("s_qr", [16, 128, TO], BF16)
    s_ng = dscr("s_ng", [16, 128, TO], BF16)
    s_gt = dscr("s_gt", [48, TO], BF16)
    s_y = dscr("s_y", [128, 32, TO], BF16)
    s_x1 = dscr("s_x1", [TO, D], F32)
    s_y2 = dscr("s_y2", [32, 128, TO], BF16)
    s_dbg = dscr("s_dbg", [128, 2048], F32)
    s_dbgb = dscr("s_dbgb", [128, 1024], BF16)

    top = ExitStack()

    def sbt(st, name, shape, dt):
        return st.enter_context(nc.sbuf_tensor(name, list(shape), dt))

    ps = [top.enter_context(nc.psum_tensor("ps%d" % i, [128, 512], F32)) for i in range(7)]
    Dps = [Dep(excl=True) for _ in range(7)]
    psb = top.enter_context(nc.psum_tensor("psb", [128, 1024], BF16))
    Dpsb = Dep(excl=True)

    ident = sbt(top, "ident", [128, 128], BF16)
    identf = sbt(top, "identf", [128, 128], F32)
    ones = sbt(top, "ones", [128, 128], BF16)
    epsc = sbt(top, "epsc", [128, 1], F32)
    Dconst = Dep()
    c.dma("sp", ident[:], t_ident, writes=[Dconst])
    c.dma("sp", identf[:], t_identf, writes=[Dconst])
    c.dma("sp", ones[:], t_ones, writes=[Dconst])
    c.op("dve", lambda e: e.memset(epsc[:], EPS), writes=[Dconst])

    act_flip = [0]

    def evac(out_ap, in_ap, reads, writes):
        act_flip[0] ^= 1
        if act_flip[0]:
            c.op("act", lambda e: e.copy(out=out_ap, in_=in_ap), reads, writes)
        else:
            c.op("dve", lambda e: e.tensor_copy(out=out_ap, in_=in_ap), reads, writes)

    def norm_parts(st, x_src, gain_src, hT, DhT, prefix):
        gbc = sbt(st, prefix + "gbc", [128, D], F32)
        xt = [sbt(st, prefix + "xt%d" % i, [128, D], F32) for i in range(2)]
        hb = [sbt(st, prefix + "hb%d" % i, [128, D], BF16) for i in range(2)]
        stat = sbt(st, prefix + "stat", [128, 4], F32)
        Dg, Dst = Dep(), Dep()
        Dxt = [Dep(), Dep()]
        Dhb = [Dep(), Dep()]
        c.dma("sp", gbc[:], gain_src.to_broadcast([128, D]), writes=[Dg])
        nr = 128

        def partA(t):
            b = t % 2
            c.dma("sp", xt[b][0:nr, :], x_src[t * nr:(t + 1) * nr, :], writes=[Dxt[b]])
            c.op("act", lambda e: e.activation(out=hb[b][0:nr, :], in_=xt[b][0:nr, :], func=AF.Square,
                                               accum_out=stat[0:nr, 0:1]), reads=[Dxt[b]], writes=[Dhb[b], Dst])
            c.op("act", lambda e: e.activation(out=stat[0:nr, 1:2], in_=stat[0:nr, 0:1], func=AF.Sqrt,
                                               scale=1.0 / D, bias=epsc[0:nr, :]), reads=[Dst, Dconst], writes=[Dst])
            c.op("dve", lambda e: e.reciprocal(out=stat[0:nr, 2:3], in_=stat[0:nr, 1:2]), reads=[Dst], writes=[Dst])
            c.op("dve", lambda e: e.scalar_tensor_tensor(out=hb[b][0:nr, :], in0=xt[b][0:nr, :], scalar=stat[0:nr, 2:3],
                                                         in1=gbc[0:nr, :], op0=ALU.mult, op1=ALU.mult),
                 reads=[Dxt[b], Dst, Dg, Dhb[b]], writes=[Dhb[b]])

        def partB(t):
            b = t % 2
            Dt = DhT[t] if isinstance(DhT, list) else DhT
            for a in range(4):
                tgt, Dtgt = (psb[:], Dpsb) if a % 2 == 0 else (ps[6][:].bitcast(BF16), Dps[6])
                for j in range(4):
                    kc = 4 * a + j
                    c.op("pe", lambda e: e.transpose(out=tgt[:, j * nr:(j + 1) * nr], in_=hb[b][0:nr, kc * 128:(kc + 1) * 128],
                                                     identity=ident[0:nr, 0:nr]), reads=[Dhb[b], Dconst], writes=[Dtgt])
                evac(hT[:, 4 * a:4 * a + 4, t * nr:(t + 1) * nr],
                     tgt[:, 0:4 * nr].rearrange("p (a b) -> p a b", b=nr), [Dtgt], [Dt])
        return partA, partB

    def norm_transpose(st, x_src, ntiles, gain_src, hT, DhT, prefix):
        pa, pb = norm_parts(st, x_src, gain_src, hT, DhT, prefix)
        for t in range(ntiles):
            pa(t)
            pb(t)

    class WS:
        def __init__(self, st, name, n=3, cols=512):
            self.t = [sbt(st, "%s%d" % (name, i), [128, 16, cols], BF16) for i in range(n)]
            self.d = [Dep() for _ in range(n)]
            self.i = 0

        def load(self, w_ap, pieces):
            k = self.i % len(self.t)
            self.i += 1
            wv = w_ap.rearrange("(kc p) n -> p kc n", p=128)
            for (co, ncol, dc) in pieces:
                for h in range(4):
                    c.dma("pool", self.t[k][:, 4 * h:4 * h + 4, dc:dc + ncol], wv[:, 4 * h:4 * h + 4, co:co + ncol],
                          writes=[self.d[k]])
            return self.t[k], self.d[k]

    psrot = [0]

    def next_ps(n=5):
        psrot[0] = (psrot[0] + 1) % n
        return psrot[0]

    def mm_group(pi, col0, ncols, lhs_fn, rhs_fn, reads, M=128, nk=16):
        for kc in range(nk):
            c.op("pe", lambda e: e.matmul(ps[pi][0:M, col0:col0 + ncols], lhsT=lhs_fn(kc), rhs=rhs_fn(kc),
                                          start=(kc == 0), stop=(kc == nk - 1)), reads=reads, writes=[Dps[pi]])

    def rope_epi(pi, ncols, plainS, Dplain, rotS, Drot, scol, cosT, sinT, tcol, Dtab, Rm, tmp, Dtmp, tmpb, Dtmpb):
        P = ps[pi]
        if plainS is not None:
            src, Dsrc, so = plainS, Dplain, scol
        else:
            src, Dsrc, so = tmpb, Dtmpb, 0
        c.op("act", lambda e: e.copy(out=src[:, so:so + ncols], in_=P[:, 0:ncols]), reads=[Dps[pi]], writes=[Dsrc])
        c.op("pe", lambda e: e.matmul(ps[5][:, 0:ncols], lhsT=Rm[:, :], rhs=src[:, so:so + ncols],
                                      start=True, stop=True), reads=[Dsrc, Dtab], writes=[Dps[5]])
        import os
        RL = int(os.environ.get("RL", "9"))
        if RL < 3:
            return
        c.op("dve", lambda e: e.tensor_tensor(out=tmp[:, 0:ncols], in0=P[:, 0:ncols], in1=cosT[:, tcol:tcol + ncols],
                                              op=ALU.mult), reads=[Dps[pi], Dtab], writes=[Dtmp])
        if RL < 4:
            return
        c.op("dve", lambda e: e.tensor_tensor(out=tmp[:, 512:512 + ncols], in0=ps[5][:, 0:ncols],
                                              in1=sinT[:, tcol:tcol + ncols], op=ALU.mult),
             reads=[Dps[5], Dtab, Dtmp], writes=[Dtmp])
        if RL < 5:
            return
        c.op("dve", lambda e: e.tensor_tensor(out=rotS[:, scol:scol + ncols], in0=tmp[:, 0:ncols],
                                              in1=tmp[:, 512:512 + ncols], op=ALU.add), reads=[Dtmp], writes=[Drot])

    if stop >= 1:
        with ExitStack() as st:
            hT = sbt(st, "hTall", [128, 16, T], BF16)
            DhT = [Dep() for _ in range(16)]
            pA, pB = norm_parts(st, x_all, norm_even, hT, DhT, "a1")
            ws = WS(st, "a1w")
            cosT = sbt(st, "a1cos", [128, T], F32)
            sinT = sbt(st, "a1sin", [128, T], F32)
            Rm = sbt(st, "a1R", [128, 128], BF16)
            tmp = sbt(st, "a1tmp", [128, 1024], F32)
            tmpb = sbt(st, "a1tmpb", [128, 512], BF16)
            Dtmpb = Dep()
            Dtab, Dtmp = Dep(), Dep()
            stg = [sbt(st, "a1stg%d" % i, [128, T], BF16) for i in range(4)]
            Dstg = [Dep() for _ in range(4)]
            vst = [sbt(st, "a1vst%d" % i, [128, 512], BF16) for i in range(4)]
            Dvst = [Dep() for _ in range(4)]
            for t in range(4):
                pA(t)
                pB(t)
            pA(4)
            c.dma("sp", cosT[:], t_cos_all, writes=[Dtab])
            c.dma("sp", sinT[:], t_sin_all, writes=[Dtab])
            c.dma("sp", Rm[:], t_R, writes=[Dtab])
            nxt = [4]

            def inject():
                n = nxt[0]
                if n <= 15:
                    pB(n)
                    if n + 1 <= 15:
                        pA(n + 1)
                    nxt[0] = n + 1

            wt, Dw = ws.load(w_in_even, [(OFF["kc"], 512, 0)])
            for tq in range(4):
                for g in range(4):
                    inject()
                    pi = next_ps()
                    mm_group(pi, 0, 512, lambda kc: wt[:, kc, g * 128:(g + 1) * 128],
                             lambda kc: hT[:, kc, tq * 512:(tq + 1) * 512], [Dw] + DhT[4 * tq:4 * tq + 4])
                    evac(stg[g][:, tq * 512:(tq + 1) * 512], ps[pi][:, :], [Dps[pi]], [Dstg[g]])
            while nxt[0] <= 15:
                inject()
            for g in range(4):
                c.dma("sp", s_kc[g], stg[g][:], reads=[Dstg[g]], writes=[Dep()])
            si = 0
            for name, dst, rope in (("vc", s_vc, False), ("ks", s_ks, True), ("kw", s_kw, True)):
                wt, Dw = ws.load(w_in_even, [(OFF[name], 512, 0)])
                for g in range(4):
                    S, DS = stg[si % 4], Dstg[si % 4]
                    si += 1
                    for tq in range(4):
                        pi = next_ps()
                        mm_group(pi, 0, 512, lambda kc: wt[:, kc, g * 128:(g + 1) * 128],
                                 lambda kc: hT[:, kc, tq * 512:(tq + 1) * 512], [Dw] + DhT[4 * tq:4 * tq + 4])
                        if rope:
                            rope_epi(pi, 512, None, None, S, DS, tq * 512, cosT, sinT, tq * 512, Dtab, Rm, tmp, Dtmp, tmpb, Dtmpb)
                        else:
                            evac(S[:, tq * 512:(tq + 1) * 512], ps[pi][:, :], [Dps[pi]], [DS])
                    c.dma("sp", dst[g], S[:], reads=[DS], writes=[Dep()])
            vi = 0
            for name, dst in (("vs", s_vs), ("vw", s_vw)):
                wt, Dw = ws.load(w_in_even, [(OFF[name], 512, 0)])
                for tt in range(16):
                    pi = next_ps()
                    mm_group(pi, 0, 512, lambda kc: hT[:, kc, tt * 128:(tt + 1) * 128], lambda kc: wt[:, kc, :], [Dw, DhT[tt]])
                    b = vi % 4
                    vi += 1
                    evac(vst[b][:], ps[pi][:, :], [Dps[pi]], [Dvst[b]])
                    c.dma("sp", dst[:, tt, :], vst[b][:], reads=[Dvst[b]], writes=[Dep()])
            c.barrier()

    if stop >= 2:
        with ExitStack() as st:
            hT = sbt(st, "hTown", [128, 16, TO], BF16)
            hTh = sbt(st, "hThalo", [128, 16, 128], BF16)
            DhT, DhTh = Dep(), Dep()
            with ExitStack() as st2:
                norm_transpose(st2, x_own, 8, norm_even, hT, DhT, "a2")
                c.barrier()
            with ExitStack() as st2:
                norm_transpose(st2, x_halo, 1, norm_even, hTh, DhTh, "a2h")
                c.barrier()
            ws = WS(st, "a2w", n=4)
            cosT = sbt(st, "a2cos", [128, TO], F32)
            sinT = sbt(st, "a2sin", [128, TO], F32)
            Rm = sbt(st, "a2R", [128, 128], BF16)
            tmp = sbt(st, "a2tmp", [128, 1024], F32)
            tmpb = sbt(st, "a2tmpb", [128, 512], BF16)
            Dtmpb = Dep()
            cwr = sbt(st, "a2cwr", [128, 128], F32)
            cw = sbt(st, "a2cw", [128, 48], F32)
            Dtab, Dtmp, Dcw = Dep(), Dep(), Dep()
            c.dma("sp", cosT[:], t_cos_own, writes=[Dtab])
            c.dma("sp", sinT[:], t_sin_own, writes=[Dtab])
            c.dma("sp", Rm[:], t_R, writes=[Dtab])
            Dz = Dep()
            c.op("dve", lambda e: e.memset(cwr[:], 0.0), writes=[Dcw, Dz])
            c.dma("sp", cwr[0:48, :], conv_w, reads=[Dz], writes=[Dcw])
            c.op("pe", lambda e: e.transpose(out=ps[6][:, 0:128], in_=cwr[:, :], identity=identf[:, :]),
                 reads=[Dcw, Dconst], writes=[Dps[6]])
            c.op("dve", lambda e: e.tensor_copy(out=cw[:], in_=ps[6][:, 0:48]), reads=[Dps[6]], writes=[Dcw])
            NB = 2
            ccS = [sbt(st, "ccS%d" % i, [128, TO], F32) for i in range(NB)]
            cbS = [sbt(st, "cbS%d" % i, [128, TO], F32) for i in range(NB)]
            sgS = [sbt(st, "sgS%d" % i, [128, TO], F32) for i in range(NB)]
            uS = [sbt(st, "uS%d" % i, [128, 8, 130], F32) for i in range(NB)]
            acc = [sbt(st, "acc%d" % i, [128, TO], F32) for i in range(NB)]
            yS = [sbt(st, "yS%d" % i, [128, TO], BF16) for i in range(NB)]
            hcc = sbt(st, "hcc", [128, 16], F32)
            Dcc, Dcb, Dsg, Du, Dacc, DyS = [[Dep() for _ in range(NB)] for _ in range(6)]
            Dhcc = Dep()
            for ct in range(16):
                b = ct % NB
                wt, Dw = ws.load(w_in_even, [(OFF["cc"] + ct * 128, 128, 0), (OFF["ch"] + ct * 128, 128, 128),
                                             (OFF["cb"] + ct * 128, 128, 256), (OFF["cg"] + ct * 128, 128, 384)])
                for f in range(2):
                    mm_group(6, 16 * f, 16, lambda kc: wt[:, kc, f * 128:(f + 1) * 128], lambda kc: hTh[:, kc, 0:16], [Dw, DhTh])
                c.op("act", lambda e: e.copy(out=hcc[:], in_=ps[6][:, 0:16]), reads=[Dps[6]], writes=[Dhcc])
                c.op("dve", lambda e: e.tensor_tensor(out=uS[b][:, :, 0:2], in0=hcc[:].rearrange("p (a b) -> p a b", b=2),
                                                      in1=ps[6][:, 16:32].rearrange("p (a b) -> p a b", b=2), op=ALU.mult),
                     reads=[Dhcc, Dps[6]], writes=[Du[b]])
                for th in range(2):
                    tsl = slice(th * 512, (th + 1) * 512)
                    rhs = lambda kc: hT[:, kc, tsl]
                    pi = next_ps()
                    mm_group(pi, 0, 512, lambda kc: wt[:, kc, 0:128], rhs, [Dw, DhT])
                    c.op("act", lambda e: e.copy(out=ccS[b][:, tsl], in_=ps[pi][:, :]), reads=[Dps[pi]], writes=[Dcc[b]])
                    pi = next_ps()
                    mm_group(pi, 0, 512, lambda kc: wt[:, kc, 128:256], rhs, [Dw, DhT])
                    c.op("dve", lambda e: e.tensor_tensor(out=uS[b][:, 4 * th:4 * th + 4, 2:130],
                                                          in0=ccS[b][:, tsl].rearrange("p (a b) -> p a b", b=128),
                                                          in1=ps[pi][:, :].rearrange("p (a b) -> p a b", b=128), op=ALU.mult),
                         reads=[Dcc[b], Dps[pi]], writes=[Du[b]])
                    pi = next_ps()
                    mm_group(pi, 0, 512, lambda kc: wt[:, kc, 256:384], rhs, [Dw, DhT])
                    c.op("act", lambda e: e.copy(out=cbS[b][:, tsl], in_=ps[pi][:, :]), reads=[Dps[pi]], writes=[Dcb[b]])
                    pi = next_ps()
                    mm_group(pi, 0, 512, lambda kc: wt[:, kc, 384:512], rhs, [Dw, DhT])
                    c.op("act", lambda e: e.activation(out=sgS[b][:, tsl], in_=ps[pi][:, :], func=AF.Silu),
                         reads=[Dps[pi]], writes=[Dsg[b]])
                a3 = acc[b][:, :].rearrange("p (a b) -> p a b", b=128)
                c.op("dve", lambda e: e.tensor_scalar(out=a3, in0=uS[b][:, :, 2:130], scalar1=cw[:, 32 + ct:33 + ct], scalar2=None,
                                                      op0=ALU.mult), reads=[Du[b], Dcw], writes=[Dacc[b]])
                c.op("dve", lambda e: e.scalar_tensor_tensor(out=a3, in0=uS[b][:, :, 1:129], scalar=cw[:, 16 + ct:17 + ct], in1=a3,
                                                             op0=ALU.mult, op1=ALU.add), reads=[Du[b], Dcw, Dacc[b]], writes=[Dacc[b]])
                c.op("dve", lambda e: e.scalar_tensor_tensor(out=a3, in0=uS[b][:, :, 0:128], scalar=cw[:, ct:ct + 1], in1=a3,
                                                             op0=ALU.mult, op1=ALU.add), reads=[Du[b], Dcw, Dacc[b]], writes=[Dacc[b]])
                c.op("dve", lambda e: e.tensor_tensor(out=acc[b][:, :], in0=acc[b][:, :], in1=cbS[b][:, :], op=ALU.mult),
                     reads=[Dacc[b], Dcb[b]], writes=[Dacc[b]])
                c.op("dve", lambda e: e.tensor_tensor(out=yS[b][:, :], in0=acc[b][:, :], in1=sgS[b][:, :], op=ALU.mult),
                     reads=[Dacc[b], Dsg[b]], writes=[DyS[b]])
                c.dma("sp", s_y[:, ct, :], yS[b][:], reads=[DyS[b]], writes=[Dep()])
            qS = [sbt(st, "qS%d" % i, [128, TO], BF16) for i in range(2)]
            qrS = [sbt(st, "qrS%d" % i, [128, TO], BF16) for i in range(2)]
            DqS = [Dep(), Dep()]
            DqrS = [Dep(), Dep()]
            si = 0
            for a in range(4):
                wt, Dw = ws.load(w_in_even, [(OFF["q"] + a * 512, 512, 0)])
                for j in range(4):
                    b = si % 2
                    si += 1
                    for th in range(2):
                        pi = next_ps()
                        mm_group(pi, 0, 512, lambda kc: wt[:, kc, j * 128:(j + 1) * 128],
                                 lambda kc: hT[:, kc, th * 512:(th + 1) * 512], [Dw, DhT])
                        rope_epi(pi, 512, qS[b], DqS[b], qrS[b], DqrS[b], th * 512, cosT, sinT, th * 512, Dtab, Rm, tmp, Dtmp, tmpb, Dtmpb)
                    c.dma("sp", s_q[4 * a + j], qS[b][:], reads=[DqS[b]], writes=[Dep()])
                    c.dma("sp", s_qr[4 * a + j], qrS[b][:], reads=[DqrS[b]], writes=[Dep()])
            for a in range(4):
                wt, Dw = ws.load(w_in_even, [(OFF["ng"] + a * 512, 512, 0)])
                for j in range(4):
                    b = si % 2
                    si += 1
                    for th in range(2):
                        pi = next_ps()
                        mm_group(pi, 0, 512, lambda kc: wt[:, kc, j * 128:(j + 1) * 128],
                                 lambda kc: hT[:, kc, th * 512:(th + 1) * 512], [Dw, DhT])
                        c.op("act", lambda e: e.activation(out=qS[b][:, th * 512:(th + 1) * 512], in_=ps[pi][:, :], func=AF.Silu),
                             reads=[Dps[pi]], writes=[DqS[b]])
                    c.dma("sp", s_ng[4 * a + j], qS[b][:], reads=[DqS[b]], writes=[Dep()])
            wt, Dw = ws.load(w_in_even, [(OFF["gl"], 48, 0)])
            b = si % 2
            for th in range(2):
                pi = next_ps()
                mm_group(pi, 0, 512, lambda kc: wt[:, kc, 0:128], lambda kc: hT[:, kc, th * 512:(th + 1) * 512], [Dw, DhT])
                c.op("act", lambda e: e.activation(out=qS[b][0:48, th * 512:(th + 1) * 512], in_=ps[pi][0:48, :], func=AF.Sigmoid),
                     reads=[Dps[pi]], writes=[DqS[b]])
            c.dma("sp", s_gt, qS[b][0:48, :], reads=[DqS[b]], writes=[Dep()])
            c.barrier()

    if stop >= 3:
        with ExitStack() as st:
            kcomp = sbt(st, "kcomp", [128, 4, 128], BF16)
            vcomp = sbt(st, "vcomp", [128, 4, 128], BF16)
            Dkcomp, Dvcomp = Dep(), Dep()
            c.op("dve", lambda e: e.memset(kcomp[:], 0.0), writes=[Dkcomp])
            c.op("dve", lambda e: e.memset(vcomp[:], 0.0), writes=[Dvcomp])
            with ExitStack() as st2:
                w1 = sbt(st2, "bw1", [128, 32, 128], BF16)
                w2 = sbt(st2, "bw2", [128, 128], BF16)
                posr = sbt(st2, "bposr", [128, 128], BF16)
                posT = sbt(st2, "bposT", [128, 128], BF16)
                b1 = sbt(st2, "bb1", [128, 1], F32)
                bias = sbt(st2, "bbias", [128, 1], F32)
                xin = [sbt(st2, "bxin%d" % i, [128, T], BF16) for i in range(2)]
                hid = sbt(st2, "bhid", [128, 128], BF16)
                Dw1, Dw2, Dpos, Db1, Dbias, Dhid = [Dep() for _ in range(6)]
                Dxin = [Dep(), Dep()]
                xi = 0
                c.op("dve", lambda e: e.memset(hid[:], 0.0), writes=[Dhid])
                for kv, (pos_d, w1_d, b1_d, w2_d, src) in enumerate(((kpos, kw1, kb1, kw2, s_kc), (vpos, vw1, vb1, vw2, s_vc))):
                    w1v = w1_d.rearrange("(l d) j -> d l j", d=128)
                    for h in range(8):
                        c.dma("pool", w1[:, 4 * h:4 * h + 4, :], w1v[:, 4 * h:4 * h + 4, :], writes=[Dw1])
                    c.dma("pool", w2[:], w2_d, writes=[Dw2])
                    Dz = Dep()
                    c.op("dve", lambda e: e.memset(posr[:], 0.0), writes=[Dpos, Dz])
                    c.dma("pool", posr[0:32, :], pos_d, reads=[Dz], writes=[Dpos])
                    c.dma("sp", b1[:], b1_d, writes=[Db1])
                    c.op("pe", lambda e: e.transpose(out=psb[:, 0:128], in_=posr[:, :], identity=ident[:, :]),
                         reads=[Dpos, Dconst], writes=[Dpsb])
                    c.op("dve", lambda e: e.tensor_copy(out=posT[:], in_=psb[:, 0:128]), reads=[Dpsb], writes=[Dpos])
                    for l in range(32):
                        c.op("pe", lambda e: e.matmul(ps[6][:, 0:1], lhsT=w1[:, l, :], rhs=posT[:, l:l + 1], start=(l == 0), stop=(l == 31)),
                             reads=[Dw1, Dpos], writes=[Dps[6]])
                    c.op("dve", lambda e: e.tensor_tensor(out=bias[:], in0=ps[6][:, 0:1], in1=b1[:], op=ALU.add),
                         reads=[Dps[6], Db1], writes=[Dbias])
                    for g in range(4):
                        X, DX = xin[xi % 2], Dxin[xi % 2]
                        xi += 1
                        c.dma("sp", X[:], src[g], writes=[DX])
                        pi = next_ps()
                        for l in range(32):
                            c.op("pe", lambda e: e.matmul(ps[pi][:, 0:127], lhsT=w1[:, l, :], rhs=X[:, l:l + 16 * 126 + 1:16],
                                                          start=(l == 0), stop=(l == 31)), reads=[Dw1, DX], writes=[Dps[pi]])
                        c.op("act", lambda e: e.activation(out=hid[:, 0:127], in_=ps[pi][:, 0:127], func=AF.Silu, bias=bias[:]),
                             reads=[Dps[pi], Dbias], writes=[Dhid])
                        pi = next_ps()
                        if kv == 0:
                            c.op("pe", lambda e: e.matmul(ps[pi][:, 0:127], lhsT=w2[:, :], rhs=hid[:, 0:127], start=True, stop=True),
                                 reads=[Dw2, Dhid], writes=[Dps[pi]])
                            c.op("dve", lambda e: e.tensor_copy(out=kcomp[:, g, 0:127], in_=ps[pi][:, 0:127]),
                                 reads=[Dps[pi]], writes=[Dkcomp])
                        else:
                            c.op("pe", lambda e: e.matmul(ps[pi][:, 0:128], lhsT=hid[:, :], rhs=w2[:, :], start=True, stop=True),
                                 reads=[Dw2, Dhid], writes=[Dps[pi]])
                            c.op("dve", lambda e: e.tensor_copy(out=vcomp[0:127, g, :], in_=ps[pi][0:127, 0:128]),
                                 reads=[Dps[pi]], writes=[Dvcomp])
                c.barrier()
            if debug:
                c.dma("sp", s_dbgb[:, 0:512], kcomp[:].rearrange("p a b -> p (a b)"), reads=[Dkcomp], writes=[Dep()])
                c.dma("sp", s_dbgb[:, 512:1024], vcomp[:].rearrange("p a b -> p (a b)"), reads=[Dvcomp], writes=[Dep()])
            if stop >= 4:
                NEGM = 30000.0
                ps.append(psb[:].bitcast(F32))
                Dps.append(Dpsb)
                gates = sbt(st, "gates", [128, TO], BF16)
                sel48 = sbt(st, "sel48", [128, 48 * 128], BF16)
                ncmask = sbt(st, "ncmask", [128, 8 * 512], BF16)
                overlap = sbt(st, "overlap", [128, 32], BF16)
                expand = sbt(st, "expand", [128, T], BF16)
                ndmask = sbt(st, "ndmask", [128, 4 * 512], BF16)
                nwmask = sbt(st, "nwmask", [128, 12 * 512], BF16)
                addm = sbt(st, "addm", [128, 256], F32)
                Dtb = Dep()
                Dz = Dep()
                c.op("dve", lambda e: e.memset(gates[:], 0.0), writes=[Dz])
                c.dma("sp", gates[0:48, :], s_gt, reads=[Dz], writes=[Dtb])
                for dst_, src_ in ((sel48, t_sel48), (ncmask, t_ncmask), (overlap, t_overlap), (expand, t_expand),
                                   (ndmask, t_ndmask), (nwmask, t_nwmask), (addm, t_addmask)):
                    c.dma("sp", dst_[:], src_, writes=[Dtb])
                NG = 2
                qg = [sbt(st, "qg%d" % i, [128, 4, TO], BF16) for i in range(NG)]
                qrg = [sbt(st, "qrg%d" % i, [128, 4, TO], BF16) for i in range(NG)]
                ngg = [sbt(st, "ngg%d" % i, [128, 4, TO], BF16) for i in range(NG)]
                ksg = [sbt(st, "ksg%d" % i, [128, T], BF16) for i in range(NG)]
                kwg = [sbt(st, "kwg%d" % i, [128, T], BF16) for i in range(NG)]
                vsg = [sbt(st, "vsg%d" % i, [128, 16, 128], BF16) for i in range(NG)]
                vwg = [sbt(st, "vwg%d" % i, [128, 16, 128], BF16) for i in range(NG)]
                yst = [sbt(st, "yst%d" % i, [128, 4, TO], BF16) for i in range(NG)]
                Dgl = [Dep() for _ in range(NG)]
                Dyst = [Dep() for _ in range(NG)]
                NE = 6
                Et = [sbt(st, "Et%d" % i, [128, 512], BF16) for i in range(NE)]
                DEt = [Dep() for _ in range(NE)]
                Ec = [sbt(st, "Ec%d" % i, [128, 512], BF16) for i in range(3)]
                Pn = [sbt(st, "Pn%d" % i, [128, 512], BF16) for i in range(3)]
                oacc = [sbt(st, "oacc%d" % i, [128, 512], F32) for i in range(3)]
                otmp = [sbt(st, "otmp%d" % i, [128, 512], F32) for i in range(2)]
                rden0 = [sbt(st, "rden0%d" % i, [128, 512], F32) for i in range(3)]
                rdenB = [sbt(st, "rdenB%d" % i, [128, 512], F32) for i in range(2)]
                gsb = [[sbt(st, "gsb%d_%d" % (i, j), [128, 512], F32) for j in range(3)] for i in range(3)]
                negsel = [sbt(st, "negsel%d" % i, [128, 512], BF16) for i in range(3)]
                imp2 = sbt(st, "imp2", [128, 32], F32)
                top8 = sbt(st, "top8", [128, 8], F32)
                selF = sbt(st, "selF", [128, 128], F32)
                tden = sbt(st, "tden", [128, 512], F32)
                Dimp2, Dtop8, DselF, Dtden = [Dep() for _ in range(4)]
                DEc = [Dep() for _ in range(3)]
                DPn = [Dep() for _ in range(3)]
                Doacc = [Dep() for _ in range(3)]
                Dotmp = [Dep(), Dep()]
                Drden0 = [Dep() for _ in range(3)]
                DrdenB = [Dep(), Dep()]
                Dgsb = [[Dep() for _ in range(3)] for _ in range(3)]
                Dnegsel = [Dep() for _ in range(3)]
                c.op("dve", lambda e: e.memset(selF[:], 0.0), writes=[DselF])
                rot = [0, 0]

                def sc_next():
                    rot[0] = (rot[0] + 1) % 4
                    return rot[0]

                def E_next():
                    rot[1] = (rot[1] + 1) % NE
                    return rot[1]

                def run_steps(steps, qsrc, ksrc, vsrc, Dg_, pO, pD, stages, depth=3):
                    n = len(steps)
                    slots = []

                    def issue_S(k):
                        kt, extras = steps[k]
                        bk = sc_next()
                        e_ = E_next()
                        c.op("pe", lambda e: e.matmul(ps[bk][:, :], lhsT=ksrc[:, kt * 128:(kt + 1) * 128], rhs=qsrc, start=True,
                                                      stop=(len(extras) == 0)), reads=[Dg_], writes=[Dps[bk]])
                        for j, (l_ap, r_ap, rd) in enumerate(extras):
                            c.op("pe", lambda e: e.matmul(ps[bk][:, :], lhsT=l_ap, rhs=r_ap, start=False, stop=(j == len(extras) - 1)),
                                 reads=rd, writes=[Dps[bk]])
                        c.op("act", lambda e: e.activation(out=Et[e_][:], in_=ps[bk][:, :], func=AF.Exp, scale=SCALE),
                             reads=[Dps[bk]], writes=[DEt[e_]])
                        slots.append(e_)

                    for k in range(min(depth, n)):
                        issue_S(k)
                    for k in range(n):
                        if k + depth < n:
                            issue_S(k + depth)
                        e_ = slots[k]
                        kt = steps[k][0]
                        c.op("pe", lambda e: e.matmul(ps[pO][:, :], lhsT=vsrc[:, kt, :], rhs=Et[e_][:, :], start=(k == 0), stop=(k == n - 1)),
                             reads=[Dg_, DEt[e_]], writes=[Dps[pO]])
                        c.op("pe", lambda e: e.matmul(ps[pD][:, :], lhsT=ones[:, :], rhs=Et[e_][:, :], start=(k == 0), stop=(k == n - 1)),
                             reads=[Dconst, DEt[e_]], writes=[Dps[pD]])
                        if stages:
                            stages.pop(0)()

                def neg_recip(src_ap, Dsrc, rb, Drb, tA, DtA):
                    c.op("dve", lambda e: e.tensor_scalar_max(out=tA[:], in0=src_ap, scalar1=1e-30), reads=[Dsrc], writes=[DtA])
                    c.op("act", lambda e: e.activation(out=rb[:], in_=tA[:], func=AF.Ln), reads=[DtA], writes=[Drb])
                    c.op("act", lambda e: e.activation(out=rb[:], in_=rb[:], func=AF.Exp, scale=-1.0), reads=[Drb], writes=[Drb])
                    c.op("dve", lambda e: e.tensor_tensor(out=tA[:], in0=tA[:], in1=rb[:], op=ALU.mult), reads=[DtA, Drb], writes=[DtA])
                    c.op("dve", lambda e: e.scalar_tensor_tensor(out=rb[:], in0=tA[:], scalar=2.0, in1=rb[:], op0=ALU.subtract, op1=ALU.mult),
                         reads=[DtA, Drb], writes=[Drb])

                def finish(pO, pD, gs, Dgs, ob, oa):
                    rb, Drb = rdenB[ob], DrdenB[ob]
                    neg_recip(ps[pD][:, :], Dps[pD], rb, Drb, otmp[ob], Dotmp[ob])
                    c.op("dve", lambda e: e.scalar_tensor_tensor(out=rb[:], in0=rb[:], scalar=-1.0, in1=gs[:], op0=ALU.mult, op1=ALU.mult),
                         reads=[Drb, Dgs], writes=[Drb])
                    c.op("dve", lambda e: e.tensor_tensor(out=otmp[ob][:], in0=ps[pO][:, :], in1=rb[:], op=ALU.mult),
                         reads=[Dps[pO], Drb], writes=[Dotmp[ob]])
                    c.op("pool", lambda e: e.tensor_tensor(out=oacc[oa][:], in0=oacc[oa][:], in1=otmp[ob][:], op=ALU.add),
                         reads=[Doacc[oa], Dotmp[ob]], writes=[Doacc[oa]])

                def load_group(g):
                    b = g % NG
                    c.dma("sp", qg[b][:], s_q[4 * g:4 * g + 4].rearrange("n p t -> p n t"), writes=[Dgl[b]])
                    c.dma("sp", qrg[b][:], s_qr[4 * g:4 * g + 4].rearrange("n p t -> p n t"), writes=[Dgl[b]])
                    c.dma("sp", ngg[b][:], s_ng[4 * g:4 * g + 4].rearrange("n p t -> p n t"), writes=[Dgl[b]])
                    c.dma("sp", ksg[b][:], s_ks[g], writes=[Dgl[b]])
                    c.dma("sp", kwg[b][:], s_kw[g], writes=[Dgl[b]])
                    c.dma("sp", vsg[b][:], s_vs[:, :, g * 128:(g + 1) * 128], writes=[Dgl[b]])
                    c.dma("sp", vwg[b][:], s_vw[:, :, g * 128:(g + 1) * 128], writes=[Dgl[b]])

                def prep(g, i):
                    b = g % NG
                    pb = (g * 8 + i) % 3
                    tsl = slice(i * 128, (i + 1) * 128)
                    q_t = qg[b][:, :, tsl]
                    for br in range(3):
                        bk = sc_next()
                        for n in range(4):
                            r = br * 16 + g * 4 + n
                            c.op("pe", lambda e: e.matmul(ps[bk][:, n * 128:(n + 1) * 128], lhsT=sel48[:, r * 128:(r + 1) * 128],
                                                          rhs=gates[:, tsl], start=True, stop=True), reads=[Dtb], writes=[Dps[bk]])
                        c.op("dve", lambda e: e.tensor_copy(out=gsb[pb][br][:], in_=ps[bk][:, :]), reads=[Dps[bk]], writes=[Dgsb[pb][br]])
                    bc = sc_next()
                    c.op("pe", lambda e: e.matmul(ps[bc][:, :], lhsT=kcomp[:, g, :], rhs=q_t, start=True, stop=False),
                         reads=[Dkcomp, Dgl[b]], writes=[Dps[bc]])
                    c.op("pe", lambda e: e.matmul(ps[bc][:, :], lhsT=ident[:, :], rhs=ncmask[:, i * 512:(i + 1) * 512], start=False, stop=True),
                         reads=[Dtb, Dconst], writes=[Dps[bc]])
                    c.op("act", lambda e: e.activation(out=Ec[pb][:], in_=ps[bc][:, :], func=AF.Exp, scale=SCALE),
                         reads=[Dps[bc]], writes=[DEc[pb]])

                    def stage_den():
                        bd = sc_next()
                        c.op("pe", lambda e: e.matmul(ps[bd][:, :], lhsT=ones[:, :], rhs=Ec[pb][:, :], start=True, stop=True),
                             reads=[Dconst, DEc[pb]], writes=[Dps[bd]])
                        neg_recip(ps[bd][:, :], Dps[bd], rden0[pb], Drden0[pb], tden, Dtden)
                        c.op("dve", lambda e: e.scalar_tensor_tensor(out=Pn[pb][:], in0=Ec[pb][:], scalar=-1.0, in1=rden0[pb][:],
                                                                     op0=ALU.mult, op1=ALU.mult),
                             reads=[DEc[pb], Drden0[pb]], writes=[DPn[pb]])

                    def stage_imp():
                        bi = sc_next()
                        for n in range(4):
                            c.op("pe", lambda e: e.matmul(ps[bi][:, 0:32], lhsT=Pn[pb][:, n * 128:(n + 1) * 128], rhs=overlap[:, :],
                                                          start=(n == 0), stop=(n == 3)), reads=[DPn[pb], Dtb], writes=[Dps[bi]])
                        bo = sc_next()
                        c.op("pe", lambda e: e.matmul(ps[bo][:, :], lhsT=vcomp[:, g, :], rhs=Pn[pb][:, :], start=True, stop=True),
                             reads=[Dvcomp, DPn[pb]], writes=[Dps[bo]])
                        c.op("dve", lambda e: e.tensor_tensor(out=imp2[:], in0=ps[bi][:, 0:32], in1=addm[:, i * 32:(i + 1) * 32], op=ALU.add),
                             reads=[Dps[bi], Dtb], writes=[Dimp2])
                        c.op("dve", lambda e: e.max(out=top8[:], in_=imp2[:]), reads=[Dimp2], writes=[Dtop8])
                        c.op("dve", lambda e: e.tensor_scalar_max(out=top8[:, 7:8], in0=top8[:, 7:8], scalar1=-1e29), reads=[Dtop8], writes=[Dtop8])
                        c.op("dve", lambda e: e.tensor_scalar(out=selF[:, 0:32], in0=imp2[:], scalar1=top8[:, 7:8], scalar2=None, op0=ALU.is_ge),
                             reads=[Dimp2, Dtop8], writes=[DselF])
                        c.op("dve", lambda e: e.tensor_tensor(out=oacc[pb][:], in0=ps[bo][:, :], in1=gsb[pb][0][:], op=ALU.mult),
                             reads=[Dps[bo], Dgsb[pb][0]], writes=[Doacc[pb]])
                        if debug and g == 0:
                            c.dma("sp", s_dbg[:, 1024 + i * 32:1024 + (i + 1) * 32], imp2[:], reads=[Dimp2], writes=[Dep()])

                    def stage_tr():
                        bt = sc_next()
                        c.op("pe", lambda e: e.transpose(out=ps[bt][:, 0:128], in_=selF[:, :], identity=identf[:, :]),
                             reads=[DselF, Dconst], writes=[Dps[bt]])
                        c.op("dve", lambda e: e.tensor_scalar(out=negsel[pb][:, :].rearrange("p (a b) -> p a b", b=128),
                                                              in0=ps[bt][:, 0:128].unsqueeze(1).to_broadcast([128, 4, 128]),
                                                              scalar1=-1.0, scalar2=NEGM, op0=ALU.add, op1=ALU.mult),
                             reads=[Dps[bt]], writes=[Dnegsel[pb]])

                    return [stage_den, None, None, None, stage_imp, None, None, stage_tr]

                tiles = [(g, i) for g in range(4) for i in range(8)]
                DEPTH = 3
                pending = []
                cur_stages = []

                def make_step(kt, extras, qsrc, ksrc, vsrc, Dg_, pO, pD, first, last, post):
                    def issue_S():
                        bk = sc_next()
                        e_ = E_next()
                        c.op("pe", lambda e: e.matmul(ps[bk][:, :], lhsT=ksrc[:, kt * 128:(kt + 1) * 128], rhs=qsrc, start=True,
                                                      stop=(len(extras) == 0)), reads=[Dg_], writes=[Dps[bk]])
                        for j, (l_ap, r_ap, rd) in enumerate(extras):
                            c.op("pe", lambda e: e.matmul(ps[bk][:, :], lhsT=l_ap, rhs=r_ap, start=False, stop=(j == len(extras) - 1)),
                                 reads=rd, writes=[Dps[bk]])
                        c.op("act", lambda e: e.activation(out=Et[e_][:], in_=ps[bk][:, :], func=AF.Exp, scale=SCALE),
                             reads=[Dps[bk]], writes=[DEt[e_]])
                        return e_

                    def issue_PV(e_):
                        c.op("pe", lambda e: e.matmul(ps[pO][:, :], lhsT=vsrc[:, kt, :], rhs=Et[e_][:, :], start=first, stop=last),
                             reads=[Dg_, DEt[e_]], writes=[Dps[pO]])
                        c.op("pe", lambda e: e.matmul(ps[pD][:, :], lhsT=ones[:, :], rhs=Et[e_][:, :], start=first, stop=last),
                             reads=[Dconst, DEt[e_]], writes=[Dps[pD]])
                        if post is not None:
                            post()
                    return issue_S, issue_PV

                def pop_one():
                    issue_PV, e_ = pending.pop(0)
                    issue_PV(e_)
                    if cur_stages:
                        (cur_stages.pop(0) or (lambda: None))()

                def push(step):
                    issue_S, issue_PV = step
                    pending.append((issue_PV, issue_S()))
                    if len(pending) > DEPTH:
                        pop_one()

                load_group(0)
                cur_stages.extend(prep(0, 0))
                while cur_stages:
                    (cur_stages.pop(0) or (lambda: None))()
                for ti, (g, i) in enumerate(tiles):
                    b = g % NG
                    p = i & 1
                    oa = ti % 3
                    tsl = slice(i * 128, (i + 1) * 128)
                    qr_t = qrg[b][:, :, tsl]
                    while cur_stages:
                        (cur_stages.pop(0) or (lambda: None))()
                    if i == 1 and g + 1 < 4:
                        load_group(g + 1)
                    if ti + 1 < len(tiles):
                        cur_stages.extend(prep(*tiles[ti + 1]))
                    wsteps = []
                    for o in range(6):
                        kt = 2 * i - 4 + o
                        if kt < 0:
                            continue
                        ex = [] if o in (2, 3) else [(ident[:, :], nwmask[:, (p * 6 + o) * 512:(p * 6 + o + 1) * 512], [Dtb, Dconst])]
                        wsteps.append((kt, ex))
                    for k, (kt, ex) in enumerate(wsteps):
                        lastw = (k == len(wsteps) - 1)
                        post = (lambda oa=oa: finish(4, 5, gsb[oa][2], Dgsb[oa][2], 0, oa)) if lastw else None
                        push(make_step(kt, ex, qr_t, kwg[b], vwg[b], Dgl[b], 4, 5, k == 0, lastw, post))
                    nsl = 2 * i + 2
                    for kt in range(nsl):
                        o = kt - 2 * i
                        ex = [(expand[:, kt * 128:(kt + 1) * 128], negsel[oa][:, :], [Dtb, Dnegsel[oa]])]
                        if o >= 0:
                            ex.append((ident[:, :], ndmask[:, (p * 2 + o) * 512:(p * 2 + o + 1) * 512], [Dtb, Dconst]))
                        lasts = (kt == nsl - 1)

                        def post_s(oa=oa, b=b, tsl=tsl, g=g, i=i):
                            finish(6, 7, gsb[oa][1], Dgsb[oa][1], 1, oa)
                            c.op("pool", lambda e: e.tensor_tensor(out=yst[b][:, :, tsl], in0=oacc[oa][:, :].rearrange("p (a b) -> p a b", b=128),
                                                                   in1=ngg[b][:, :, tsl], op=ALU.mult),
                                 reads=[Doacc[oa], Dgl[b]], writes=[Dyst[b]])
                            if i == 7:
                                c.dma("pool", s_y[:, 16 + 4 * g:16 + 4 * g + 4, :], yst[b][:], reads=[Dyst[b]], writes=[Dep()])
                        push(make_step(kt, ex, qr_t, ksg[b], vsg[b], Dgl[b], 6, 7, kt == 0, lasts, post_s if lasts else None))
                while pending:
                    pop_one()
                while cur_stages:
                    (cur_stages.pop(0) or (lambda: None))()
                ps.pop()
                Dps.pop()
            c.barrier()

    def out_proj(s_ysrc, w_out, x_src, dst, final_gain, prefix):
        with ExitStack() as st:
            wo = sbt(st, prefix + "wo", [128, 32, D], BF16)
            Dwo = [Dep() for _ in range(4)]
            wv = w_out.rearrange("(kc p) n -> p kc n", p=128)
            for nb in range(4):
                for h in range(8):
                    c.dma("pool", wo[:, 4 * h:4 * h + 4, nb * 512:(nb + 1) * 512], wv[:, 4 * h:4 * h + 4, nb * 512:(nb + 1) * 512],
                          writes=[Dwo[nb]])
            yT = [sbt(st, prefix + "yT%d" % i, [128, 32, 128], BF16) for i in range(2)]
            xt = [sbt(st, prefix + "xt%d" % i, [128, D], F32) for i in range(2)]
            xo = xt
            DyT = [Dep(), Dep()]
            Dxt = [Dep(), Dep()]
            Dxo = Dxt
            if final_gain is not None:
                gbc = sbt(st, prefix + "gbc", [128, D], F32)
                junk = sbt(st, prefix + "junk", [128, D], BF16)
                stat = sbt(st, prefix + "stat", [128, 4], F32)
                Dg, Dj, Dst = Dep(), Dep(), Dep()
                c.dma("sp", gbc[:], final_gain.to_broadcast([128, D]), writes=[Dg])
            for tt in range(8):
                b = tt % 2
                c.dma("sp", yT[b][:], s_ysrc[:, :, tt * 128:(tt + 1) * 128].rearrange("k p t -> p k t"), writes=[DyT[b]])
                c.dma("sp", xt[b][:], x_src[tt * 128:(tt + 1) * 128, :], writes=[Dxt[b]])
                for nb in range(4):
                    pi = next_ps()
                    mm_group(pi, 0, 512, lambda kc: yT[b][:, kc, :], lambda kc: wo[:, kc, nb * 512:(nb + 1) * 512], [DyT[b], Dwo[nb]], nk=32)
                    c.op("dve", lambda e: e.tensor_tensor(out=xo[b][:, nb * 512:(nb + 1) * 512], in0=ps[pi][:, :],
                                                          in1=xt[b][:, nb * 512:(nb + 1) * 512], op=ALU.add),
                         reads=[Dps[pi], Dxt[b]], writes=[Dxo[b]])
                if final_gain is not None:
                    c.op("act", lambda e: e.activation(out=junk[:], in_=xo[b][:], func=AF.Square, accum_out=stat[:, 0:1]),
                         reads=[Dxo[b]], writes=[Dj, Dst])
                    c.op("act", lambda e: e.activation(out=stat[:, 1:2], in_=stat[:, 0:1], func=AF.Sqrt, scale=1.0 / D, bias=epsc[:]),
                         reads=[Dst, Dconst], writes=[Dst])
                    c.op("dve", lambda e: e.reciprocal(out=stat[:, 2:3], in_=stat[:, 1:2]), reads=[Dst], writes=[Dst])
                    c.op("dve", lambda e: e.scalar_tensor_tensor(out=xo[b][:], in0=xo[b][:], scalar=stat[:, 2:3], in1=gbc[:],
                                                                 op0=ALU.mult, op1=ALU.mult), reads=[Dxo[b], Dst, Dg], writes=[Dxo[b]])
                c.dma("pool", dst[tt * 128:(tt + 1) * 128, :], xo[b][:], reads=[Dxo[b]], writes=[Dep()])
            c.barrier()

    def out_proj_nb(s_ysrc, w_out, x_src, dst, prefix):
        with ExitStack() as st:
            yT = sbt(st, prefix + "yT", [128, 32, TO], BF16)
            DyT = [Dep()] * 8
            for j in range(4):
                c.dma("sp", yT[:, 8 * j:8 * j + 8, :], s_ysrc[:, 8 * j:8 * j + 8, :], writes=[DyT[0]])
            wo = [sbt(st, prefix + "wo%d" % i, [128, 32, 512], BF16) for i in range(3)]
            Dwo = [Dep() for _ in range(3)]
            wv = w_out.rearrange("(kc p) n -> p kc n", p=128)
            xp = [sbt(st, prefix + "xp%d" % i, [128, 512], F32) for i in range(4)]
            Dxp = [Dep() for _ in range(4)]
            xi = 0
            for nb in range(4):
                k = nb % 3
                for h in range(8):
                    c.dma("pool", wo[k][:, 4 * h:4 * h + 4, :], wv[:, 4 * h:4 * h + 4, nb * 512:(nb + 1) * 512], writes=[Dwo[k]])
                for tt in range(8):
                    b = xi % 4
                    xi += 1
                    c.dma("sp", xp[b][:], x_src[tt * 128:(tt + 1) * 128, nb * 512:(nb + 1) * 512], writes=[Dxp[b]])
                    pi = next_ps()
                    mm_group(pi, 0, 512, lambda kc: yT[:, kc, tt * 128:(tt + 1) * 128], lambda kc: wo[k][:, kc, :], [DyT[tt], Dwo[k]], nk=32)
                    c.op("dve", lambda e: e.tensor_tensor(out=xp[b][:], in0=ps[pi][:, :], in1=xp[b][:], op=ALU.add),
                         reads=[Dps[pi], Dxp[b]], writes=[Dxp[b]])
                    c.dma("act", dst[tt * 128:(tt + 1) * 128, nb * 512:(nb + 1) * 512], xp[b][:], reads=[Dxp[b]], writes=[Dep()])
            c.barrier()

    if stop >= 5:
        out_proj_nb(s_y, w_out_even, x_own, s_x1, "d")

    if stop >= 6:
        with ExitStack() as st:
            hT = sbt(st, "hT1", [128, 16, TO], BF16)
            vsb = sbt(st, "vsb", [128, 8, 4096], BF16)
            Dv = [Dep() for _ in range(8)]
            lg = sbt(st, "lg", [128, 32], F32)
            lb = sbt(st, "lb", [128, 32], F32)
            wsT = sbt(st, "wsT", [128, 16, 128], BF16)
            rsbc = sbt(st, "rsbc", [128, 16, 128], F32)
            bsbc = sbt(st, "bsbc", [128, 16, 128], F32)
            ssum = sbt(st, "fssum", [128, 8, 8], F32)
            ssq = sbt(st, "fssq", [128, 8, 8], F32)
            sqj = sbt(st, "fsqj", [128, 512], BF16)
            stat = sbt(st, "fstat", [128, 8, 8], F32)
            st2 = ExitStack()
            lgr = sbt(st2, "lgr", [128, 128], F32)
            lbr = sbt(st2, "lbr", [128, 128], F32)
            wsr = sbt(st2, "wsr", [128, 16, 128], F32)
            tril = sbt(st2, "tril", [128, 128], F32)
            wsm = sbt(st2, "wsm", [128, 16, 128], BF16)
            Dlg, Dws, DwsT, Drs, Dbs = [Dep() for _ in range(5)]
            Dz = Dep()
            c.op("dve", lambda e: e.memset(lgr[:], 0.0), writes=[Dz])
            c.op("dve", lambda e: e.memset(lbr[:], 0.0), writes=[Dz])
            c.dma("sp", lgr[0:32, :], ln_g, reads=[Dz], writes=[Dlg])
            c.dma("sp", lbr[0:32, :], ln_b, reads=[Dz], writes=[Dlg])
            c.dma("sp", wsr[:], w_s.rearrange("g t s -> t g s"), writes=[Dws])
            c.dma("sp", tril[:], t_tril, writes=[Dws])
            c.dma("sp", bsbc[:].rearrange("p a b -> p (a b)"), b_s.to_broadcast([128, 2048]), writes=[Dbs])
            c.op("pe", lambda e: e.transpose(out=ps[6][:, 0:128], in_=lgr[:, :], identity=identf[:, :]), reads=[Dlg, Dconst], writes=[Dps[6]])
            c.op("pe", lambda e: e.transpose(out=ps[6][:, 128:256], in_=lbr[:, :], identity=identf[:, :]), reads=[Dlg, Dconst], writes=[Dps[6]])
            c.op("dve", lambda e: e.tensor_copy(out=lg[:], in_=ps[6][:, 0:32]), reads=[Dps[6]], writes=[Dlg])
            c.op("dve", lambda e: e.tensor_copy(out=lb[:], in_=ps[6][:, 128:160]), reads=[Dps[6]], writes=[Dlg])
            c.op("dve", lambda e: e.tensor_tensor(out=wsm[:], in0=wsr[:], in1=tril[:, :].unsqueeze(1).to_broadcast([128, 16, 128]), op=ALU.mult),
                 reads=[Dws], writes=[Dws])
            for a in range(2):
                for j in range(8):
                    c.op("pe", lambda e: e.transpose(out=psb[:, j * 128:(j + 1) * 128], in_=wsm[:, 8 * a + j, :], identity=ident[:, :]),
                         reads=[Dws, Dconst], writes=[Dpsb])
                c.op("dve", lambda e: e.tensor_copy(out=wsT[:, 8 * a:8 * a + 8, :], in_=psb[:, :].rearrange("p (a b) -> p a b", b=128)),
                     reads=[Dpsb], writes=[DwsT])
            for a in range(4):
                c.op("pe", lambda e: e.matmul(ps[6][:, :], lhsT=ones[:, :], rhs=wsT[:, 4 * a:4 * a + 4, :], start=True, stop=True),
                     reads=[DwsT, Dconst], writes=[Dps[6]])
                c.op("dve", lambda e: e.tensor_copy(out=rsbc[:, 4 * a:4 * a + 4, :], in_=ps[6][:, :].rearrange("p (a b) -> p a b", b=128)),
                     reads=[Dps[6]], writes=[Drs])
            DhT = [Dep() for _ in range(8)]
            pA, pB = norm_parts(st2, s_x1, norm_odd, hT, DhT, "e")
            w0 = sbt(st2, "fvw0", [128, 16, 512], BF16)
            Dw0 = Dep()
            Dss, Dsq, Dsqj, Dst = Dep(), Dep(), Dep(), Dep()
            wv_ = w_in_odd.rearrange("(kc p) n -> p kc n", p=128)
            for h in range(4):
                c.dma("pool", w0[:, 4 * h:4 * h + 4, :], wv_[:, 4 * h:4 * h + 4, 4096:4096 + 512], writes=[Dw0])

            def v_group(vb, tt, wt, Dw):
                pi = next_ps()
                mm_group(pi, 0, 512, lambda kc: hT[:, kc, tt * 128:(tt + 1) * 128], lambda kc: wt[:, kc, :], [Dw, DhT[tt]])
                c.op("dve", lambda e: e.tensor_scalar(out=vsb[:, tt, vb * 512:(vb + 1) * 512], in0=ps[pi][:, :], scalar1=1.0, scalar2=0.0,
                                                      op0=ALU.mult, op1=ALU.add, accum_out=ssum[:, tt, vb:vb + 1]),
                     reads=[Dps[pi]], writes=[Dv[tt], Dss])
                c.op("act", lambda e: e.activation(out=sqj[:], in_=ps[pi][:, :], func=AF.Square, accum_out=ssq[:, tt, vb:vb + 1]),
                     reads=[Dps[pi]], writes=[Dsqj, Dsq])

            pA(0)
            pB(0)
            pA(1)
            for tt in range(8):
                if tt + 1 < 8:
                    pB(tt + 1)
                if tt + 2 < 8:
                    pA(tt + 2)
                v_group(0, tt, w0, Dw0)
            c.barrier()
            st2.close()
            ws = WS(st, "fw", n=4)
            for vb in range(1, 8):
                wt, Dw = ws.load(w_in_odd, [(4096 + vb * 512, 512, 0)])
                for tt in range(8):
                    v_group(vb, tt, wt, Dw)
            for tt in range(8):
                s_ = stat[:, tt, :]
                c.op("dve", lambda e: e.reduce_sum(out=s_[:, 0:1], in_=ssum[:, tt, :], axis=AX.X), reads=[Dss], writes=[Dst])
                c.op("dve", lambda e: e.reduce_sum(out=s_[:, 1:2], in_=ssq[:, tt, :], axis=AX.X), reads=[Dsq], writes=[Dst])
                c.op("dve", lambda e: e.tensor_scalar(out=s_[:, 2:3], in0=s_[:, 0:1], scalar1=1.0 / 4096, scalar2=None, op0=ALU.mult),
                     reads=[Dst], writes=[Dst])
                c.op("dve", lambda e: e.tensor_tensor(out=s_[:, 3:4], in0=s_[:, 2:3], in1=s_[:, 2:3], op=ALU.mult), reads=[Dst], writes=[Dst])
                c.op("dve", lambda e: e.scalar_tensor_tensor(out=s_[:, 4:5], in0=s_[:, 1:2], scalar=1.0 / 4096, in1=s_[:, 3:4],
                                                             op0=ALU.mult, op1=ALU.subtract), reads=[Dst], writes=[Dst])
                c.op("act", lambda e: e.activation(out=s_[:, 5:6], in_=s_[:, 4:5], func=AF.Sqrt, scale=1.0, bias=epsc[:]),
                     reads=[Dst, Dconst], writes=[Dst])
                c.op("dve", lambda e: e.reciprocal(out=s_[:, 6:7], in_=s_[:, 5:6]), reads=[Dst], writes=[Dst])
            for tt in range(8):
                s_ = stat[:, tt, :]
                eng = "dve"
                c.op(eng, lambda e: e.tensor_scalar(out=vsb[:, tt, :], in0=vsb[:, tt, :], scalar1=s_[:, 2:3], scalar2=s_[:, 6:7],
                                                    op0=ALU.subtract, op1=ALU.mult), reads=[Dv[tt], Dst], writes=[Dv[tt]])
            B2 = sbt(st, "B2", [128, 128], F32)
            szS = sbt(st, "szS", [128, 512], F32)
            m1 = sbt(st, "m1", [128, 512], F32)
            y2 = [sbt(st, "y2S%d" % i, [128, TO], BF16) for i in range(2)]
            DB2, Dsz, Dm1 = Dep(), Dep(), Dep()
            Dy2 = [Dep(), Dep()]
            for cb4 in range(8):
                wt, Dw = ws.load(w_in_odd, [(cb4 * 512, 512, 0)])
                wt2, Dw2 = ws.load(w_in_odd, [(8192 + cb4 * 512, 512, 0)])
                for j in range(4):
                    ct = cb4 * 4 + j
                    g = ct // 2
                    b = ct % 2
                    c.op("dve", lambda e: e.scalar_tensor_tensor(out=B2[:], in0=rsbc[:, g, :], scalar=lb[:, ct:ct + 1], in1=bsbc[:, g, :],
                                                                 op0=ALU.mult, op1=ALU.add), reads=[Drs, Dbs, Dlg], writes=[DB2])
                    for th in range(2):
                        tsl = slice(th * 512, (th + 1) * 512)
                        pu = next_ps()
                        mm_group(pu, 0, 512, lambda kc: wt[:, kc, j * 128:(j + 1) * 128], lambda kc: hT[:, kc, tsl], [Dw] + DhT[4 * th:4 * th + 4])
                        pz = next_ps()
                        mm_group(pz, 0, 512, lambda kc: wt2[:, kc, j * 128:(j + 1) * 128], lambda kc: hT[:, kc, tsl], [Dw2] + DhT[4 * th:4 * th + 4])
                        c.op("act", lambda e: e.activation(out=szS[:], in_=ps[pz][:, :], func=AF.Silu), reads=[Dps[pz]], writes=[Dsz])
                        pm = next_ps()
                        for k4 in range(4):
                            tt = th * 4 + k4
                            c.op("pe", lambda e: e.matmul(ps[pm][:, k4 * 128:(k4 + 1) * 128], lhsT=vsb[:, tt, ct * 128:(ct + 1) * 128],
                                                          rhs=wsT[:, g, :], start=True, stop=True), reads=[Dv[tt], DwsT], writes=[Dps[pm]])
                        c.op("dve", lambda e: e.scalar_tensor_tensor(out=m1[:, :].rearrange("p (a b) -> p a b", b=128),
                                                                     in0=ps[pm][:, :].rearrange("p (a b) -> p a b", b=128),
                                                                     scalar=lg[:, ct:ct + 1],
                                                                     in1=B2[:, :].unsqueeze(1).to_broadcast([128, 4, 128]),
                                                                     op0=ALU.mult, op1=ALU.add), reads=[Dps[pm], DB2, Dlg], writes=[Dm1])
                        c.op("dve", lambda e: e.tensor_tensor(out=m1[:], in0=m1[:], in1=ps[pu][:, :], op=ALU.mult),
                             reads=[Dm1, Dps[pu]], writes=[Dm1])
                        c.op("dve", lambda e: e.tensor_tensor(out=y2[b][:, tsl], in0=m1[:], in1=szS[:], op=ALU.mult),
                             reads=[Dm1, Dsz], writes=[Dy2[b]])
                    c.dma("sp", s_y2[ct], y2[b][:], reads=[Dy2[b]], writes=[Dep()])
            c.barrier()

    if stop >= 7:
        out_proj(s_y2, w_out_odd, s_x1, out, norm_final, "g")

    c.barrier()
    top.close()
    c.close()
    return nc


def _bf(a):
    return np.asarray(a, dtype=np.float32).astype(ml_dtypes.bfloat16)


def own_tiles(hh):
    return [2 * i + (hh ^ (i & 1)) for i in range(8)]


def host_tables(hh):
    tiles = own_tiles(hh)
    own_pos = np.concatenate([np.arange(128 * t, 128 * t + 128) for t in tiles])
    half = 16
    inv_freq = np.power(np.float32(500000.0), -np.arange(half, dtype=np.float32) * np.float32(2.0) / np.float32(32)).astype(np.float32)

    def cs(pos):
        ang = pos.astype(np.float32)[:, None] * inv_freq[None, :]
        co = np.cos(ang).astype(np.float32).T
        si = np.sin(ang).astype(np.float32).T
        n = co.shape[1]
        return (np.ascontiguousarray(np.concatenate([co, co, np.ones((96, n), np.float32)], 0)),
                np.ascontiguousarray(np.concatenate([si, si, np.zeros((96, n), np.float32)], 0)))

    ca, sa = cs(np.arange(T))
    co, so = cs(own_pos)
    R = np.zeros((128, 128), np.float32)
    for m in range(16):
        R[m + 16, m] = -1.0
        R[m, m + 16] = 1.0
    tb = {"t_cos_all": ca, "t_sin_all": sa, "t_cos_own": co, "t_sin_own": so, "t_R": _bf(R),
          "t_ident": _bf(np.eye(128)), "t_identf": np.eye(128, dtype=np.float32), "t_ones": _bf(np.ones((128, 128)))}
    j = np.arange(32)
    am = np.zeros((1024, 32), np.float32)
    tpos = own_pos[:, None]
    forced = (j[None, :] == 0) | (j[None, :] == tpos // 64)
    causal = (64 * j[None, :]) <= tpos
    am[~causal] = -BIG
    am[forced] = BIG
    tb["t_addmask"] = np.ascontiguousarray(am.reshape(8, 128, 32).transpose(1, 0, 2).reshape(128, 256))
    cc = np.arange(128)[:, None]
    cm = ((cc < 127) & (16 * cc + 31 <= own_pos[None, :])).astype(np.float32)
    ncm = (cm.reshape(128, 8, 1, 128) - 1.0) * 30000.0
    tb["t_ncmask"] = _bf(np.broadcast_to(ncm, (128, 8, 4, 128)).reshape(128, 8 * 512))
    tb["t_overlap"] = _bf(((cc < 127) & (16 * cc < 64 * j[None, :] + 64) & (16 * cc + 32 > 64 * j[None, :])).astype(np.float32))
    tb["t_expand"] = _bf((np.arange(T)[None, :] // 64 == np.arange(128)[:, None]).astype(np.float32))
    tk = np.arange(128)[:, None]
    tq = np.arange(128)[None, :]
    dm = np.zeros((128, 4, 128), np.float32)
    for p in range(2):
        for o in range(2):
            r = o - (hh ^ p)
            dm[:, p * 2 + o, :] = (128 * r + tk <= tq)
    tb["t_ndmask"] = _bf(np.broadcast_to((dm.reshape(128, 4, 1, 128) - 1.0) * 30000.0, (128, 4, 4, 128)).reshape(128, 4 * 512))
    wm = np.zeros((128, 12, 128), np.float32)
    for p in range(2):
        for o in range(6):
            r = o - 4 - (hh ^ p)
            diff = tq - tk - 128 * r
            wm[:, p * 6 + o, :] = (diff >= 0) & (diff < 512)
    tb["t_nwmask"] = _bf(np.broadcast_to((wm.reshape(128, 12, 1, 128) - 1.0) * 30000.0, (128, 12, 4, 128)).reshape(128, 12 * 512))
    s48 = np.zeros((128, 48, 128), np.float32)
    for r in range(48):
        s48[r, r, :] = 1.0
    tb["t_sel48"] = _bf(s48.reshape(128, 48 * 128))
    tb["t_tril"] = np.tril(np.ones((128, 128), np.float32))
    return tb


def make_in_maps(inp):
    f = lambda a: np.ascontiguousarray(np.asarray(a, dtype=np.float32))
    x = f(inp["x"])
    shared = {
        "norm_even": f(inp["norm_even"]).reshape(1, D),
        "w_in_even": f(inp["w_in_even"]).reshape(D, L0),
        "conv_w": f(inp["conv_w"]).reshape(3, 16, 128).reshape(48, 128),
        "cmp_k_pos": f(inp["cmp_k_pos"]).reshape(32, 128), "cmp_k_w1": f(inp["cmp_k_w1"]).reshape(4096, 128),
        "cmp_k_b1": f(inp["cmp_k_b1"]).reshape(128, 1), "cmp_k_w2": f(inp["cmp_k_w2"]).reshape(128, 128),
        "cmp_v_pos": f(inp["cmp_v_pos"]).reshape(32, 128), "cmp_v_w1": f(inp["cmp_v_w1"]).reshape(4096, 128),
        "cmp_v_b1": f(inp["cmp_v_b1"]).reshape(128, 1), "cmp_v_w2": f(inp["cmp_v_w2"]).reshape(128, 128),
        "w_out_even": f(inp["w_out_even"]).reshape(4096, D),
        "norm_odd": f(inp["norm_odd"]).reshape(1, D),
        "w_in_odd": f(inp["w_in_odd"]).reshape(D, 12288),
        "sgu_ln_g": f(inp["sgu_ln_g"]).reshape(32, 128), "sgu_ln_b": f(inp["sgu_ln_b"]).reshape(32, 128),
        "sgu_w_s": f(inp["sgu_w_s"]).reshape(16, 128, 128), "sgu_b_s": f(inp["sgu_b_s"]).reshape(1, 2048),
        "w_out_odd": f(inp["w_out_odd"]).reshape(4096, D),
        "norm_final": f(inp["norm_final"]).reshape(1, D),
    }
    tabs = [host_tables(0), host_tables(1)]
    maps = []
    for cidx in range(8):
        b, hh = cidx // 2, cidx % 2
        tiles = own_tiles(hh)
        rows = np.concatenate([np.arange(128 * t, 128 * t + 128) for t in tiles])
        xh = np.zeros((128, D), np.float32)
        for i, t in enumerate(tiles):
            if t > 0:
                xh[2 * i:2 * i + 2] = x[b, 128 * t - 2:128 * t]
        m = dict(shared)
        m.update(tabs[hh])
        m["x_all"] = x[b]
        m["x_own"] = np.ascontiguousarray(x[b][rows])
        m["x_halo"] = xh
        maps.append(m)
    return maps


_NC = {}


def kernel(**inputs):
    if "nc" not in _NC:
        _NC["nc"] = build()
    maps = make_in_maps(inputs)
    res = run_bass_kernel_spmd(_NC["nc"], maps, core_ids=list(range(8)))
    outp = np.zeros((4, T, D), np.float32)
    for cidx in range(8):
        b, hh = cidx // 2, cidx % 2
        o = np.asarray(res.results[cidx]["out"], dtype=np.float32)
        for i, t in enumerate(own_tiles(hh)):
            outp[b, 128 * t:128 * t + 128] = o[128 * i:128 * i + 128]
    return outp
```

```python
from contextlib import ExitStack
import numpy as np
import ml_dtypes
import concourse.bass as bass
import concourse.mybir as mybir
from concourse.bass_utils import run_bass_kernel_spmd

F32 = mybir.dt.float32
BF16 = mybir.dt.bfloat16
ALU = mybir.AluOpType
AF = mybir.ActivationFunctionType
AX = mybir.AxisListType

D = 2048
T = 2048
TO = 1024
L0 = 15408
OFF = dict(cb=0, cc=2048, ch=4096, cg=6144, q=8192, kc=10240, vc=10752, ks=11264, vs=11776, kw=12288,
           vw=12800, gl=13312, ng=13360)
SCALE = 128 ** -0.5
EPS = 1e-6
BIG = 1e30


class Dep:
    __slots__ = ("w", "r", "pr", "excl")

    def __init__(self, excl=False):
        self.w = {}
        self.r = {}
        self.pr = {}
        self.excl = excl


class Ctx:
    DMA_POOL = 12

    def __init__(self, nc, same_engine_sync=True):
        self.nc = nc
        self.same = same_engine_sync
        self.eng = {"pe": nc.tensor, "act": nc.scalar, "dve": nc.vector, "pool": nc.gpsimd, "sp": nc.sync}
        self.sem = {}
        self.cnt = {}
        self.seen = {k: {} for k in self.eng}
        self._stack = []
        for k in self.eng:
            cm = nc.semaphore("s_" + k)
            self.sem[k] = cm.__enter__()
            self._stack.append(cm)
            self.cnt[k] = 0
        self.dpool = {}
        self.dpos = {}
        for q in ("sp", "pool", "act"):
            lst = []
            for i in range(self.DMA_POOL):
                cm = nc.semaphore("d_%s%d" % (q, i))
                lst.append([cm.__enter__(), 0])
                self._stack.append(cm)
            self.dpool[q] = lst
            self.dpos[q] = 0

    def close(self):
        for cm in reversed(self._stack):
            cm.__exit__(None, None, None)

    def _wait(self, e, tickets):
        E = self.eng[e]
        seen = self.seen[e]
        own = id(self.sem[e])
        for sid, (sem, val) in tickets.items():
            if seen.get(sid, 0) >= val:
                continue
            if sid == own and (e == "pe" or not self.same):
                continue
            E.wait_ge(sem, val)
            seen[sid] = val

    @staticmethod
    def _merge(dst, src):
        for sid, tv in src.items():
            if sid not in dst or dst[sid][1] < tv[1]:
                dst[sid] = tv

    def _collect(self, reads, writes, own=None):
        t = {}
        for d in reads:
            self._merge(t, d.w)
            if d.excl:
                self._merge(t, {k: v for k, v in d.r.items() if k != own})
        for d in writes:
            if d.r:
                self._merge(t, d.r)
            elif d.pr:
                self._merge(t, d.pr)
        return t

    def _record(self, reads, writes, ticket):
        tk = {id(ticket[0]): ticket}
        for d in writes:
            if d.r:
                d.w = dict(tk)
                d.pr = d.r
                d.r = {}
            else:
                self._merge(d.w, tk)
        for d in reads:
            self._merge(d.r, tk)

    def op(self, e, fn, reads=(), writes=()):
        self._wait(e, self._collect(reads, writes, id(self.sem[e])))
        inst = fn(self.eng[e])
        self.cnt[e] += 1
        inst.then_inc(self.sem[e], 1)
        self._record(reads, writes, (self.sem[e], self.cnt[e]))
        return inst

    def dma(self, q, out, in_, reads=(), writes=(), **kw):
        self._wait(q, self._collect(reads, writes))
        slot = self.dpool[q][self.dpos[q] % self.DMA_POOL]
        self.dpos[q] += 1
        E = self.eng[q]
        if slot[1] > 0 and self.seen[q].get(id(slot[0]), 0) < slot[1]:
            E.wait_ge(slot[0], slot[1])
            self.seen[q][id(slot[0])] = slot[1]
        inst = E.dma_start(out=out, in_=in_, **kw)
        slot[1] += 16
        inst.then_inc(slot[0], 16)
        self._record(reads, writes, (slot[0], slot[1]))
        return inst

    def barrier(self):
        t = {}
        for k in self.eng:
            if self.cnt[k]:
                t[id(self.sem[k])] = (self.sem[k], self.cnt[k])
        for q in self.dpool:
            for slot in self.dpool[q]:
                if slot[1]:
                    t[id(slot[0])] = (slot[0], slot[1])
        for e in self.eng:
            E = self.eng[e]
            seen = self.seen[e]
            for sid, (sem, val) in t.items():
                if sid == id(self.sem[e]) or seen.get(sid, 0) >= val:
                    continue
                E.wait_ge(sem, val)
                seen[sid] = val


def build(debug=False, stop=99):
    nc = bass.Bass("TRN2", target_bir_lowering=False)
    c = Ctx(nc)

    def din(name, shape, dt=F32):
        return nc.dram_tensor(name, list(shape), dt, kind="ExternalInput").ap()

    def dscr(name, shape, dt):
        return nc.dram_tensor(name, list(shape), dt, kind=("ExternalOutput" if debug else "Internal")).ap()

    x_all = din("x_all", [T, D])
    x_own = din("x_own", [TO, D])
    x_halo = din("x_halo", [128, D])
    norm_even = din("norm_even", [1, D])
    w_in_even = din("w_in_even", [D, L0])
    conv_w = din("conv_w", [48, 128])
    kpos = din("cmp_k_pos", [32, 128])
    kw1 = din("cmp_k_w1", [4096, 128])
    kb1 = din("cmp_k_b1", [128, 1])
    kw2 = din("cmp_k_w2", [128, 128])
    vpos = din("cmp_v_pos", [32, 128])
    vw1 = din("cmp_v_w1", [4096, 128])
    vb1 = din("cmp_v_b1", [128, 1])
    vw2 = din("cmp_v_w2", [128, 128])
    w_out_even = din("w_out_even", [4096, D])
    norm_odd = din("norm_odd", [1, D])
    w_in_odd = din("w_in_odd", [D, 12288])
    ln_g = din("sgu_ln_g", [32, 128])
    ln_b = din("sgu_ln_b", [32, 128])
    w_s = din("sgu_w_s", [16, 128, 128])
    b_s = din("sgu_b_s", [1, 2048])
    w_out_odd = din("w_out_odd", [4096, D])
    norm_final = din("norm_final", [1, D])
    t_cos_all = din("t_cos_all", [128, T])
    t_sin_all = din("t_sin_all", [128, T])
    t_cos_own = din("t_cos_own", [128, TO])
    t_sin_own = din("t_sin_own", [128, TO])
    t_R = din("t_R", [128, 128], BF16)
    t_ident = din("t_ident", [128, 128], BF16)
    t_identf = din("t_identf", [128, 128], F32)
    t_ones = din("t_ones", [128, 128], BF16)
    t_addmask = din("t_addmask", [128, 8 * 32])
    t_ncmask = din("t_ncmask", [128, 8 * 512], BF16)
    t_overlap = din("t_overlap", [128, 32], BF16)
    t_expand = din("t_expand", [128, T], BF16)
    t_ndmask = din("t_ndmask", [128, 4 * 512], BF16)
    t_nwmask = din("t_nwmask", [128, 12 * 512], BF16)
    t_sel48 = din("t_sel48", [128, 48 * 128], BF16)
    t_tril = din("t_tril", [128, 128])

    out = nc.dram_tensor("out", [TO, D], F32, kind="ExternalOutput").ap()

    s_kc = dscr("s_kc", [4, 128, T], BF16)
    s_vc = dscr("s_vc", [4, 128, T], BF16)
    s_ks = dscr("s_ks", [4, 128, T], BF16)
    s_kw = dscr("s_kw", [4, 128, T], BF16)
    s_vs = dscr("s_vs", [128, 16, 512], BF16)
    s_vw = dscr("s_vw", [128, 16, 512], BF16)
    s_q = dscr("s_q", [16, 128, TO], BF16)
    s_qr = dscr("s_qr", [16, 128, TO], BF16)
    s_ng = dscr("s_ng", [16, 128, TO], BF16)
    s_gt = dscr("s_gt", [48, TO], BF16)
    s_y = dscr("s_y", [128, 32, TO], BF16)
    s_x1 = dscr("s_x1", [TO, D], F32)
    s_y2 = dscr("s_y2", [32, 128, TO], BF16)
    s_dbg = dscr("s_dbg", [128, 2048], F32)
    s_dbgb = dscr("s_dbgb", [128, 1024], BF16)

    top = ExitStack()

    def sbt(st, name, shape, dt):
        return st.enter_context(nc.sbuf_tensor(name, list(shape), dt))

    ps = [top.enter_context(nc.psum_tensor("ps%d" % i, [128, 512], F32)) for i in range(7)]
    Dps = [Dep(excl=True) for _ in range(7)]
    psb = top.enter_context(nc.psum_tensor("psb", [128, 1024], BF16))
    Dpsb = Dep(excl=True)

    ident = sbt(top, "ident", [128, 128], BF16)
    identf = sbt(top, "identf", [128, 128], F32)
    ones = sbt(top, "ones", [128, 128], BF16)
    epsc = sbt(top, "epsc", [128, 1], F32)
    Dconst = Dep()
    c.dma("sp", ident[:], t_ident, writes=[Dconst])
    c.dma("sp", identf[:], t_identf, writes=[Dconst])
    c.dma("sp", ones[:], t_ones, writes=[Dconst])
    c.op("dve", lambda e: e.memset(epsc[:], EPS), writes=[Dconst])

    act_flip = [0]

    def evac(out_ap, in_ap, reads, writes):
        act_flip[0] ^= 1
        if act_flip[0]:
            c.op("act", lambda e: e.copy(out=out_ap, in_=in_ap), reads, writes)
        else:
            c.op("dve", lambda e: e.tensor_copy(out=out_ap, in_=in_ap), reads, writes)

    def norm_parts(st, x_src, gain_src, hT, DhT, prefix):
        gbc = sbt(st, prefix + "gbc", [128, D], F32)
        xt = [sbt(st, prefix + "xt%d" % i, [128, D], F32) for i in range(2)]
        hb = [sbt(st, prefix + "hb%d" % i, [128, D], BF16) for i in range(2)]
        stat = sbt(st, prefix + "stat", [128, 4], F32)
        Dg, Dst = Dep(), Dep()
        Dxt = [Dep(), Dep()]
        Dhb = [Dep(), Dep()]
        c.dma("sp", gbc[:], gain_src.to_broadcast([128, D]), writes=[Dg])
        nr = 128

        def partA(t):
            b = t % 2
            c.dma("sp", xt[b][0:nr, :], x_src[t * nr:(t + 1) * nr, :], writes=[Dxt[b]])
            c.op("act", lambda e: e.activation(out=hb[b][0:nr, :], in_=xt[b][0:nr, :], func=AF.Square,
                                               accum_out=stat[0:nr, 0:1]), reads=[Dxt[b]], writes=[Dhb[b], Dst])
            c.op("act", lambda e: e.activation(out=stat[0:nr, 1:2], in_=stat[0:nr, 0:1], func=AF.Sqrt,
                                               scale=1.0 / D, bias=epsc[0:nr, :]), reads=[Dst, Dconst], writes=[Dst])
            c.op("dve", lambda e: e.reciprocal(out=stat[0:nr, 2:3], in_=stat[0:nr, 1:2]), reads=[Dst], writes=[Dst])
            c.op("dve", lambda e: e.scalar_tensor_tensor(out=hb[b][0:nr, :], in0=xt[b][0:nr, :], scalar=stat[0:nr, 2:3],
                                                         in1=gbc[0:nr, :], op0=ALU.mult, op1=ALU.mult),
                 reads=[Dxt[b], Dst, Dg, Dhb[b]], writes=[Dhb[b]])

        def partB(t):
            b = t % 2
            Dt = DhT[t] if isinstance(DhT, list) else DhT
            for a in range(4):
                tgt, Dtgt = (psb[:], Dpsb) if a % 2 == 0 else (ps[6][:].bitcast(BF16), Dps[6])
                for j in range(4):
                    kc = 4 * a + j
                    c.op("pe", lambda e: e.transpose(out=tgt[:, j * nr:(j + 1) * nr], in_=hb[b][0:nr, kc * 128:(kc + 1) * 128],
                                                     identity=ident[0:nr, 0:nr]), reads=[Dhb[b], Dconst], writes=[Dtgt])
                evac(hT[:, 4 * a:4 * a + 4, t * nr:(t + 1) * nr],
                     tgt[:, 0:4 * nr].rearrange("p (a b) -> p a b", b=nr), [Dtgt], [Dt])
        return partA, partB

    def norm_transpose(st, x_src, ntiles, gain_src, hT, DhT, prefix):
        pa, pb = norm_parts(st, x_src, gain_src, hT, DhT, prefix)
        for t in range(ntiles):
            pa(t)
            pb(t)

    class WS:
        def __init__(self, st, name, n=3, cols=512):
            self.t = [sbt(st, "%s%d" % (name, i), [128, 16, cols], BF16) for i in range(n)]
            self.d = [Dep() for _ in range(n)]
            self.i = 0

        def load(self, w_ap, pieces):
            k = self.i % len(self.t)
            self.i += 1
            wv = w_ap.rearrange("(kc p) n -> p kc n", p=128)
            for (co, ncol, dc) in pieces:
                for h in range(4):
                    c.dma("pool", self.t[k][:, 4 * h:4 * h + 4, dc:dc + ncol], wv[:, 4 * h:4 * h + 4, co:co + ncol],
                          writes=[self.d[k]])
            return self.t[k], self.d[k]

    psrot = [0]

    def next_ps(n=5):
        psrot[0] = (psrot[0] + 1) % n
        return psrot[0]

    def mm_group(pi, col0, ncols, lhs_fn, rhs_fn, reads, M=128, nk=16):
        for kc in range(nk):
            c.op("pe", lambda e: e.matmul(ps[pi][0:M, col0:col0 + ncols], lhsT=lhs_fn(kc), rhs=rhs_fn(kc),
                                          start=(kc == 0), stop=(kc == nk - 1)), reads=reads, writes=[Dps[pi]])

    def rope_epi(pi, ncols, plainS, Dplain, rotS, Drot, scol, cosT, sinT, tcol, Dtab, Rm, tmp, Dtmp, tmpb, Dtmpb):
        P = ps[pi]
        if plainS is not None:
            src, Dsrc, so = plainS, Dplain, scol
        else:
            src, Dsrc, so = tmpb, Dtmpb, 0
        c.op("act", lambda e: e.copy(out=src[:, so:so + ncols], in_=P[:, 0:ncols]), reads=[Dps[pi]], writes=[Dsrc])
        c.op("pe", lambda e: e.matmul(ps[5][:, 0:ncols], lhsT=Rm[:, :], rhs=src[:, so:so + ncols],
                                      start=True, stop=True), reads=[Dsrc, Dtab], writes=[Dps[5]])
        import os
        RL = int(os.environ.get("RL", "9"))
        if RL < 3:
            return
        c.op("dve", lambda e: e.tensor_tensor(out=tmp[:, 0:ncols], in0=P[:, 0:ncols], in1=cosT[:, tcol:tcol + ncols],
                                              op=ALU.mult), reads=[Dps[pi], Dtab], writes=[Dtmp])
        if RL < 4:
            return
        c.op("dve", lambda e: e.tensor_tensor(out=tmp[:, 512:512 + ncols], in0=ps[5][:, 0:ncols],
                                              in1=sinT[:, tcol:tcol + ncols], op=ALU.mult),
             reads=[Dps[5], Dtab, Dtmp], writes=[Dtmp])
        if RL < 5:
            return
        c.op("dve", lambda e: e.tensor_tensor(out=rotS[:, scol:scol + ncols], in0=tmp[:, 0:ncols],
                                              in1=tmp[:, 512:512 + ncols], op=ALU.add), reads=[Dtmp], writes=[Drot])

    if stop >= 1:
        with ExitStack() as st:
            hT = sbt(st, "hTall", [128, 16, T], BF16)
            DhT = [Dep() for _ in range(16)]
            pA, pB = norm_parts(st, x_all, norm_even, hT, DhT, "a1")
            ws = WS(st, "a1w")
            cosT = sbt(st, "a1cos", [128, T], F32)
            sinT = sbt(st, "a1sin", [128, T], F32)
            Rm = sbt(st, "a1R", [128, 128], BF16)
            tmp = sbt(st, "a1tmp", [128, 1024], F32)
            tmpb = sbt(st, "a1tmpb", [128, 512], BF16)
            Dtmpb = Dep()
            Dtab, Dtmp = Dep(), Dep()
            stg = [sbt(st, "a1stg%d" % i, [128, T], BF16) for i in range(4)]
            Dstg = [Dep() for _ in range(4)]
            vst = [sbt(st, "a1vst%d" % i, [128, 512], BF16) for i in range(4)]
            Dvst = [Dep() for _ in range(4)]
            for t in range(4):
                pA(t)
                pB(t)
            pA(4)
            c.dma("sp", cosT[:], t_cos_all, writes=[Dtab])
            c.dma("sp", sinT[:], t_sin_all, writes=[Dtab])
            c.dma("sp", Rm[:], t_R, writes=[Dtab])
            nxt = [4]

            def inject():
                n = nxt[0]
                if n <= 15:
                    pB(n)
                    if n + 1 <= 15:
                        pA(n + 1)
                    nxt[0] = n + 1

            wt, Dw = ws.load(w_in_even, [(OFF["kc"], 512, 0)])
            for tq in range(4):
                for g in range(4):
                    inject()
                    pi = next_ps()
                    mm_group(pi, 0, 512, lambda kc: wt[:, kc, g * 128:(g + 1) * 128],
                             lambda kc: hT[:, kc, tq * 512:(tq + 1) * 512], [Dw] + DhT[4 * tq:4 * tq + 4])
                    evac(stg[g][:, tq * 512:(tq + 1) * 512], ps[pi][:, :], [Dps[pi]], [Dstg[g]])
            while nxt[0] <= 15:
                inject()
            for g in range(4):
                c.dma("sp", s_kc[g], stg[g][:], reads=[Dstg[g]], writes=[Dep()])
            si = 0
            for name, dst, rope in (("vc", s_vc, False), ("ks", s_ks, True), ("kw", s_kw, True)):
                wt, Dw = ws.load(w_in_even, [(OFF[name], 512, 0)])
                for g in range(4):
                    S, DS = stg[si % 4], Dstg[si % 4]
                    si += 1
                    for tq in range(4):
                        pi = next_ps()
                        mm_group(pi, 0, 512, lambda kc: wt[:, kc, g * 128:(g + 1) * 128],
                                 lambda kc: hT[:, kc, tq * 512:(tq + 1) * 512], [Dw] + DhT[4 * tq:4 * tq + 4])
                        if rope:
                            rope_epi(pi, 512, None, None, S, DS, tq * 512, cosT, sinT, tq * 512, Dtab, Rm, tmp, Dtmp, tmpb, Dtmpb)
                        else:
                            evac(S[:, tq * 512:(tq + 1) * 512], ps[pi][:, :], [Dps[pi]], [DS])
                    c.dma("sp", dst[g], S[:], reads=[DS], writes=[Dep()])
            vi = 0
            for name, dst in (("vs", s_vs), ("vw", s_vw)):
                wt, Dw = ws.load(w_in_even, [(OFF[name], 512, 0)])
                for tt in range(16):
                    pi = next_ps()
                    mm_group(pi, 0, 512, lambda kc: hT[:, kc, tt * 128:(tt + 1) * 128], lambda kc: wt[:, kc, :], [Dw, DhT[tt]])
                    b = vi % 4
                    vi += 1
                    evac(vst[b][:], ps[pi][:, :], [Dps[pi]], [Dvst[b]])
                    c.dma("sp", dst[:, tt, :], vst[b][:], reads=[Dvst[b]], writes=[Dep()])
            c.barrier()

    if stop >= 2:
        with ExitStack() as st:
            hT = sbt(st, "hTown", [128, 16, TO], BF16)
            hTh = sbt(st, "hThalo", [128, 16, 128], BF16)
            DhT, DhTh = Dep(), Dep()
            with ExitStack() as st2:
                norm_transpose(st2, x_own, 8, norm_even, hT, DhT, "a2")
                c.barrier()
            with ExitStack() as st2:
                norm_transpose(st2, x_halo, 1, norm_even, hTh, DhTh, "a2h")
                c.barrier()
            ws = WS(st, "a2w", n=4)
            cosT = sbt(st, "a2cos", [128, TO], F32)
            sinT = sbt(st, "a2sin", [128, TO], F32)
            Rm = sbt(st, "a2R", [128, 128], BF16)
            tmp = sbt(st, "a2tmp", [128, 1024], F32)
            tmpb = sbt(st, "a2tmpb", [128, 512], BF16)
            Dtmpb = Dep()
            cwr = sbt(st, "a2cwr", [128, 128], F32)
            cw = sbt(st, "a2cw", [128, 48], F32)
            Dtab, Dtmp, Dcw = Dep(), Dep(), Dep()
            c.dma("sp", cosT[:], t_cos_own, writes=[Dtab])
            c.dma("sp", sinT[:], t_sin_own, writes=[Dtab])
            c.dma("sp", Rm[:], t_R, writes=[Dtab])
            Dz = Dep()
            c.op("dve", lambda e: e.memset(cwr[:], 0.0), writes=[Dcw, Dz])
            c.dma("sp", cwr[0:48, :], conv_w, reads=[Dz], writes=[Dcw])
            c.op("pe", lambda e: e.transpose(out=ps[6][:, 0:128], in_=cwr[:, :], identity=identf[:, :]),
                 reads=[Dcw, Dconst], writes=[Dps[6]])
            c.op("dve", lambda e: e.tensor_copy(out=cw[:], in_=ps[6][:, 0:48]), reads=[Dps[6]], writes=[Dcw])
            NB = 2
            ccS = [sbt(st, "ccS%d" % i, [128, TO], F32) for i in range(NB)]
            cbS = [sbt(st, "cbS%d" % i, [128, TO], F32) for i in range(NB)]
            sgS = [sbt(st, "sgS%d" % i, [128, TO], F32) for i in range(NB)]
            uS = [sbt(st, "uS%d" % i, [128, 8, 130], F32) for i in range(NB)]
            acc = [sbt(st, "acc%d" % i, [128, TO], F32) for i in range(NB)]
            yS = [sbt(st, "yS%d" % i, [128, TO], BF16) for i in range(NB)]
            hcc = sbt(st, "hcc", [128, 16], F32)
            Dcc, Dcb, Dsg, Du, Dacc, DyS = [[Dep() for _ in range(NB)] for _ in range(6)]
            Dhcc = Dep()
            for ct in range(16):
                b = ct % NB
                wt, Dw = ws.load(w_in_even, [(OFF["cc"] + ct * 128, 128, 0), (OFF["ch"] + ct * 128, 128, 128),
                                             (OFF["cb"] + ct * 128, 128, 256), (OFF["cg"] + ct * 128, 128, 384)])
                for f in range(2):
                    mm_group(6, 16 * f, 16, lambda kc: wt[:, kc, f * 128:(f + 1) * 128], lambda kc: hTh[:, kc, 0:16], [Dw, DhTh])
                c.op("act", lambda e: e.copy(out=hcc[:], in_=ps[6][:, 0:16]), reads=[Dps[6]], writes=[Dhcc])
                c.op("dve", lambda e: e.tensor_tensor(out=uS[b][:, :, 0:2], in0=hcc[:].rearrange("p (a b) -> p a b", b=2),
                                                      in1=ps[6][:, 16:32].rearrange("p (a b) -> p a b", b=2), op=ALU.mult),
                     reads=[Dhcc, Dps[6]], writes=[Du[b]])
                for th in range(2):
                    tsl = slice(th * 512, (th + 1) * 512)
                    rhs = lambda kc: hT[:, kc, tsl]
                    pi = next_ps()
                    mm_group(pi, 0, 512, lambda kc: wt[:, kc, 0:128], rhs, [Dw, DhT])
                    c.op("act", lambda e: e.copy(out=ccS[b][:, tsl], in_=ps[pi][:, :]), reads=[Dps[pi]], writes=[Dcc[b]])
                    pi = next_ps()
                    mm_group(pi, 0, 512, lambda kc: wt[:, kc, 128:256], rhs, [Dw, DhT])
                    c.op("dve", lambda e: e.tensor_tensor(out=uS[b][:, 4 * th:4 * th + 4, 2:130],
                                                          in0=ccS[b][:, tsl].rearrange("p (a b) -> p a b", b=128),
                                                          in1=ps[pi][:, :].rearrange("p (a b) -> p a b", b=128), op=ALU.mult),
                         reads=[Dcc[b], Dps[pi]], writes=[Du[b]])
                    pi = next_ps()
                    mm_group(pi, 0, 512, lambda kc: wt[:, kc, 256:384], rhs, [Dw, DhT])
                    c.op("act", lambda e: e.copy(out=cbS[b][:, tsl], in_=ps[pi][:, :]), reads=[Dps[pi]], writes=[Dcb[b]])
                    pi = next_ps()
                    mm_group(pi, 0, 512, lambda kc: wt[:, kc, 384:512], rhs, [Dw, DhT])
                    c.op("act", lambda e: e.activation(out=sgS[b][:, tsl], in_=ps[pi][:, :], func=AF.Silu),
                         reads=[Dps[pi]], writes=[Dsg[b]])
                a3 = acc[b][:, :].rearrange("p (a b) -> p a b", b=128)
                c.op("dve", lambda e: e.tensor_scalar(out=a3, in0=uS[b][:, :, 2:130], scalar1=cw[:, 32 + ct:33 + ct], scalar2=None,
                                                      op0=ALU.mult), reads=[Du[b], Dcw], writes=[Dacc[b]])
                c.op("dve", lambda e: e.scalar_tensor_tensor(out=a3, in0=uS[b][:, :, 1:129], scalar=cw[:, 16 + ct:17 + ct], in1=a3,
                                                             op0=ALU.mult, op1=ALU.add), reads=[Du[b], Dcw, Dacc[b]], writes=[Dacc[b]])
                c.op("dve", lambda e: e.scalar_tensor_tensor(out=a3, in0=uS[b][:, :, 0:128], scalar=cw[:, ct:ct + 1], in1=a3,
                                                             op0=ALU.mult, op1=ALU.add), reads=[Du[b], Dcw, Dacc[b]], writes=[Dacc[b]])
                c.op("dve", lambda e: e.tensor_tensor(out=acc[b][:, :], in0=acc[b][:, :], in1=cbS[b][:, :], op=ALU.mult),
                     reads=[Dacc[b], Dcb[b]], writes=[Dacc[b]])
                c.op("dve", lambda e: e.tensor_tensor(out=yS[b][:, :], in0=acc[b][:, :], in1=sgS[b][:, :], op=ALU.mult),
                     reads=[Dacc[b], Dsg[b]], writes=[DyS[b]])
                c.dma("sp", s_y[:, ct, :], yS[b][:], reads=[DyS[b]], writes=[Dep()])
            qS = [sbt(st, "qS%d" % i, [128, TO], BF16) for i in range(2)]
            qrS = [sbt(st, "qrS%d" % i, [128, TO], BF16) for i in range(2)]
            DqS = [Dep(), Dep()]
            DqrS = [Dep(), Dep()]
            si = 0
            for a in range(4):
                wt, Dw = ws.load(w_in_even, [(OFF["q"] + a * 512, 512, 0)])
                for j in range(4):
                    b = si % 2
                    si += 1
                    for th in range(2):
                        pi = next_ps()
                        mm_group(pi, 0, 512, lambda kc: wt[:, kc, j * 128:(j + 1) * 128],
                                 lambda kc: hT[:, kc, th * 512:(th + 1) * 512], [Dw, DhT])
                        rope_epi(pi, 512, qS[b], DqS[b], qrS[b], DqrS[b], th * 512, cosT, sinT, th * 512, Dtab, Rm, tmp, Dtmp, tmpb, Dtmpb)
                    c.dma("sp", s_q[4 * a + j], qS[b][:], reads=[DqS[b]], writes=[Dep()])
                    c.dma("sp", s_qr[4 * a + j], qrS[b][:], reads=[DqrS[b]], writes=[Dep()])
            for a in range(4):
                wt, Dw = ws.load(w_in_even, [(OFF["ng"] + a * 512, 512, 0)])
                for j in range(4):
                    b = si % 2
                    si += 1
                    for th in range(2):
                        pi = next_ps()
                        mm_group(pi, 0, 512, lambda kc: wt[:, kc, j * 128:(j + 1) * 128],
                                 lambda kc: hT[:, kc, th * 512:(th + 1) * 512], [Dw, DhT])
                        c.op("act", lambda e: e.activation(out=qS[b][:, th * 512:(th + 1) * 512], in_=ps[pi][:, :], func=AF.Silu),
                             reads=[Dps[pi]], writes=[DqS[b]])
                    c.dma("sp", s_ng[4 * a + j], qS[b][:], reads=[DqS[b]], writes=[Dep()])
            wt, Dw = ws.load(w_in_even, [(OFF["gl"], 48, 0)])
            b = si % 2
            for th in range(2):
                pi = next_ps()
                mm_group(pi, 0, 512, lambda kc: wt[:, kc, 0:128], lambda kc: hT[:, kc, th * 512:(th + 1) * 512], [Dw, DhT])
                c.op("act", lambda e: e.activation(out=qS[b][0:48, th * 512:(th + 1) * 512], in_=ps[pi][0:48, :], func=AF.Sigmoid),
                     reads=[Dps[pi]], writes=[DqS[b]])
            c.dma("sp", s_gt, qS[b][0:48, :], reads=[DqS[b]], writes=[Dep()])
            c.barrier()

    if stop >= 3:
        with ExitStack() as st:
            kcomp = sbt(st, "kcomp", [128, 4, 128], BF16)
            vcomp = sbt(st, "vcomp", [128, 4, 128], BF16)
            Dkcomp, Dvcomp = Dep(), Dep()
            c.op("dve", lambda e: e.memset(kcomp[:], 0.0), writes=[Dkcomp])
            c.op("dve", lambda e: e.memset(vcomp[:], 0.0), writes=[Dvcomp])
            with ExitStack() as st2:
                w1 = sbt(st2, "bw1", [128, 32, 128], BF16)
                w2 = sbt(st2, "bw2", [128, 128], BF16)
                posr = sbt(st2, "bposr", [128, 128], BF16)
                posT = sbt(st2, "bposT", [128, 128], BF16)
                b1 = sbt(st2, "bb1", [128, 1], F32)
                bias = sbt(st2, "bbias", [128, 1], F32)
                xin = [sbt(st2, "bxin%d" % i, [128, T], BF16) for i in range(2)]
                hid = sbt(st2, "bhid", [128, 128], BF16)
                Dw1, Dw2, Dpos, Db1, Dbias, Dhid = [Dep() for _ in range(6)]
                Dxin = [Dep(), Dep()]
                xi = 0
                c.op("dve", lambda e: e.memset(hid[:], 0.0), writes=[Dhid])
                for kv, (pos_d, w1_d, b1_d, w2_d, src) in enumerate(((kpos, kw1, kb1, kw2, s_kc), (vpos, vw1, vb1, vw2, s_vc))):
                    w1v = w1_d.rearrange("(l d) j -> d l j", d=128)
                    for h in range(8):
                        c.dma("pool", w1[:, 4 * h:4 * h + 4, :], w1v[:, 4 * h:4 * h + 4, :], writes=[Dw1])
                    c.dma("pool", w2[:], w2_d, writes=[Dw2])
                    Dz = Dep()
                    c.op("dve", lambda e: e.memset(posr[:], 0.0), writes=[Dpos, Dz])
                    c.dma("pool", posr[0:32, :], pos_d, reads=[Dz], writes=[Dpos])
                    c.dma("sp", b1[:], b1_d, writes=[Db1])
                    c.op("pe", lambda e: e.transpose(out=psb[:, 0:128], in_=posr[:, :], identity=ident[:, :]),
                         reads=[Dpos, Dconst], writes=[Dpsb])
                    c.op("dve", lambda e: e.tensor_copy(out=posT[:], in_=psb[:, 0:128]), reads=[Dpsb], writes=[Dpos])
                    for l in range(32):
                        c.op("pe", lambda e: e.matmul(ps[6][:, 0:1], lhsT=w1[:, l, :], rhs=posT[:, l:l + 1], start=(l == 0), stop=(l == 31)),
                             reads=[Dw1, Dpos], writes=[Dps[6]])
                    c.op("dve", lambda e: e.tensor_tensor(out=bias[:], in0=ps[6][:, 0:1], in1=b1[:], op=ALU.add),
                         reads=[Dps[6], Db1], writes=[Dbias])
                    for g in range(4):
                        X, DX = xin[xi % 2], Dxin[xi % 2]
                        xi += 1
                        c.dma("sp", X[:], src[g], writes=[DX])
                        pi = next_ps()
                        for l in range(32):
                            c.op("pe", lambda e: e.matmul(ps[pi][:, 0:127], lhsT=w1[:, l, :], rhs=X[:, l:l + 16 * 126 + 1:16],
                                                          start=(l == 0), stop=(l == 31)), reads=[Dw1, DX], writes=[Dps[pi]])
                        c.op("act", lambda e: e.activation(out=hid[:, 0:127], in_=ps[pi][:, 0:127], func=AF.Silu, bias=bias[:]),
                             reads=[Dps[pi], Dbias], writes=[Dhid])
                        pi = next_ps()
                        if kv == 0:
                            c.op("pe", lambda e: e.matmul(ps[pi][:, 0:127], lhsT=w2[:, :], rhs=hid[:, 0:127], start=True, stop=True),
                                 reads=[Dw2, Dhid], writes=[Dps[pi]])
                            c.op("dve", lambda e: e.tensor_copy(out=kcomp[:, g, 0:127], in_=ps[pi][:, 0:127]),
                                 reads=[Dps[pi]], writes=[Dkcomp])
                        else:
                            c.op("pe", lambda e: e.matmul(ps[pi][:, 0:128], lhsT=hid[:, :], rhs=w2[:, :], start=True, stop=True),
                                 reads=[Dw2, Dhid], writes=[Dps[pi]])
                            c.op("dve", lambda e: e.tensor_copy(out=vcomp[0:127, g, :], in_=ps[pi][0:127, 0:128]),
                                 reads=[Dps[pi]], writes=[Dvcomp])
                c.barrier()
            if debug:
                c.dma("sp", s_dbgb[:, 0:512], kcomp[:].rearrange("p a b -> p (a b)"), reads=[Dkcomp], writes=[Dep()])
                c.dma("sp", s_dbgb[:, 512:1024], vcomp[:].rearrange("p a b -> p (a b)"), reads=[Dvcomp], writes=[Dep()])
            if stop >= 4:
                NEGM = 30000.0
                ps.append(psb[:].bitcast(F32))
                Dps.append(Dpsb)
                gates = sbt(st, "gates", [128, TO], BF16)
                sel48 = sbt(st, "sel48", [128, 48 * 128], BF16)
                ncmask = sbt(st, "ncmask", [128, 8 * 512], BF16)
                overlap = sbt(st, "overlap", [128, 32], BF16)
                expand = sbt(st, "expand", [128, T], BF16)
                ndmask = sbt(st, "ndmask", [128, 4 * 512], BF16)
                nwmask = sbt(st, "nwmask", [128, 12 * 512], BF16)
                addm = sbt(st, "addm", [128, 256], F32)
                Dtb = Dep()
                Dz = Dep()
                c.op("dve", lambda e: e.memset(gates[:], 0.0), writes=[Dz])
                c.dma("sp", gates[0:48, :], s_gt, reads=[Dz], writes=[Dtb])
                for dst_, src_ in ((sel48, t_sel48), (ncmask, t_ncmask), (overlap, t_overlap), (expand, t_expand),
                                   (ndmask, t_ndmask), (nwmask, t_nwmask), (addm, t_addmask)):
                    c.dma("sp", dst_[:], src_, writes=[Dtb])
                NG = 2
                qg = [sbt(st, "qg%d" % i, [128, 4, TO], BF16) for i in range(NG)]
                qrg = [sbt(st, "qrg%d" % i, [128, 4, TO], BF16) for i in range(NG)]
                ngg = [sbt(st, "ngg%d" % i, [128, 4, TO], BF16) for i in range(NG)]
                ksg = [sbt(st, "ksg%d" % i, [128, T], BF16) for i in range(NG)]
                kwg = [sbt(st, "kwg%d" % i, [128, T], BF16) for i in range(NG)]
                vsg = [sbt(st, "vsg%d" % i, [128, 16, 128], BF16) for i in range(NG)]
                vwg = [sbt(st, "vwg%d" % i, [128, 16, 128], BF16) for i in range(NG)]
                yst = [sbt(st, "yst%d" % i, [128, 4, TO], BF16) for i in range(NG)]
                Dgl = [Dep() for _ in range(NG)]
                Dyst = [Dep() for _ in range(NG)]
                NE = 6
                Et = [sbt(st, "Et%d" % i, [128, 512], BF16) for i in range(NE)]
                DEt = [Dep() for _ in range(NE)]
                Ec = [sbt(st, "Ec%d" % i, [128, 512], BF16) for i in range(3)]
                Pn = [sbt(st, "Pn%d" % i, [128, 512], BF16) for i in range(3)]
                oacc = [sbt(st, "oacc%d" % i, [128, 512], F32) for i in range(3)]
                otmp = [sbt(st, "otmp%d" % i, [128, 512], F32) for i in range(2)]
                rden0 = [sbt(st, "rden0%d" % i, [128, 512], F32) for i in range(3)]
                rdenB = [sbt(st, "rdenB%d" % i, [128, 512], F32) for i in range(2)]
                gsb = [[sbt(st, "gsb%d_%d" % (i, j), [128, 512], F32) for j in range(3)] for i in range(3)]
                negsel = [sbt(st, "negsel%d" % i, [128, 512], BF16) for i in range(3)]
                imp2 = sbt(st, "imp2", [128, 32], F32)
                top8 = sbt(st, "top8", [128, 8], F32)
                selF = sbt(st, "selF", [128, 128], F32)
                tden = sbt(st, "tden", [128, 512], F32)
                Dimp2, Dtop8, DselF, Dtden = [Dep() for _ in range(4)]
                DEc = [Dep() for _ in range(3)]
                DPn = [Dep() for _ in range(3)]
                Doacc = [Dep() for _ in range(3)]
                Dotmp = [Dep(), Dep()]
                Drden0 = [Dep() for _ in range(3)]
                DrdenB = [Dep(), Dep()]
                Dgsb = [[Dep() for _ in range(3)] for _ in range(3)]
                Dnegsel = [Dep() for _ in range(3)]
                c.op("dve", lambda e: e.memset(selF[:], 0.0), writes=[DselF])
                rot = [0, 0]

                def sc_next():
                    rot[0] = (rot[0] + 1) % 4
                    return rot[0]

                def E_next():
                    rot[1] = (rot[1] + 1) % NE
                    return rot[1]

                def run_steps(steps, qsrc, ksrc, vsrc, Dg_, pO, pD, stages, depth=3):
                    n = len(steps)
                    slots = []

                    def issue_S(k):
                        kt, extras = steps[k]
                        bk = sc_next()
                        e_ = E_next()
                        c.op("pe", lambda e: e.matmul(ps[bk][:, :], lhsT=ksrc[:, kt * 128:(kt + 1) * 128], rhs=qsrc, start=True,
                                                      stop=(len(extras) == 0)), reads=[Dg_], writes=[Dps[bk]])
                        for j, (l_ap, r_ap, rd) in enumerate(extras):
                            c.op("pe", lambda e: e.matmul(ps[bk][:, :], lhsT=l_ap, rhs=r_ap, start=False, stop=(j == len(extras) - 1)),
                                 reads=rd, writes=[Dps[bk]])
                        c.op("act", lambda e: e.activation(out=Et[e_][:], in_=ps[bk][:, :], func=AF.Exp, scale=SCALE),
                             reads=[Dps[bk]], writes=[DEt[e_]])
                        slots.append(e_)

                    for k in range(min(depth, n)):
                        issue_S(k)
                    for k in range(n):
                        if k + depth < n:
                            issue_S(k + depth)
                        e_ = slots[k]
                        kt = steps[k][0]
                        c.op("pe", lambda e: e.matmul(ps[pO][:, :], lhsT=vsrc[:, kt, :], rhs=Et[e_][:, :], start=(k == 0), stop=(k == n - 1)),
                             reads=[Dg_, DEt[e_]], writes=[Dps[pO]])
                        c.op("pe", lambda e: e.matmul(ps[pD][:, :], lhsT=ones[:, :], rhs=Et[e_][:, :], start=(k == 0), stop=(k == n - 1)),
                             reads=[Dconst, DEt[e_]], writes=[Dps[pD]])
                        if stages:
                            stages.pop(0)()

                def neg_recip(src_ap, Dsrc, rb, Drb, tA, DtA):
                    c.op("dve", lambda e: e.tensor_scalar_max(out=tA[:], in0=src_ap, scalar1=1e-30), reads=[Dsrc], writes=[DtA])
                    c.op("act", lambda e: e.activation(out=rb[:], in_=tA[:], func=AF.Ln), reads=[DtA], writes=[Drb])
                    c.op("act", lambda e: e.activation(out=rb[:], in_=rb[:], func=AF.Exp, scale=-1.0), reads=[Drb], writes=[Drb])
                    c.op("dve", lambda e: e.tensor_tensor(out=tA[:], in0=tA[:], in1=rb[:], op=ALU.mult), reads=[DtA, Drb], writes=[DtA])
                    c.op("dve", lambda e: e.scalar_tensor_tensor(out=rb[:], in0=tA[:], scalar=2.0, in1=rb[:], op0=ALU.subtract, op1=ALU.mult),
                         reads=[DtA, Drb], writes=[Drb])

                def finish(pO, pD, gs, Dgs, ob, oa):
                    rb, Drb = rdenB[ob], DrdenB[ob]
                    neg_recip(ps[pD][:, :], Dps[pD], rb, Drb, otmp[ob], Dotmp[ob])
                    c.op("dve", lambda e: e.scalar_tensor_tensor(out=rb[:], in0=rb[:], scalar=-1.0, in1=gs[:], op0=ALU.mult, op1=ALU.mult),
                         reads=[Drb, Dgs], writes=[Drb])
                    c.op("dve", lambda e: e.tensor_tensor(out=otmp[ob][:], in0=ps[pO][:, :], in1=rb[:], op=ALU.mult),
                         reads=[Dps[pO], Drb], writes=[Dotmp[ob]])
                    c.op("pool", lambda e: e.tensor_tensor(out=oacc[oa][:], in0=oacc[oa][:], in1=otmp[ob][:], op=ALU.add),
                         reads=[Doacc[oa], Dotmp[ob]], writes=[Doacc[oa]])

                def load_group(g):
                    b = g % NG
                    c.dma("sp", qg[b][:], s_q[4 * g:4 * g + 4].rearrange("n p t -> p n t"), writes=[Dgl[b]])
                    c.dma("sp", qrg[b][:], s_qr[4 * g:4 * g + 4].rearrange("n p t -> p n t"), writes=[Dgl[b]])
                    c.dma("sp", ngg[b][:], s_ng[4 * g:4 * g + 4].rearrange("n p t -> p n t"), writes=[Dgl[b]])
                    c.dma("sp", ksg[b][:], s_ks[g], writes=[Dgl[b]])
                    c.dma("sp", kwg[b][:], s_kw[g], writes=[Dgl[b]])
                    c.dma("sp", vsg[b][:], s_vs[:, :, g * 128:(g + 1) * 128], writes=[Dgl[b]])
                    c.dma("sp", vwg[b][:], s_vw[:, :, g * 128:(g + 1) * 128], writes=[Dgl[b]])

                def prep(g, i):
                    b = g % NG
                    pb = (g * 8 + i) % 3
                    tsl = slice(i * 128, (i + 1) * 128)
                    q_t = qg[b][:, :, tsl]
                    for br in range(3):
                        bk = sc_next()
                        for n in range(4):
                            r = br * 16 + g * 4 + n
                            c.op("pe", lambda e: e.matmul(ps[bk][:, n * 128:(n + 1) * 128], lhsT=sel48[:, r * 128:(r + 1) * 128],
                                                          rhs=gates[:, tsl], start=True, stop=True), reads=[Dtb], writes=[Dps[bk]])
                        c.op("dve", lambda e: e.tensor_copy(out=gsb[pb][br][:], in_=ps[bk][:, :]), reads=[Dps[bk]], writes=[Dgsb[pb][br]])
                    bc = sc_next()
                    c.op("pe", lambda e: e.matmul(ps[bc][:, :], lhsT=kcomp[:, g, :], rhs=q_t, start=True, stop=False),
                         reads=[Dkcomp, Dgl[b]], writes=[Dps[bc]])
                    c.op("pe", lambda e: e.matmul(ps[bc][:, :], lhsT=ident[:, :], rhs=ncmask[:, i * 512:(i + 1) * 512], start=False, stop=True),
                         reads=[Dtb, Dconst], writes=[Dps[bc]])
                    c.op("act", lambda e: e.activation(out=Ec[pb][:], in_=ps[bc][:, :], func=AF.Exp, scale=SCALE),
                         reads=[Dps[bc]], writes=[DEc[pb]])

                    def stage_den():
                        bd = sc_next()
                        c.op("pe", lambda e: e.matmul(ps[bd][:, :], lhsT=ones[:, :], rhs=Ec[pb][:, :], start=True, stop=True),
                             reads=[Dconst, DEc[pb]], writes=[Dps[bd]])
                        neg_recip(ps[bd][:, :], Dps[bd], rden0[pb], Drden0[pb], tden, Dtden)
                        c.op("dve", lambda e: e.scalar_tensor_tensor(out=Pn[pb][:], in0=Ec[pb][:], scalar=-1.0, in1=rden0[pb][:],
                                                                     op0=ALU.mult, op1=ALU.mult),
                             reads=[DEc[pb], Drden0[pb]], writes=[DPn[pb]])

                    def stage_imp():
                        bi = sc_next()
                        for n in range(4):
                            c.op("pe", lambda e: e.matmul(ps[bi][:, 0:32], lhsT=Pn[pb][:, n * 128:(n + 1) * 128], rhs=overlap[:, :],
                                                          start=(n == 0), stop=(n == 3)), reads=[DPn[pb], Dtb], writes=[Dps[bi]])
                        bo = sc_next()
                        c.op("pe", lambda e: e.matmul(ps[bo][:, :], lhsT=vcomp[:, g, :], rhs=Pn[pb][:, :], start=True, stop=True),
                             reads=[Dvcomp, DPn[pb]], writes=[Dps[bo]])
                        c.op("dve", lambda e: e.tensor_tensor(out=imp2[:], in0=ps[bi][:, 0:32], in1=addm[:, i * 32:(i + 1) * 32], op=ALU.add),
                             reads=[Dps[bi], Dtb], writes=[Dimp2])
                        c.op("dve", lambda e: e.max(out=top8[:], in_=imp2[:]), reads=[Dimp2], writes=[Dtop8])
                        c.op("dve", lambda e: e.tensor_scalar_max(out=top8[:, 7:8], in0=top8[:, 7:8], scalar1=-1e29), reads=[Dtop8], writes=[Dtop8])
                        c.op("dve", lambda e: e.tensor_scalar(out=selF[:, 0:32], in0=imp2[:], scalar1=top8[:, 7:8], scalar2=None, op0=ALU.is_ge),
                             reads=[Dimp2, Dtop8], writes=[DselF])
                        c.op("dve", lambda e: e.tensor_tensor(out=oacc[pb][:], in0=ps[bo][:, :], in1=gsb[pb][0][:], op=ALU.mult),
                             reads=[Dps[bo], Dgsb[pb][0]], writes=[Doacc[pb]])
                        if debug and g == 0:
                            c.dma("sp", s_dbg[:, 1024 + i * 32:1024 + (i + 1) * 32], imp2[:], reads=[Dimp2], writes=[Dep()])

                    def stage_tr():
                        bt = sc_next()
                        c.op("pe", lambda e: e.transpose(out=ps[bt][:, 0:128], in_=selF[:, :], identity=identf[:, :]),
                             reads=[DselF, Dconst], writes=[Dps[bt]])
                        c.op("dve", lambda e: e.tensor_scalar(out=negsel[pb][:, :].rearrange("p (a b) -> p a b", b=128),
                                                              in0=ps[bt][:, 0:128].unsqueeze(1).to_broadcast([128, 4, 128]),
                                                              scalar1=-1.0, scalar2=NEGM, op0=ALU.add, op1=ALU.mult),
                             reads=[Dps[bt]], writes=[Dnegsel[pb]])

                    return [stage_den, None, None, None, stage_imp, None, None, stage_tr]

                tiles = [(g, i) for g in range(4) for i in range(8)]
                DEPTH = 3
                pending = []
                cur_stages = []

                def make_step(kt, extras, qsrc, ksrc, vsrc, Dg_, pO, pD, first, last, post):
                    def issue_S():
                        bk = sc_next()
                        e_ = E_next()
                        c.op("pe", lambda e: e.matmul(ps[bk][:, :], lhsT=ksrc[:, kt * 128:(kt + 1) * 128], rhs=qsrc, start=True,
                                                      stop=(len(extras) == 0)), reads=[Dg_], writes=[Dps[bk]])
                        for j, (l_ap, r_ap, rd) in enumerate(extras):
                            c.op("pe", lambda e: e.matmul(ps[bk][:, :], lhsT=l_ap, rhs=r_ap, start=False, stop=(j == len(extras) - 1)),
                                 reads=rd, writes=[Dps[bk]])
                        c.op("act", lambda e: e.activation(out=Et[e_][:], in_=ps[bk][:, :], func=AF.Exp, scale=SCALE),
                             reads=[Dps[bk]], writes=[DEt[e_]])
                        return e_

                    def issue_PV(e_):
                        c.op("pe", lambda e: e.matmul(ps[pO][:, :], lhsT=vsrc[:, kt, :], rhs=Et[e_][:, :], start=first, stop=last),
                             reads=[Dg_, DEt[e_]], writes=[Dps[pO]])
                        c.op("pe", lambda e: e.matmul(ps[pD][:, :], lhsT=ones[:, :], rhs=Et[e_][:, :], start=first, stop=last),
                             reads=[Dconst, DEt[e_]], writes=[Dps[pD]])
                        if post is not None:
                            post()
                    return issue_S, issue_PV

                def pop_one():
                    issue_PV, e_ = pending.pop(0)
                    issue_PV(e_)
                    if cur_stages:
                        (cur_stages.pop(0) or (lambda: None))()

                def push(step):
                    issue_S, issue_PV = step
                    pending.append((issue_PV, issue_S()))
                    if len(pending) > DEPTH:
                        pop_one()

                load_group(0)
                cur_stages.extend(prep(0, 0))
                while cur_stages:
                    (cur_stages.pop(0) or (lambda: None))()
                for ti, (g, i) in enumerate(tiles):
                    b = g % NG
                    p = i & 1
                    oa = ti % 3
                    tsl = slice(i * 128, (i + 1) * 128)
                    qr_t = qrg[b][:, :, tsl]
                    while cur_stages:
                        (cur_stages.pop(0) or (lambda: None))()
                    if i == 1 and g + 1 < 4:
                        load_group(g + 1)
                    if ti + 1 < len(tiles):
                        cur_stages.extend(prep(*tiles[ti + 1]))
                    wsteps = []
                    for o in range(6):
                        kt = 2 * i - 4 + o
                        if kt < 0:
                            continue
                        ex = [] if o in (2, 3) else [(ident[:, :], nwmask[:, (p * 6 + o) * 512:(p * 6 + o + 1) * 512], [Dtb, Dconst])]
                        wsteps.append((kt, ex))
                    for k, (kt, ex) in enumerate(wsteps):
                        lastw = (k == len(wsteps) - 1)
                        post = (lambda oa=oa: finish(4, 5, gsb[oa][2], Dgsb[oa][2], 0, oa)) if lastw else None
                        push(make_step(kt, ex, qr_t, kwg[b], vwg[b], Dgl[b], 4, 5, k == 0, lastw, post))
                    nsl = 2 * i + 2
                    for kt in range(nsl):
                        o = kt - 2 * i
                        ex = [(expand[:, kt * 128:(kt + 1) * 128], negsel[oa][:, :], [Dtb, Dnegsel[oa]])]
                        if o >= 0:
                            ex.append((ident[:, :], ndmask[:, (p * 2 + o) * 512:(p * 2 + o + 1) * 512], [Dtb, Dconst]))
                        lasts = (kt == nsl - 1)

                        def post_s(oa=oa, b=b, tsl=tsl, g=g, i=i):
                            finish(6, 7, gsb[oa][1], Dgsb[oa][1], 1, oa)
                            c.op("pool", lambda e: e.tensor_tensor(out=yst[b][:, :, tsl], in0=oacc[oa][:, :].rearrange("p (a b) -> p a b", b=128),
                                                                   in1=ngg[b][:, :, tsl], op=ALU.mult),
                                 reads=[Doacc[oa], Dgl[b]], writes=[Dyst[b]])
                            if i == 7:
                                c.dma("pool", s_y[:, 16 + 4 * g:16 + 4 * g + 4, :], yst[b][:], reads=[Dyst[b]], writes=[Dep()])
                        push(make_step(kt, ex, qr_t, ksg[b], vsg[b], Dgl[b], 6, 7, kt == 0, lasts, post_s if lasts else None))
                while pending:
                    pop_one()
                while cur_stages:
                    (cur_stages.pop(0) or (lambda: None))()
                ps.pop()
                Dps.pop()
            c.barrier()

    def out_proj(s_ysrc, w_out, x_src, dst, final_gain, prefix):
        with ExitStack() as st:
            wo = sbt(st, prefix + "wo", [128, 32, D], BF16)
            Dwo = [Dep() for _ in range(4)]
            wv = w_out.rearrange("(kc p) n -> p kc n", p=128)
            for nb in range(4):
                for h in range(8):
                    c.dma("pool", wo[:, 4 * h:4 * h + 4, nb * 512:(nb + 1) * 512], wv[:, 4 * h:4 * h + 4, nb * 512:(nb + 1) * 512],
                          writes=[Dwo[nb]])
            yT = [sbt(st, prefix + "yT%d" % i, [128, 32, 128], BF16) for i in range(2)]
            xt = [sbt(st, prefix + "xt%d" % i, [128, D], F32) for i in range(2)]
            xo = xt
            DyT = [Dep(), Dep()]
            Dxt = [Dep(), Dep()]
            Dxo = Dxt
            if final_gain is not None:
                gbc = sbt(st, prefix + "gbc", [128, D], F32)
                junk = sbt(st, prefix + "junk", [128, D], BF16)
                stat = sbt(st, prefix + "stat", [128, 4], F32)
                Dg, Dj, Dst = Dep(), Dep(), Dep()
                c.dma("sp", gbc[:], final_gain.to_broadcast([128, D]), writes=[Dg])
            for tt in range(8):
                b = tt % 2
                c.dma("sp", yT[b][:], s_ysrc[:, :, tt * 128:(tt + 1) * 128].rearrange("k p t -> p k t"), writes=[DyT[b]])
                c.dma("sp", xt[b][:], x_src[tt * 128:(tt + 1) * 128, :], writes=[Dxt[b]])
                for nb in range(4):
                    pi = next_ps()
                    mm_group(pi, 0, 512, lambda kc: yT[b][:, kc, :], lambda kc: wo[:, kc, nb * 512:(nb + 1) * 512], [DyT[b], Dwo[nb]], nk=32)
                    c.op("dve", lambda e: e.tensor_tensor(out=xo[b][:, nb * 512:(nb + 1) * 512], in0=ps[pi][:, :],
                                                          in1=xt[b][:, nb * 512:(nb + 1) * 512], op=ALU.add),
                         reads=[Dps[pi], Dxt[b]], writes=[Dxo[b]])
                if final_gain is not None:
                    c.op("act", lambda e: e.activation(out=junk[:], in_=xo[b][:], func=AF.Square, accum_out=stat[:, 0:1]),
                         reads=[Dxo[b]], writes=[Dj, Dst])
                    c.op("act", lambda e: e.activation(out=stat[:, 1:2], in_=stat[:, 0:1], func=AF.Sqrt, scale=1.0 / D, bias=epsc[:]),
                         reads=[Dst, Dconst], writes=[Dst])
                    c.op("dve", lambda e: e.reciprocal(out=stat[:, 2:3], in_=stat[:, 1:2]), reads=[Dst], writes=[Dst])
                    c.op("dve", lambda e: e.scalar_tensor_tensor(out=xo[b][:], in0=xo[b][:], scalar=stat[:, 2:3], in1=gbc[:],
                                                                 op0=ALU.mult, op1=ALU.mult), reads=[Dxo[b], Dst, Dg], writes=[Dxo[b]])
                c.dma("pool", dst[tt * 128:(tt + 1) * 128, :], xo[b][:], reads=[Dxo[b]], writes=[Dep()])
            c.barrier()

    def out_proj_nb(s_ysrc, w_out, x_src, dst, prefix):
        with ExitStack() as st:
            yT = sbt(st, prefix + "yT", [128, 32, TO], BF16)
            DyT = [Dep()] * 8
            for j in range(4):
                c.dma("sp", yT[:, 8 * j:8 * j + 8, :], s_ysrc[:, 8 * j:8 * j + 8, :], writes=[DyT[0]])
            wo = [sbt(st, prefix + "wo%d" % i, [128, 32, 512], BF16) for i in range(3)]
            Dwo = [Dep() for _ in range(3)]
            wv = w_out.rearrange("(kc p) n -> p kc n", p=128)
            xp = [sbt(st, prefix + "xp%d" % i, [128, 512], F32) for i in range(4)]
            Dxp = [Dep() for _ in range(4)]
            xi = 0
            for nb in range(4):
                k = nb % 3
                for h in range(8):
                    c.dma("pool", wo[k][:, 4 * h:4 * h + 4, :], wv[:, 4 * h:4 * h + 4, nb * 512:(nb + 1) * 512], writes=[Dwo[k]])
                for tt in range(8):
                    b = xi % 4
                    xi += 1
                    c.dma("sp", xp[b][:], x_src[tt * 128:(tt + 1) * 128, nb * 512:(nb + 1) * 512], writes=[Dxp[b]])
                    pi = next_ps()
                    mm_group(pi, 0, 512, lambda kc: yT[:, kc, tt * 128:(tt + 1) * 128], lambda kc: wo[k][:, kc, :], [DyT[tt], Dwo[k]], nk=32)
                    c.op("dve", lambda e: e.tensor_tensor(out=xp[b][:], in0=ps[pi][:, :], in1=xp[b][:], op=ALU.add),
                         reads=[Dps[pi], Dxp[b]], writes=[Dxp[b]])
                    c.dma("act", dst[tt * 128:(tt + 1) * 128, nb * 512:(nb + 1) * 512], xp[b][:], reads=[Dxp[b]], writes=[Dep()])
            c.barrier()

    if stop >= 5:
        out_proj_nb(s_y, w_out_even, x_own, s_x1, "d")

    if stop >= 6:
        with ExitStack() as st:
            hT = sbt(st, "hT1", [128, 16, TO], BF16)
            vsb = sbt(st, "vsb", [128, 8, 4096], BF16)
            Dv = [Dep() for _ in range(8)]
            lg = sbt(st, "lg", [128, 32], F32)
            lb = sbt(st, "lb", [128, 32], F32)
            wsT = sbt(st, "wsT", [128, 16, 128], BF16)
            rsbc = sbt(st, "rsbc", [128, 16, 128], F32)
            bsbc = sbt(st, "bsbc", [128, 16, 128], F32)
            ssum = sbt(st, "fssum", [128, 8, 8], F32)
            ssq = sbt(st, "fssq", [128, 8, 8], F32)
            sqj = sbt(st, "fsqj", [128, 512], BF16)
            stat = sbt(st, "fstat", [128, 8, 8], F32)
            st2 = ExitStack()
            lgr = sbt(st2, "lgr", [128, 128], F32)
            lbr = sbt(st2, "lbr", [128, 128], F32)
            wsr = sbt(st2, "wsr", [128, 16, 128], F32)
            tril = sbt(st2, "tril", [128, 128], F32)
            wsm = sbt(st2, "wsm", [128, 16, 128], BF16)
            Dlg, Dws, DwsT, Drs, Dbs = [Dep() for _ in range(5)]
            Dz = Dep()
            c.op("dve", lambda e: e.memset(lgr[:], 0.0), writes=[Dz])
            c.op("dve", lambda e: e.memset(lbr[:], 0.0), writes=[Dz])
            c.dma("sp", lgr[0:32, :], ln_g, reads=[Dz], writes=[Dlg])
            c.dma("sp", lbr[0:32, :], ln_b, reads=[Dz], writes=[Dlg])
            c.dma("sp", wsr[:], w_s.rearrange("g t s -> t g s"), writes=[Dws])
            c.dma("sp", tril[:], t_tril, writes=[Dws])
            c.dma("sp", bsbc[:].rearrange("p a b -> p (a b)"), b_s.to_broadcast([128, 2048]), writes=[Dbs])
            c.op("pe", lambda e: e.transpose(out=ps[6][:, 0:128], in_=lgr[:, :], identity=identf[:, :]), reads=[Dlg, Dconst], writes=[Dps[6]])
            c.op("pe", lambda e: e.transpose(out=ps[6][:, 128:256], in_=lbr[:, :], identity=identf[:, :]), reads=[Dlg, Dconst], writes=[Dps[6]])
            c.op("dve", lambda e: e.tensor_copy(out=lg[:], in_=ps[6][:, 0:32]), reads=[Dps[6]], writes=[Dlg])
            c.op("dve", lambda e: e.tensor_copy(out=lb[:], in_=ps[6][:, 128:160]), reads=[Dps[6]], writes=[Dlg])
            c.op("dve", lambda e: e.tensor_tensor(out=wsm[:], in0=wsr[:], in1=tril[:, :].unsqueeze(1).to_broadcast([128, 16, 128]), op=ALU.mult),
                 reads=[Dws], writes=[Dws])
            for a in range(2):
                for j in range(8):
                    c.op("pe", lambda e: e.transpose(out=psb[:, j * 128:(j + 1) * 128], in_=wsm[:, 8 * a + j, :], identity=ident[:, :]),
                         reads=[Dws, Dconst], writes=[Dpsb])
                c.op("dve", lambda e: e.tensor_copy(out=wsT[:, 8 * a:8 * a + 8, :], in_=psb[:, :].rearrange("p (a b) -> p a b", b=128)),
                     reads=[Dpsb], writes=[DwsT])
            for a in range(4):
                c.op("pe", lambda e: e.matmul(ps[6][:, :], lhsT=ones[:, :], rhs=wsT[:, 4 * a:4 * a + 4, :], start=True, stop=True),
                     reads=[DwsT, Dconst], writes=[Dps[6]])
                c.op("dve", lambda e: e.tensor_copy(out=rsbc[:, 4 * a:4 * a + 4, :], in_=ps[6][:, :].rearrange("p (a b) -> p a b", b=128)),
                     reads=[Dps[6]], writes=[Drs])
            DhT = [Dep() for _ in range(8)]
            pA, pB = norm_parts(st2, s_x1, norm_odd, hT, DhT, "e")
            w0 = sbt(st2, "fvw0", [128, 16, 512], BF16)
            Dw0 = Dep()
            Dss, Dsq, Dsqj, Dst = Dep(), Dep(), Dep(), Dep()
            wv_ = w_in_odd.rearrange("(kc p) n -> p kc n", p=128)
            for h in range(4):
                c.dma("pool", w0[:, 4 * h:4 * h + 4, :], wv_[:, 4 * h:4 * h + 4, 4096:4096 + 512], writes=[Dw0])

            def v_group(vb, tt, wt, Dw):
                pi = next_ps()
                mm_group(pi, 0, 512, lambda kc: hT[:, kc, tt * 128:(tt + 1) * 128], lambda kc: wt[:, kc, :], [Dw, DhT[tt]])
                c.op("dve", lambda e: e.tensor_scalar(out=vsb[:, tt, vb * 512:(vb + 1) * 512], in0=ps[pi][:, :], scalar1=1.0, scalar2=0.0,
                                                      op0=ALU.mult, op1=ALU.add, accum_out=ssum[:, tt, vb:vb + 1]),
                     reads=[Dps[pi]], writes=[Dv[tt], Dss])
                c.op("act", lambda e: e.activation(out=sqj[:], in_=ps[pi][:, :], func=AF.Square, accum_out=ssq[:, tt, vb:vb + 1]),
                     reads=[Dps[pi]], writes=[Dsqj, Dsq])

            pA(0)
            pB(0)
            pA(1)
            for tt in range(8):
                if tt + 1 < 8:
                    pB(tt + 1)
                if tt + 2 < 8:
                    pA(tt + 2)
                v_group(0, tt, w0, Dw0)
            c.barrier()
            st2.close()
            ws = WS(st, "fw", n=4)
            for vb in range(1, 8):
                wt, Dw = ws.load(w_in_odd, [(4096 + vb * 512, 512, 0)])
                for tt in range(8):
                    v_group(vb, tt, wt, Dw)
            for tt in range(8):
                s_ = stat[:, tt, :]
                c.op("dve", lambda e: e.reduce_sum(out=s_[:, 0:1], in_=ssum[:, tt, :], axis=AX.X), reads=[Dss], writes=[Dst])
                c.op("dve", lambda e: e.reduce_sum(out=s_[:, 1:2], in_=ssq[:, tt, :], axis=AX.X), reads=[Dsq], writes=[Dst])
                c.op("dve", lambda e: e.tensor_scalar(out=s_[:, 2:3], in0=s_[:, 0:1], scalar1=1.0 / 4096, scalar2=None, op0=ALU.mult),
                     reads=[Dst], writes=[Dst])
                c.op("dve", lambda e: e.tensor_tensor(out=s_[:, 3:4], in0=s_[:, 2:3], in1=s_[:, 2:3], op=ALU.mult), reads=[Dst], writes=[Dst])
                c.op("dve", lambda e: e.scalar_tensor_tensor(out=s_[:, 4:5], in0=s_[:, 1:2], scalar=1.0 / 4096, in1=s_[:, 3:4],
                                                             op0=ALU.mult, op1=ALU.subtract), reads=[Dst], writes=[Dst])
                c.op("act", lambda e: e.activation(out=s_[:, 5:6], in_=s_[:, 4:5], func=AF.Sqrt, scale=1.0, bias=epsc[:]),
                     reads=[Dst, Dconst], writes=[Dst])
                c.op("dve", lambda e: e.reciprocal(out=s_[:, 6:7], in_=s_[:, 5:6]), reads=[Dst], writes=[Dst])
            for tt in range(8):
                s_ = stat[:, tt, :]
                eng = "dve"
                c.op(eng, lambda e: e.tensor_scalar(out=vsb[:, tt, :], in0=vsb[:, tt, :], scalar1=s_[:, 2:3], scalar2=s_[:, 6:7],
                                                    op0=ALU.subtract, op1=ALU.mult), reads=[Dv[tt], Dst], writes=[Dv[tt]])
            B2 = sbt(st, "B2", [128, 128], F32)
            szS = sbt(st, "szS", [128, 512], F32)
            m1 = sbt(st, "m1", [128, 512], F32)
            y2 = [sbt(st, "y2S%d" % i, [128, TO], BF16) for i in range(2)]
            DB2, Dsz, Dm1 = Dep(), Dep(), Dep()
            Dy2 = [Dep(), Dep()]
            for cb4 in range(8):
                wt, Dw = ws.load(w_in_odd, [(cb4 * 512, 512, 0)])
                wt2, Dw2 = ws.load(w_in_odd, [(8192 + cb4 * 512, 512, 0)])
                for j in range(4):
                    ct = cb4 * 4 + j
                    g = ct // 2
                    b = ct % 2
                    c.op("dve", lambda e: e.scalar_tensor_tensor(out=B2[:], in0=rsbc[:, g, :], scalar=lb[:, ct:ct + 1], in1=bsbc[:, g, :],
                                                                 op0=ALU.mult, op1=ALU.add), reads=[Drs, Dbs, Dlg], writes=[DB2])
                    for th in range(2):
                        tsl = slice(th * 512, (th + 1) * 512)
                        pu = next_ps()
                        mm_group(pu, 0, 512, lambda kc: wt[:, kc, j * 128:(j + 1) * 128], lambda kc: hT[:, kc, tsl], [Dw] + DhT[4 * th:4 * th + 4])
                        pz = next_ps()
                        mm_group(pz, 0, 512, lambda kc: wt2[:, kc, j * 128:(j + 1) * 128], lambda kc: hT[:, kc, tsl], [Dw2] + DhT[4 * th:4 * th + 4])
                        c.op("act", lambda e: e.activation(out=szS[:], in_=ps[pz][:, :], func=AF.Silu), reads=[Dps[pz]], writes=[Dsz])
                        pm = next_ps()
                        for k4 in range(4):
                            tt = th * 4 + k4
                            c.op("pe", lambda e: e.matmul(ps[pm][:, k4 * 128:(k4 + 1) * 128], lhsT=vsb[:, tt, ct * 128:(ct + 1) * 128],
                                                          rhs=wsT[:, g, :], start=True, stop=True), reads=[Dv[tt], DwsT], writes=[Dps[pm]])
                        c.op("dve", lambda e: e.scalar_tensor_tensor(out=m1[:, :].rearrange("p (a b) -> p a b", b=128),
                                                                     in0=ps[pm][:, :].rearrange("p (a b) -> p a b", b=128),
                                                                     scalar=lg[:, ct:ct + 1],
                                                                     in1=B2[:, :].unsqueeze(1).to_broadcast([128, 4, 128]),
                                                                     op0=ALU.mult, op1=ALU.add), reads=[Dps[pm], DB2, Dlg], writes=[Dm1])
                        c.op("dve", lambda e: e.tensor_tensor(out=m1[:], in0=m1[:], in1=ps[pu][:, :], op=ALU.mult),
                             reads=[Dm1, Dps[pu]], writes=[Dm1])
                        c.op("dve", lambda e: e.tensor_tensor(out=y2[b][:, tsl], in0=m1[:], in1=szS[:], op=ALU.mult),
                             reads=[Dm1, Dsz], writes=[Dy2[b]])
                    c.dma("sp", s_y2[ct], y2[b][:], reads=[Dy2[b]], writes=[Dep()])
            c.barrier()

    if stop >= 7:
        out_proj(s_y2, w_out_odd, s_x1, out, norm_final, "g")

    c.barrier()
    top.close()
    c.close()
    return nc


def _bf(a):
    return np.asarray(a, dtype=np.float32).astype(ml_dtypes.bfloat16)


def own_tiles(hh):
    return [2 * i + (hh ^ (i & 1)) for i in range(8)]


def host_tables(hh):
    tiles = own_tiles(hh)
    own_pos = np.concatenate([np.arange(128 * t, 128 * t + 128) for t in tiles])
    half = 16
    inv_freq = np.power(np.float32(500000.0), -np.arange(half, dtype=np.float32) * np.float32(2.0) / np.float32(32)).astype(np.float32)

    def cs(pos):
        ang = pos.astype(np.float32)[:, None] * inv_freq[None, :]
        co = np.cos(ang).astype(np.float32).T
        si = np.sin(ang).astype(np.float32).T
        n = co.shape[1]
        return (np.ascontiguousarray(np.concatenate([co, co, np.ones((96, n), np.float32)], 0)),
                np.ascontiguousarray(np.concatenate([si, si, np.zeros((96, n), np.float32)], 0)))

    ca, sa = cs(np.arange(T))
    co, so = cs(own_pos)
    R = np.zeros((128, 128), np.float32)
    for m in range(16):
        R[m + 16, m] = -1.0
        R[m, m + 16] = 1.0
    tb = {"t_cos_all": ca, "t_sin_all": sa, "t_cos_own": co, "t_sin_own": so, "t_R": _bf(R),
          "t_ident": _bf(np.eye(128)), "t_identf": np.eye(128, dtype=np.float32), "t_ones": _bf(np.ones((128, 128)))}
    j = np.arange(32)
    am = np.zeros((1024, 32), np.float32)
    tpos = own_pos[:, None]
    forced = (j[None, :] == 0) | (j[None, :] == tpos // 64)
    causal = (64 * j[None, :]) <= tpos
    am[~causal] = -BIG
    am[forced] = BIG
    tb["t_addmask"] = np.ascontiguousarray(am.reshape(8, 128, 32).transpose(1, 0, 2).reshape(128, 256))
    cc = np.arange(128)[:, None]
    cm = ((cc < 127) & (16 * cc + 31 <= own_pos[None, :])).astype(np.float32)
    ncm = (cm.reshape(128, 8, 1, 128) - 1.0) * 30000.0
    tb["t_ncmask"] = _bf(np.broadcast_to(ncm, (128, 8, 4, 128)).reshape(128, 8 * 512))
    tb["t_overlap"] = _bf(((cc < 127) & (16 * cc < 64 * j[None, :] + 64) & (16 * cc + 32 > 64 * j[None, :])).astype(np.float32))
    tb["t_expand"] = _bf((np.arange(T)[None, :] // 64 == np.arange(128)[:, None]).astype(np.float32))
    tk = np.arange(128)[:, None]
    tq = np.arange(128)[None, :]
    dm = np.zeros((128, 4, 128), np.float32)
    for p in range(2):
        for o in range(2):
            r = o - (hh ^ p)
            dm[:, p * 2 + o, :] = (128 * r + tk <= tq)
    tb["t_ndmask"] = _bf(np.broadcast_to((dm.reshape(128, 4, 1, 128) - 1.0) * 30000.0, (128, 4, 4, 128)).reshape(128, 4 * 512))
    wm = np.zeros((128, 12, 128), np.float32)
    for p in range(2):
        for o in range(6):
            r = o - 4 - (hh ^ p)
            diff = tq - tk - 128 * r
            wm[:, p * 6 + o, :] = (diff >= 0) & (diff < 512)
    tb["t_nwmask"] = _bf(np.broadcast_to((wm.reshape(128, 12, 1, 128) - 1.0) * 30000.0, (128, 12, 4, 128)).reshape(128, 12 * 512))
    s48 = np.zeros((128, 48, 128), np.float32)
    for r in range(48):
        s48[r, r, :] = 1.0
    tb["t_sel48"] = _bf(s48.reshape(128, 48 * 128))
    tb["t_tril"] = np.tril(np.ones((128, 128), np.float32))
    return tb


def make_in_maps(inp):
    f = lambda a: np.ascontiguousarray(np.asarray(a, dtype=np.float32))
    x = f(inp["x"])
    shared = {
        "norm_even": f(inp["norm_even"]).reshape(1, D),
        "w_in_even": f(inp["w_in_even"]).reshape(D, L0),
        "conv_w": f(inp["conv_w"]).reshape(3, 16, 128).reshape(48, 128),
        "cmp_k_pos": f(inp["cmp_k_pos"]).reshape(32, 128), "cmp_k_w1": f(inp["cmp_k_w1"]).reshape(4096, 128),
        "cmp_k_b1": f(inp["cmp_k_b1"]).reshape(128, 1), "cmp_k_w2": f(inp["cmp_k_w2"]).reshape(128, 128),
        "cmp_v_pos": f(inp["cmp_v_pos"]).reshape(32, 128), "cmp_v_w1": f(inp["cmp_v_w1"]).reshape(4096, 128),
        "cmp_v_b1": f(inp["cmp_v_b1"]).reshape(128, 1), "cmp_v_w2": f(inp["cmp_v_w2"]).reshape(128, 128),
        "w_out_even": f(inp["w_out_even"]).reshape(4096, D),
        "norm_odd": f(inp["norm_odd"]).reshape(1, D),
        "w_in_odd": f(inp["w_in_odd"]).reshape(D, 12288),
        "sgu_ln_g": f(inp["sgu_ln_g"]).reshape(32, 128), "sgu_ln_b": f(inp["sgu_ln_b"]).reshape(32, 128),
        "sgu_w_s": f(inp["sgu_w_s"]).reshape(16, 128, 128), "sgu_b_s": f(inp["sgu_b_s"]).reshape(1, 2048),
        "w_out_odd": f(inp["w_out_odd"]).reshape(4096, D),
        "norm_final": f(inp["norm_final"]).reshape(1, D),
    }
    tabs = [host_tables(0), host_tables(1)]
    maps = []
    for cidx in range(8):
        b, hh = cidx // 2, cidx % 2
        tiles = own_tiles(hh)
        rows = np.concatenate([np.arange(128 * t, 128 * t + 128) for t in tiles])
        xh = np.zeros((128, D), np.float32)
        for i, t in enumerate(tiles):
            if t > 0:
                xh[2 * i:2 * i + 2] = x[b, 128 * t - 2:128 * t]
        m = dict(shared)
        m.update(tabs[hh])
        m["x_all"] = x[b]
        m["x_own"] = np.ascontiguousarray(x[b][rows])
        m["x_halo"] = xh
        maps.append(m)
    return maps


_NC = {}


def kernel(**inputs):
    if "nc" not in _NC:
        _NC["nc"] = build()
    maps = make_in_maps(inputs)
    res = run_bass_kernel_spmd(_NC["nc"], maps, core_ids=list(range(8)))
    outp = np.zeros((4, T, D), np.float32)
    for cidx in range(8):
        b, hh = cidx // 2, cidx % 2
        o = np.asarray(res.results[cidx]["out"], dtype=np.float32)
        for i, t in enumerate(own_tiles(hh)):
            outp[b, 128 * t:128 * t + 128] = o[128 * i:128 * i + 128]
    return outp
```

```python
from contextlib import ExitStack
import numpy as np
import ml_dtypes
import concourse.bass as bass
import concourse.mybir as mybir
from concourse.bass_utils import run_bass_kernel_spmd

F32 = mybir.dt.float32
BF16 = mybir.dt.bfloat16
ALU = mybir.AluOpType
AF = mybir.ActivationFunctionType
AX = mybir.AxisListType

D = 2048
T = 2048
TO = 1024
L0 = 15408
OFF = dict(cb=0, cc=2048, ch=4096, cg=6144, q=8192, kc=10240, vc=10752, ks=11264, vs=11776, kw=12288,
           vw=12800, gl=13312, ng=13360)
SCALE = 128 ** -0.5
EPS = 1e-6
BIG = 1e30


class Dep:
    __slots__ = ("w", "r", "pr", "excl")

    def __init__(self, excl=False):
        self.w = {}
        self.r = {}
        self.pr = {}
        self.excl = excl


class Ctx:
    DMA_POOL = 12

    def __init__(self, nc, same_engine_sync=True):
        self.nc = nc
        self.same = same_engine_sync
        self.eng = {"pe": nc.tensor, "act": nc.scalar, "dve": nc.vector, "pool": nc.gpsimd, "sp": nc.sync}
        self.sem = {}
        self.cnt = {}
        self.seen = {k: {} for k in self.eng}
        self._stack = []
        for k in self.eng:
            cm = nc.semaphore("s_" + k)
            self.sem[k] = cm.__enter__()
            self._stack.append(cm)
            self.cnt[k] = 0
        self.dpool = {}
        self.dpos = {}
        for q in ("sp", "pool", "act"):
            lst = []
            for i in range(self.DMA_POOL):
                cm = nc.semaphore("d_%s%d" % (q, i))
                lst.append([cm.__enter__(), 0])
                self._stack.append(cm)
            self.dpool[q] = lst
            self.dpos[q] = 0

    def close(self):
        for cm in reversed(self._stack):
            cm.__exit__(None, None, None)

    def _wait(self, e, tickets):
        E = self.eng[e]
        seen = self.seen[e]
        own = id(self.sem[e])
        for sid, (sem, val) in tickets.items():
            if seen.get(sid, 0) >= val:
                continue
            if sid == own and (e == "pe" or not self.same):
                continue
            E.wait_ge(sem, val)
            seen[sid] = val

    @staticmethod
    def _merge(dst, src):
        for sid, tv in src.items():
            if sid not in dst or dst[sid][1] < tv[1]:
                dst[sid] = tv

    def _collect(self, reads, writes, own=None):
        t = {}
        for d in reads:
            self._merge(t, d.w)
            if d.excl:
                self._merge(t, {k: v for k, v in d.r.items() if k != own})
        for d in writes:
            if d.r:
                self._merge(t, d.r)
            elif d.pr:
                self._merge(t, d.pr)
        return t

    def _record(self, reads, writes, ticket):
        tk = {id(ticket[0]): ticket}
        for d in writes:
            if d.r:
                d.w = dict(tk)
                d.pr = d.r
                d.r = {}
            else:
                self._merge(d.w, tk)
        for d in reads:
            self._merge(d.r, tk)

    def op(self, e, fn, reads=(), writes=()):
        self._wait(e, self._collect(reads, writes, id(self.sem[e])))
        inst = fn(self.eng[e])
        self.cnt[e] += 1
        inst.then_inc(self.sem[e], 1)
        self._record(reads, writes, (self.sem[e], self.cnt[e]))
        return inst

    def dma(self, q, out, in_, reads=(), writes=(), **kw):
        self._wait(q, self._collect(reads, writes))
        slot = self.dpool[q][self.dpos[q] % self.DMA_POOL]
        self.dpos[q] += 1
        E = self.eng[q]
        if slot[1] > 0 and self.seen[q].get(id(slot[0]), 0) < slot[1]:
            E.wait_ge(slot[0], slot[1])
            self.seen[q][id(slot[0])] = slot[1]
        inst = E.dma_start(out=out, in_=in_, **kw)
        slot[1] += 16
        inst.then_inc(slot[0], 16)
        self._record(reads, writes, (slot[0], slot[1]))
        return inst

    def barrier(self):
        t = {}
        for k in self.eng:
            if self.cnt[k]:
                t[id(self.sem[k])] = (self.sem[k], self.cnt[k])
        for q in self.dpool:
            for slot in self.dpool[q]:
                if slot[1]:
                    t[id(slot[0])] = (slot[0], slot[1])
        for e in self.eng:
            E = self.eng[e]
            seen = self.seen[e]
            for sid, (sem, val) in t.items():
                if sid == id(self.sem[e]) or seen.get(sid, 0) >= val:
                    continue
                E.wait_ge(sem, val)
                seen[sid] = val


def build(debug=False, stop=99):
    nc = bass.Bass("TRN2", target_bir_lowering=False)
    c = Ctx(nc)

    def din(name, shape, dt=F32):
        return nc.dram_tensor(name, list(shape), dt, kind="ExternalInput").ap()

    def dscr(name, shape, dt):
        return nc.dram_tensor(name, list(shape), dt, kind=("ExternalOutput" if debug else "Internal")).ap()

    x_all = din("x_all", [T, D])
    x_own = din("x_own", [TO, D])
    x_halo = din("x_halo", [128, D])
    norm_even = din("norm_even", [1, D])
    w_in_even = din("w_in_even", [D, L0])
    conv_w = din("conv_w", [48, 128])
    kpos = din("cmp_k_pos", [32, 128])
    kw1 = din("cmp_k_w1", [4096, 128])
    kb1 = din("cmp_k_b1", [128, 1])
    kw2 = din("cmp_k_w2", [128, 128])
    vpos = din("cmp_v_pos", [32, 128])
    vw1 = din("cmp_v_w1", [4096, 128])
    vb1 = din("cmp_v_b1", [128, 1])
    vw2 = din("cmp_v_w2", [128, 128])
    w_out_even = din("w_out_even", [4096, D])
    norm_odd = din("norm_odd", [1, D])
    w_in_odd = din("w_in_odd", [D, 12288])
    ln_g = din("sgu_ln_g", [32, 128])
    ln_b = din("sgu_ln_b", [32, 128])
    w_s = din("sgu_w_s", [16, 128, 128])
    b_s = din("sgu_b_s", [1, 2048])
    w_out_odd = din("w_out_odd", [4096, D])
    norm_final = din("norm_final", [1, D])
    t_cos_all = din("t_cos_all", [128, T])
    t_sin_all = din("t_sin_all", [128, T])
    t_cos_own = din("t_cos_own", [128, TO])
    t_sin_own = din("t_sin_own", [128, TO])
    t_R = din("t_R", [128, 128], BF16)
    t_ident = din("t_ident", [128, 128], BF16)
    t_identf = din("t_identf", [128, 128], F32)
    t_ones = din("t_ones", [128, 128], BF16)
    t_addmask = din("t_addmask", [128, 8 * 32])
    t_ncmask = din("t_ncmask", [128, 8 * 512], BF16)
    t_overlap = din("t_overlap", [128, 32], BF16)
    t_expand = din("t_expand", [128, T], BF16)
    t_ndmask = din("t_ndmask", [128, 4 * 512], BF16)
    t_nwmask = din("t_nwmask", [128, 12 * 512], BF16)
    t_sel48 = din("t_sel48", [128, 48 * 128], BF16)
    t_tril = din("t_tril", [128, 128])

    out = nc.dram_tensor("out", [TO, D], F32, kind="ExternalOutput").ap()

    s_kc = dscr("s_kc", [4, 128, T], BF16)
    s_vc = dscr("s_vc", [4, 128, T], BF16)
    s_ks = dscr("s_ks", [4, 128, T], BF16)
    s_kw = dscr("s_kw", [4, 128, T], BF16)
    s_vs = dscr("s_vs", [128, 16, 512], BF16)
    s_vw = dscr("s_vw", [128, 16, 512], BF16)
    s_q = dscr("s_q", [16, 128, TO], BF16)
    s_qr = dscr("s_qr", [16, 128, TO], BF16)
    s_ng = dscr("s_ng", [16, 128, TO], BF16)
    s_gt = dscr("s_gt", [48, TO], BF16)
    s_y = dscr("s_y", [128, 32, TO], BF16)
    s_x1 = dscr("s_x1", [TO, D], F32)
    s_y2 = dscr("s_y2", [32, 128, TO], BF16)
    s_dbg = dscr("s_dbg", [128, 2048], F32)
    s_dbgb = dscr("s_dbgb", [128, 1024], BF16)

    top = ExitStack()

    def sbt(st, name, shape, dt):
        return st.enter_context(nc.sbuf_tensor(name, list(shape), dt))

    ps = [top.enter_context(nc.psum_tensor("ps%d" % i, [128, 512], F32)) for i in range(7)]
    Dps = [Dep(excl=True) for _ in range(7)]
    psb = top.enter_context(nc.psum_tensor("psb", [128, 1024], BF16))
    Dpsb = Dep(excl=True)

    ident = sbt(top, "ident", [128, 128], BF16)
    identf = sbt(top, "identf", [128, 128], F32)
    ones = sbt(top, "ones", [128, 128], BF16)
    epsc = sbt(top, "epsc", [128, 1], F32)
    Dconst = Dep()
    c.dma("sp", ident[:], t_ident, writes=[Dconst])
    c.dma("sp", identf[:], t_identf, writes=[Dconst])
    c.dma("sp", ones[:], t_ones, writes=[Dconst])
    c.op("dve", lambda e: e.memset(epsc[:], EPS), writes=[Dconst])

    act_flip = [0]

    def evac(out_ap, in_ap, reads, writes):
        act_flip[0] ^= 1
        if act_flip[0]:
            c.op("act", lambda e: e.copy(out=out_ap, in_=in_ap), reads, writes)
        else:
            c.op("dve", lambda e: e.tensor_copy(out=out_ap, in_=in_ap), reads, writes)

    def norm_parts(st, x_src, gain_src, hT, DhT, prefix):
        gbc = sbt(st, prefix + "gbc", [128, D], F32)
        xt = [sbt(st, prefix + "xt%d" % i, [128, D], F32) for i in range(2)]
        hb = [sbt(st, prefix + "hb%d" % i, [128, D], BF16) for i in range(2)]
        stat = sbt(st, prefix + "stat", [128, 4], F32)
        Dg, Dst = Dep(), Dep()
        Dxt = [Dep(), Dep()]
        Dhb = [Dep(), Dep()]
        c.dma("sp", gbc[:], gain_src.to_broadcast([128, D]), writes=[Dg])
        nr = 128
        fifo = []
        cnt = [0]

        def partA(t, src=None):
            b = cnt[0] % 2
            cnt[0] += 1
            fifo.append(b)
            src_ap = src if src is not None else x_src[t * nr:(t + 1) * nr, :]
            c.dma("sp", xt[b][0:nr, :], src_ap, writes=[Dxt[b]])
            c.op("act", lambda e: e.activation(out=hb[b][0:nr, :], in_=xt[b][0:nr, :], func=AF.Square,
                                               accum_out=stat[0:nr, 0:1]), reads=[Dxt[b]], writes=[Dhb[b], Dst])
            c.op("act", lambda e: e.activation(out=stat[0:nr, 1:2], in_=stat[0:nr, 0:1], func=AF.Sqrt,
                                               scale=1.0 / D, bias=epsc[0:nr, :]), reads=[Dst, Dconst], writes=[Dst])
            c.op("dve", lambda e: e.reciprocal(out=stat[0:nr, 2:3], in_=stat[0:nr, 1:2]), reads=[Dst], writes=[Dst])
            c.op("dve", lambda e: e.scalar_tensor_tensor(out=hb[b][0:nr, :], in0=xt[b][0:nr, :], scalar=stat[0:nr, 2:3],
                                                         in1=gbc[0:nr, :], op0=ALU.mult, op1=ALU.mult),
                 reads=[Dxt[b], Dst, Dg, Dhb[b]], writes=[Dhb[b]])

        def partB(t, dst=None):
            b = fifo.pop(0)
            if dst is not None:
                hT_d, col0, Dt = dst
            else:
                hT_d, col0, Dt = hT, t * nr, (DhT[t] if isinstance(DhT, list) else DhT)
            for a in range(4):
                tgt, Dtgt = (psb[:], Dpsb) if a % 2 == 0 else (ps[6][:].bitcast(BF16), Dps[6])
                for j in range(4):
                    kc = 4 * a + j
                    c.op("pe", lambda e: e.transpose(out=tgt[:, j * nr:(j + 1) * nr], in_=hb[b][0:nr, kc * 128:(kc + 1) * 128],
                                                     identity=ident[0:nr, 0:nr]), reads=[Dhb[b], Dconst], writes=[Dtgt])
                evac(hT_d[:, 4 * a:4 * a + 4, col0:col0 + nr],
                     tgt[:, 0:4 * nr].rearrange("p (a b) -> p a b", b=nr), [Dtgt], [Dt])
        return partA, partB

    def norm_transpose(st, x_src, ntiles, gain_src, hT, DhT, prefix):
        pa, pb = norm_parts(st, x_src, gain_src, hT, DhT, prefix)
        for t in range(ntiles):
            pa(t)
            pb(t)

    class WS:
        def __init__(self, st, name, n=3, cols=512):
            self.t = [sbt(st, "%s%d" % (name, i), [128, 16, cols], BF16) for i in range(n)]
            self.d = [Dep() for _ in range(n)]
            self.i = 0

        def load(self, w_ap, pieces):
            k = self.i % len(self.t)
            self.i += 1
            wv = w_ap.rearrange("(kc p) n -> p kc n", p=128)
            for (co, ncol, dc) in pieces:
                for h in range(4):
                    c.dma("pool", self.t[k][:, 4 * h:4 * h + 4, dc:dc + ncol], wv[:, 4 * h:4 * h + 4, co:co + ncol],
                          writes=[self.d[k]])
            return self.t[k], self.d[k]

    psrot = [0]

    def next_ps(n=5):
        psrot[0] = (psrot[0] + 1) % n
        return psrot[0]

    def mm_group(pi, col0, ncols, lhs_fn, rhs_fn, reads, M=128, nk=16):
        for kc in range(nk):
            c.op("pe", lambda e: e.matmul(ps[pi][0:M, col0:col0 + ncols], lhsT=lhs_fn(kc), rhs=rhs_fn(kc),
                                          start=(kc == 0), stop=(kc == nk - 1)), reads=reads, writes=[Dps[pi]])

    def rope_epi(pi, ncols, plainS, Dplain, rotS, Drot, scol, cosT, sinT, tcol, Dtab, Rm, tmp, Dtmp, tmpb, Dtmpb):
        P = ps[pi]
        if plainS is not None:
            src, Dsrc, so = plainS, Dplain, scol
        else:
            src, Dsrc, so = tmpb, Dtmpb, 0
        c.op("act", lambda e: e.copy(out=src[:, so:so + ncols], in_=P[:, 0:ncols]), reads=[Dps[pi]], writes=[Dsrc])
        c.op("pe", lambda e: e.matmul(ps[5][:, 0:ncols], lhsT=Rm[:, :], rhs=src[:, so:so + ncols],
                                      start=True, stop=True), reads=[Dsrc, Dtab], writes=[Dps[5]])
        import os
        RL = int(os.environ.get("RL", "9"))
        if RL < 3:
            return
        c.op("dve", lambda e: e.tensor_tensor(out=tmp[:, 0:ncols], in0=P[:, 0:ncols], in1=cosT[:, tcol:tcol + ncols],
                                              op=ALU.mult), reads=[Dps[pi], Dtab], writes=[Dtmp])
        if RL < 4:
            return
        c.op("dve", lambda e: e.tensor_tensor(out=tmp[:, 512:512 + ncols], in0=ps[5][:, 0:ncols],
                                              in1=sinT[:, tcol:tcol + ncols], op=ALU.mult),
             reads=[Dps[5], Dtab, Dtmp], writes=[Dtmp])
        if RL < 5:
            return
        c.op("dve", lambda e: e.tensor_tensor(out=rotS[:, scol:scol + ncols], in0=tmp[:, 0:ncols],
                                              in1=tmp[:, 512:512 + ncols], op=ALU.add), reads=[Dtmp], writes=[Drot])

    if stop >= 1:
        with ExitStack() as st:
            hT = sbt(st, "hTall", [128, 16, T], BF16)
            DhT = [Dep() for _ in range(16)]
            pA, pB = norm_parts(st, x_all, norm_even, hT, DhT, "a1")
            ws = WS(st, "a1w")
            cosT = sbt(st, "a1cos", [128, T], F32)
            sinT = sbt(st, "a1sin", [128, T], F32)
            Rm = sbt(st, "a1R", [128, 128], BF16)
            tmp = sbt(st, "a1tmp", [128, 1024], F32)
            tmpb = sbt(st, "a1tmpb", [128, 512], BF16)
            Dtmpb = Dep()
            Dtab, Dtmp = Dep(), Dep()
            stg = [sbt(st, "a1stg%d" % i, [128, T], BF16) for i in range(4)]
            Dstg = [Dep() for _ in range(4)]
            vst = [sbt(st, "a1vst%d" % i, [128, 512], BF16) for i in range(4)]
            Dvst = [Dep() for _ in range(4)]
            for t in range(4):
                pA(t)
                pB(t)
            pA(4)
            c.dma("sp", cosT[:], t_cos_all, writes=[Dtab])
            c.dma("sp", sinT[:], t_sin_all, writes=[Dtab])
            c.dma("sp", Rm[:], t_R, writes=[Dtab])
            nxt = [4]

            def inject():
                n = nxt[0]
                if n <= 15:
                    pB(n)
                    if n + 1 <= 15:
                        pA(n + 1)
                    nxt[0] = n + 1

            wt, Dw = ws.load(w_in_even, [(OFF["kc"], 512, 0)])
            for tq in range(4):
                for g in range(4):
                    inject()
                    pi = next_ps()
                    mm_group(pi, 0, 512, lambda kc: wt[:, kc, g * 128:(g + 1) * 128],
                             lambda kc: hT[:, kc, tq * 512:(tq + 1) * 512], [Dw] + DhT[4 * tq:4 * tq + 4])
                    evac(stg[g][:, tq * 512:(tq + 1) * 512], ps[pi][:, :], [Dps[pi]], [Dstg[g]])
            while nxt[0] <= 15:
                inject()
            for g in range(4):
                c.dma("sp", s_kc[g], stg[g][:], reads=[Dstg[g]], writes=[Dep()])
            si = 0
            for name, dst, rope in (("vc", s_vc, False), ("ks", s_ks, True), ("kw", s_kw, True)):
                wt, Dw = ws.load(w_in_even, [(OFF[name], 512, 0)])
                for g in range(4):
                    S, DS = stg[si % 4], Dstg[si % 4]
                    si += 1
                    for tq in range(4):
                        pi = next_ps()
                        mm_group(pi, 0, 512, lambda kc: wt[:, kc, g * 128:(g + 1) * 128],
                                 lambda kc: hT[:, kc, tq * 512:(tq + 1) * 512], [Dw] + DhT[4 * tq:4 * tq + 4])
                        if rope:
                            rope_epi(pi, 512, None, None, S, DS, tq * 512, cosT, sinT, tq * 512, Dtab, Rm, tmp, Dtmp, tmpb, Dtmpb)
                        else:
                            evac(S[:, tq * 512:(tq + 1) * 512], ps[pi][:, :], [Dps[pi]], [DS])
                    c.dma("sp", dst[g], S[:], reads=[DS], writes=[Dep()])
            vi = 0
            for name, dst in (("vs", s_vs), ("vw", s_vw)):
                wt, Dw = ws.load(w_in_even, [(OFF[name], 512, 0)])
                for tt in range(16):
                    pi = next_ps()
                    mm_group(pi, 0, 512, lambda kc: hT[:, kc, tt * 128:(tt + 1) * 128], lambda kc: wt[:, kc, :], [Dw, DhT[tt]])
                    b = vi % 4
                    vi += 1
                    evac(vst[b][:], ps[pi][:, :], [Dps[pi]], [Dvst[b]])
                    c.dma("sp", dst[:, tt, :], vst[b][:], reads=[Dvst[b]], writes=[Dep()])
            c.barrier()

    if stop >= 2:
        with ExitStack() as st:
            hT = sbt(st, "hTown", [128, 16, TO], BF16)
            hTh = sbt(st, "hThalo", [128, 16, 128], BF16)
            DhT, DhTh = [Dep() for _ in range(8)], Dep()
            pA, pB = norm_parts(st, x_own, norm_even, hT, DhT, "a2")
            pA(0, src=x_halo[0:128, :])
            pB(0, dst=(hTh, 0, DhTh))
            for t in range(4):
                pA(t)
                pB(t)
            pA(4)
            nxt = [4]

            def inject():
                n = nxt[0]
                if n <= 7:
                    pB(n)
                    if n + 1 <= 7:
                        pA(n + 1)
                    nxt[0] = n + 1

            ws = WS(st, "a2w", n=3)
            cosT = sbt(st, "a2cos", [128, TO], F32)
            sinT = sbt(st, "a2sin", [128, TO], F32)
            Rm = sbt(st, "a2R", [128, 128], BF16)
            tmp = sbt(st, "a2tmp", [128, 1024], F32)
            tmpb = sbt(st, "a2tmpb", [128, 512], BF16)
            Dtmpb = Dep()
            cwr = sbt(st, "a2cwr", [128, 128], F32)
            cw = sbt(st, "a2cw", [128, 48], F32)
            Dtab, Dtmp, Dcw = Dep(), Dep(), Dep()
            c.dma("sp", cosT[:], t_cos_own, writes=[Dtab])
            c.dma("sp", sinT[:], t_sin_own, writes=[Dtab])
            c.dma("sp", Rm[:], t_R, writes=[Dtab])
            Dz = Dep()
            c.op("dve", lambda e: e.memset(cwr[:], 0.0), writes=[Dcw, Dz])
            c.dma("sp", cwr[0:48, :], conv_w, reads=[Dz], writes=[Dcw])
            c.op("pe", lambda e: e.transpose(out=ps[6][:, 0:128], in_=cwr[:, :], identity=identf[:, :]),
                 reads=[Dcw, Dconst], writes=[Dps[6]])
            c.op("dve", lambda e: e.tensor_copy(out=cw[:], in_=ps[6][:, 0:48]), reads=[Dps[6]], writes=[Dcw])
            NB = 2
            ccS = [sbt(st, "ccS%d" % i, [128, TO], F32) for i in range(NB)]
            cbS = [sbt(st, "cbS%d" % i, [128, TO], F32) for i in range(NB)]
            sgS = [sbt(st, "sgS%d" % i, [128, TO], F32) for i in range(NB)]
            uS = [sbt(st, "uS%d" % i, [128, 8, 130], F32) for i in range(NB)]
            acc = [sbt(st, "acc%d" % i, [128, TO], F32) for i in range(NB)]
            yS = [sbt(st, "yS%d" % i, [128, TO], BF16) for i in range(NB)]
            hcc = sbt(st, "hcc", [128, 16], F32)
            Dcc, Dcb, Dsg, Du, Dacc, DyS = [[Dep() for _ in range(NB)] for _ in range(6)]
            Dhcc = Dep()
            for ct in range(16):
                b = ct % NB
                wt, Dw = ws.load(w_in_even, [(OFF["cc"] + ct * 128, 128, 0), (OFF["ch"] + ct * 128, 128, 128),
                                             (OFF["cb"] + ct * 128, 128, 256), (OFF["cg"] + ct * 128, 128, 384)])
                for f in range(2):
                    mm_group(6, 16 * f, 16, lambda kc: wt[:, kc, f * 128:(f + 1) * 128], lambda kc: hTh[:, kc, 0:16], [Dw, DhTh])
                c.op("act", lambda e: e.copy(out=hcc[:], in_=ps[6][:, 0:16]), reads=[Dps[6]], writes=[Dhcc])
                c.op("dve", lambda e: e.tensor_tensor(out=uS[b][:, :, 0:2], in0=hcc[:].rearrange("p (a b) -> p a b", b=2),
                                                      in1=ps[6][:, 16:32].rearrange("p (a b) -> p a b", b=2), op=ALU.mult),
                     reads=[Dhcc, Dps[6]], writes=[Du[b]])
                for th in range(2):
                    tsl = slice(th * 512, (th + 1) * 512)
                    rhs = lambda kc: hT[:, kc, tsl]
                    pi = next_ps()
                    if ct == 0 and th == 0:
                        inject()
                    mm_group(pi, 0, 512, lambda kc: wt[:, kc, 0:128], rhs, [Dw] + DhT[4 * th:4 * th + 4])
                    c.op("act", lambda e: e.copy(out=ccS[b][:, tsl], in_=ps[pi][:, :]), reads=[Dps[pi]], writes=[Dcc[b]])
                    pi = next_ps()
                    if ct == 0 and th == 0:
                        inject()
                    mm_group(pi, 0, 512, lambda kc: wt[:, kc, 128:256], rhs, [Dw] + DhT[4 * th:4 * th + 4])
                    c.op("dve", lambda e: e.tensor_tensor(out=uS[b][:, 4 * th:4 * th + 4, 2:130],
                                                          in0=ccS[b][:, tsl].rearrange("p (a b) -> p a b", b=128),
                                                          in1=ps[pi][:, :].rearrange("p (a b) -> p a b", b=128), op=ALU.mult),
                         reads=[Dcc[b], Dps[pi]], writes=[Du[b]])
                    pi = next_ps()
                    if ct == 0 and th == 0:
                        inject()
                    mm_group(pi, 0, 512, lambda kc: wt[:, kc, 256:384], rhs, [Dw] + DhT[4 * th:4 * th + 4])
                    c.op("act", lambda e: e.copy(out=cbS[b][:, tsl], in_=ps[pi][:, :]), reads=[Dps[pi]], writes=[Dcb[b]])
                    pi = next_ps()
                    if ct == 0 and th == 0:
                        inject()
                    mm_group(pi, 0, 512, lambda kc: wt[:, kc, 384:512], rhs, [Dw] + DhT[4 * th:4 * th + 4])
                    c.op("act", lambda e: e.activation(out=sgS[b][:, tsl], in_=ps[pi][:, :], func=AF.Silu),
                         reads=[Dps[pi]], writes=[Dsg[b]])
                a3 = acc[b][:, :].rearrange("p (a b) -> p a b", b=128)
                c.op("dve", lambda e: e.tensor_scalar(out=a3, in0=uS[b][:, :, 2:130], scalar1=cw[:, 32 + ct:33 + ct], scalar2=None,
                                                      op0=ALU.mult), reads=[Du[b], Dcw], writes=[Dacc[b]])
                c.op("dve", lambda e: e.scalar_tensor_tensor(out=a3, in0=uS[b][:, :, 1:129], scalar=cw[:, 16 + ct:17 + ct], in1=a3,
                                                             op0=ALU.mult, op1=ALU.add), reads=[Du[b], Dcw, Dacc[b]], writes=[Dacc[b]])
                c.op("dve", lambda e: e.scalar_tensor_tensor(out=a3, in0=uS[b][:, :, 0:128], scalar=cw[:, ct:ct + 1], in1=a3,
                                                             op0=ALU.mult, op1=ALU.add), reads=[Du[b], Dcw, Dacc[b]], writes=[Dacc[b]])
                c.op("dve", lambda e: e.tensor_tensor(out=acc[b][:, :], in0=acc[b][:, :], in1=cbS[b][:, :], op=ALU.mult),
                     reads=[Dacc[b], Dcb[b]], writes=[Dacc[b]])
                c.op("dve", lambda e: e.tensor_tensor(out=yS[b][:, :], in0=acc[b][:, :], in1=sgS[b][:, :], op=ALU.mult),
                     reads=[Dacc[b], Dsg[b]], writes=[DyS[b]])
                c.dma("sp", s_y[:, ct, :], yS[b][:], reads=[DyS[b]], writes=[Dep()])
            qS = [sbt(st, "qS%d" % i, [128, TO], BF16) for i in range(2)]
            qrS = [sbt(st, "qrS%d" % i, [128, TO], BF16) for i in range(2)]
            DqS = [Dep(), Dep()]
            DqrS = [Dep(), Dep()]
            si = 0
            for a in range(4):
                wt, Dw = ws.load(w_in_even, [(OFF["q"] + a * 512, 512, 0)])
                for j in range(4):
                    b = si % 2
                    si += 1
                    for th in range(2):
                        pi = next_ps()
                        mm_group(pi, 0, 512, lambda kc: wt[:, kc, j * 128:(j + 1) * 128],
                                 lambda kc: hT[:, kc, th * 512:(th + 1) * 512], [Dw] + DhT[4 * th:4 * th + 4])
                        rope_epi(pi, 512, qS[b], DqS[b], qrS[b], DqrS[b], th * 512, cosT, sinT, th * 512, Dtab, Rm, tmp, Dtmp, tmpb, Dtmpb)
                    c.dma("sp", s_q[4 * a + j], qS[b][:], reads=[DqS[b]], writes=[Dep()])
                    c.dma("sp", s_qr[4 * a + j], qrS[b][:], reads=[DqrS[b]], writes=[Dep()])
            for a in range(4):
                wt, Dw = ws.load(w_in_even, [(OFF["ng"] + a * 512, 512, 0)])
                for j in range(4):
                    b = si % 2
                    si += 1
                    for th in range(2):
                        pi = next_ps()
                        mm_group(pi, 0, 512, lambda kc: wt[:, kc, j * 128:(j + 1) * 128],
                                 lambda kc: hT[:, kc, th * 512:(th + 1) * 512], [Dw] + DhT[4 * th:4 * th + 4])
                        c.op("act", lambda e: e.activation(out=qS[b][:, th * 512:(th + 1) * 512], in_=ps[pi][:, :], func=AF.Silu),
                             reads=[Dps[pi]], writes=[DqS[b]])
                    c.dma("sp", s_ng[4 * a + j], qS[b][:], reads=[DqS[b]], writes=[Dep()])
            wt, Dw = ws.load(w_in_even, [(OFF["gl"], 48, 0)])
            b = si % 2
            for th in range(2):
                pi = next_ps()
                mm_group(pi, 0, 512, lambda kc: wt[:, kc, 0:128], lambda kc: hT[:, kc, th * 512:(th + 1) * 512], [Dw] + DhT[4 * th:4 * th + 4])
                c.op("act", lambda e: e.activation(out=qS[b][0:48, th * 512:(th + 1) * 512], in_=ps[pi][0:48, :], func=AF.Sigmoid),
                     reads=[Dps[pi]], writes=[DqS[b]])
            c.dma("sp", s_gt, qS[b][0:48, :], reads=[DqS[b]], writes=[Dep()])
            c.barrier()

    if stop >= 3:
        with ExitStack() as st:
            kcomp = sbt(st, "kcomp", [128, 4, 128], BF16)
            vcomp = sbt(st, "vcomp", [128, 4, 128], BF16)
            Dkcomp, Dvcomp = Dep(), Dep()
            c.op("dve", lambda e: e.memset(kcomp[:], 0.0), writes=[Dkcomp])
            c.op("dve", lambda e: e.memset(vcomp[:], 0.0), writes=[Dvcomp])
            with ExitStack() as st2:
                w1 = sbt(st2, "bw1", [128, 32, 128], BF16)
                w2 = sbt(st2, "bw2", [128, 128], BF16)
                posr = sbt(st2, "bposr", [128, 128], BF16)
                posT = sbt(st2, "bposT", [128, 128], BF16)
                b1 = sbt(st2, "bb1", [128, 1], F32)
                bias = sbt(st2, "bbias", [128, 1], F32)
                xin = [sbt(st2, "bxin%d" % i, [128, T], BF16) for i in range(2)]
                hid = sbt(st2, "bhid", [128, 128], BF16)
                Dw1, Dw2, Dpos, Db1, Dbias, Dhid = [Dep() for _ in range(6)]
                Dxin = [Dep(), Dep()]
                xi = 0
                c.op("dve", lambda e: e.memset(hid[:], 0.0), writes=[Dhid])
                for kv, (pos_d, w1_d, b1_d, w2_d, src) in enumerate(((kpos, kw1, kb1, kw2, s_kc), (vpos, vw1, vb1, vw2, s_vc))):
                    w1v = w1_d.rearrange("(l d) j -> d l j", d=128)
                    for h in range(8):
                        c.dma("pool", w1[:, 4 * h:4 * h + 4, :], w1v[:, 4 * h:4 * h + 4, :], writes=[Dw1])
                    c.dma("pool", w2[:], w2_d, writes=[Dw2])
                    Dz = Dep()
                    c.op("dve", lambda e: e.memset(posr[:], 0.0), writes=[Dpos, Dz])
                    c.dma("pool", posr[0:32, :], pos_d, reads=[Dz], writes=[Dpos])
                    c.dma("sp", b1[:], b1_d, writes=[Db1])
                    c.op("pe", lambda e: e.transpose(out=psb[:, 0:128], in_=posr[:, :], identity=ident[:, :]),
                         reads=[Dpos, Dconst], writes=[Dpsb])
                    c.op("dve", lambda e: e.tensor_copy(out=posT[:], in_=psb[:, 0:128]), reads=[Dpsb], writes=[Dpos])
                    for l in range(32):
                        c.op("pe", lambda e: e.matmul(ps[6][:, 0:1], lhsT=w1[:, l, :], rhs=posT[:, l:l + 1], start=(l == 0), stop=(l == 31)),
                             reads=[Dw1, Dpos], writes=[Dps[6]])
                    c.op("dve", lambda e: e.tensor_tensor(out=bias[:], in0=ps[6][:, 0:1], in1=b1[:], op=ALU.add),
                         reads=[Dps[6], Db1], writes=[Dbias])
                    for g in range(4):
                        X, DX = xin[xi % 2], Dxin[xi % 2]
                        xi += 1
                        c.dma("sp", X[:], src[g], writes=[DX])
                        pi = next_ps()
                        for l in range(32):
                            c.op("pe", lambda e: e.matmul(ps[pi][:, 0:127], lhsT=w1[:, l, :], rhs=X[:, l:l + 16 * 126 + 1:16],
                                                          start=(l == 0), stop=(l == 31)), reads=[Dw1, DX], writes=[Dps[pi]])
                        c.op("act", lambda e: e.activation(out=hid[:, 0:127], in_=ps[pi][:, 0:127], func=AF.Silu, bias=bias[:]),
                             reads=[Dps[pi], Dbias], writes=[Dhid])
                        pi = next_ps()
                        if kv == 0:
                            c.op("pe", lambda e: e.matmul(ps[pi][:, 0:127], lhsT=w2[:, :], rhs=hid[:, 0:127], start=True, stop=True),
                                 reads=[Dw2, Dhid], writes=[Dps[pi]])
                            c.op("dve", lambda e: e.tensor_copy(out=kcomp[:, g, 0:127], in_=ps[pi][:, 0:127]),
                                 reads=[Dps[pi]], writes=[Dkcomp])
                        else:
                            c.op("pe", lambda e: e.matmul(ps[pi][:, 0:128], lhsT=hid[:, :], rhs=w2[:, :], start=True, stop=True),
                                 reads=[Dw2, Dhid], writes=[Dps[pi]])
                            c.op("dve", lambda e: e.tensor_copy(out=vcomp[0:127, g, :], in_=ps[pi][0:127, 0:128]),
                                 reads=[Dps[pi]], writes=[Dvcomp])
                c.barrier()
            if debug:
                c.dma("sp", s_dbgb[:, 0:512], kcomp[:].rearrange("p a b -> p (a b)"), reads=[Dkcomp], writes=[Dep()])
                c.dma("sp", s_dbgb[:, 512:1024], vcomp[:].rearrange("p a b -> p (a b)"), reads=[Dvcomp], writes=[Dep()])
            if stop >= 4:
                NEGM = 30000.0
                ps.append(psb[:].bitcast(F32))
                Dps.append(Dpsb)
                gates = sbt(st, "gates", [128, TO], BF16)
                sel48 = sbt(st, "sel48", [128, 48 * 128], BF16)
                ncmask = sbt(st, "ncmask", [128, 8 * 512], BF16)
                overlap = sbt(st, "overlap", [128, 32], BF16)
                expand = sbt(st, "expand", [128, T], BF16)
                ndmask = sbt(st, "ndmask", [128, 4 * 512], BF16)
                nwmask = sbt(st, "nwmask", [128, 12 * 512], BF16)
                addm = sbt(st, "addm", [128, 256], F32)
                Dtb = Dep()
                Dz = Dep()
                c.op("dve", lambda e: e.memset(gates[:], 0.0), writes=[Dz])
                c.dma("sp", gates[0:48, :], s_gt, reads=[Dz], writes=[Dtb])
                for dst_, src_ in ((sel48, t_sel48), (ncmask, t_ncmask), (overlap, t_overlap), (expand, t_expand),
                                   (ndmask, t_ndmask), (nwmask, t_nwmask), (addm, t_addmask)):
                    c.dma("sp", dst_[:], src_, writes=[Dtb])
                NG = 2
                qg = [sbt(st, "qg%d" % i, [128, 4, TO], BF16) for i in range(NG)]
                qrg = [sbt(st, "qrg%d" % i, [128, 4, TO], BF16) for i in range(NG)]
                ngg = [sbt(st, "ngg%d" % i, [128, 4, TO], BF16) for i in range(NG)]
                ksg = [sbt(st, "ksg%d" % i, [128, T], BF16) for i in range(NG)]
                kwg = [sbt(st, "kwg%d" % i, [128, T], BF16) for i in range(NG)]
                vsg = [sbt(st, "vsg%d" % i, [128, 16, 128], BF16) for i in range(NG)]
                vwg = [sbt(st, "vwg%d" % i, [128, 16, 128], BF16) for i in range(NG)]
                yst = [sbt(st, "yst%d" % i, [128, 4, TO], BF16) for i in range(NG)]
                Dgl = [Dep() for _ in range(NG)]
                Dyst = [Dep() for _ in range(NG)]
                NE = 6
                Et = [sbt(st, "Et%d" % i, [128, 512], BF16) for i in range(NE)]
                DEt = [Dep() for _ in range(NE)]
                Ec = [sbt(st, "Ec%d" % i, [128, 512], BF16) for i in range(3)]
                Pn = [sbt(st, "Pn%d" % i, [128, 512], BF16) for i in range(3)]
                oacc = [sbt(st, "oacc%d" % i, [128, 512], F32) for i in range(3)]
                otmp = [sbt(st, "otmp%d" % i, [128, 512], F32) for i in range(2)]
                rden0 = [sbt(st, "rden0%d" % i, [128, 512], F32) for i in range(3)]
                rdenB = [sbt(st, "rdenB%d" % i, [128, 512], F32) for i in range(2)]
                gsb = [[sbt(st, "gsb%d_%d" % (i, j), [128, 512], F32) for j in range(3)] for i in range(3)]
                negsel = [sbt(st, "negsel%d" % i, [128, 512], BF16) for i in range(3)]
                imp2 = sbt(st, "imp2", [128, 32], F32)
                top8 = sbt(st, "top8", [128, 8], F32)
                selF = sbt(st, "selF", [128, 128], F32)
                tden = sbt(st, "tden", [128, 512], F32)
                Dimp2, Dtop8, DselF, Dtden = [Dep() for _ in range(4)]
                DEc = [Dep() for _ in range(3)]
                DPn = [Dep() for _ in range(3)]
                Doacc = [Dep() for _ in range(3)]
                Dotmp = [Dep(), Dep()]
                Drden0 = [Dep() for _ in range(3)]
                DrdenB = [Dep(), Dep()]
                Dgsb = [[Dep() for _ in range(3)] for _ in range(3)]
                Dnegsel = [Dep() for _ in range(3)]
                c.op("dve", lambda e: e.memset(selF[:], 0.0), writes=[DselF])
                rot = [0, 0]

                def sc_next():
                    rot[0] = (rot[0] + 1) % 4
                    return rot[0]

                def E_next():
                    rot[1] = (rot[1] + 1) % NE
                    return rot[1]

                def run_steps(steps, qsrc, ksrc, vsrc, Dg_, pO, pD, stages, depth=3):
                    n = len(steps)
                    slots = []

                    def issue_S(k):
                        kt, extras = steps[k]
                        bk = sc_next()
                        e_ = E_next()
                        c.op("pe", lambda e: e.matmul(ps[bk][:, :], lhsT=ksrc[:, kt * 128:(kt + 1) * 128], rhs=qsrc, start=True,
                                                      stop=(len(extras) == 0)), reads=[Dg_], writes=[Dps[bk]])
                        for j, (l_ap, r_ap, rd) in enumerate(extras):
                            c.op("pe", lambda e: e.matmul(ps[bk][:, :], lhsT=l_ap, rhs=r_ap, start=False, stop=(j == len(extras) - 1)),
                                 reads=rd, writes=[Dps[bk]])
                        c.op("act", lambda e: e.activation(out=Et[e_][:], in_=ps[bk][:, :], func=AF.Exp, scale=SCALE),
                             reads=[Dps[bk]], writes=[DEt[e_]])
                        slots.append(e_)

                    for k in range(min(depth, n)):
                        issue_S(k)
                    for k in range(n):
                        if k + depth < n:
                            issue_S(k + depth)
                        e_ = slots[k]
                        kt = steps[k][0]
                        c.op("pe", lambda e: e.matmul(ps[pO][:, :], lhsT=vsrc[:, kt, :], rhs=Et[e_][:, :], start=(k == 0), stop=(k == n - 1)),
                             reads=[Dg_, DEt[e_]], writes=[Dps[pO]])
                        c.op("pe", lambda e: e.matmul(ps[pD][:, :], lhsT=ones[:, :], rhs=Et[e_][:, :], start=(k == 0), stop=(k == n - 1)),
                             reads=[Dconst, DEt[e_]], writes=[Dps[pD]])
                        if stages:
                            stages.pop(0)()

                def neg_recip(src_ap, Dsrc, rb, Drb, tA, DtA):
                    c.op("dve", lambda e: e.tensor_scalar_max(out=tA[:], in0=src_ap, scalar1=1e-30), reads=[Dsrc], writes=[DtA])
                    c.op("act", lambda e: e.activation(out=rb[:], in_=tA[:], func=AF.Ln), reads=[DtA], writes=[Drb])
                    c.op("act", lambda e: e.activation(out=rb[:], in_=rb[:], func=AF.Exp, scale=-1.0), reads=[Drb], writes=[Drb])
                    c.op("dve", lambda e: e.tensor_tensor(out=tA[:], in0=tA[:], in1=rb[:], op=ALU.mult), reads=[DtA, Drb], writes=[DtA])
                    c.op("dve", lambda e: e.scalar_tensor_tensor(out=rb[:], in0=tA[:], scalar=2.0, in1=rb[:], op0=ALU.subtract, op1=ALU.mult),
                         reads=[DtA, Drb], writes=[Drb])

                def finish(pO, pD, gs, Dgs, ob, oa):
                    rb, Drb = rdenB[ob], DrdenB[ob]
                    neg_recip(ps[pD][:, :], Dps[pD], rb, Drb, otmp[ob], Dotmp[ob])
                    c.op("dve", lambda e: e.scalar_tensor_tensor(out=rb[:], in0=rb[:], scalar=-1.0, in1=gs[:], op0=ALU.mult, op1=ALU.mult),
                         reads=[Drb, Dgs], writes=[Drb])
                    c.op("dve", lambda e: e.tensor_tensor(out=otmp[ob][:], in0=ps[pO][:, :], in1=rb[:], op=ALU.mult),
                         reads=[Dps[pO], Drb], writes=[Dotmp[ob]])
                    c.op("pool", lambda e: e.tensor_tensor(out=oacc[oa][:], in0=oacc[oa][:], in1=otmp[ob][:], op=ALU.add),
                         reads=[Doacc[oa], Dotmp[ob]], writes=[Doacc[oa]])

                def load_group(g):
                    b = g % NG
                    c.dma("sp", qg[b][:], s_q[4 * g:4 * g + 4].rearrange("n p t -> p n t"), writes=[Dgl[b]])
                    c.dma("sp", qrg[b][:], s_qr[4 * g:4 * g + 4].rearrange("n p t -> p n t"), writes=[Dgl[b]])
                    c.dma("sp", ngg[b][:], s_ng[4 * g:4 * g + 4].rearrange("n p t -> p n t"), writes=[Dgl[b]])
                    c.dma("sp", ksg[b][:], s_ks[g], writes=[Dgl[b]])
                    c.dma("sp", kwg[b][:], s_kw[g], writes=[Dgl[b]])
                    c.dma("sp", vsg[b][:], s_vs[:, :, g * 128:(g + 1) * 128], writes=[Dgl[b]])
                    c.dma("sp", vwg[b][:], s_vw[:, :, g * 128:(g + 1) * 128], writes=[Dgl[b]])

                def prep(g, i):
                    b = g % NG
                    pb = (g * 8 + i) % 3
                    tsl = slice(i * 128, (i + 1) * 128)
                    q_t = qg[b][:, :, tsl]
                    for br in range(3):
                        bk = sc_next()
                        for n in range(4):
                            r = br * 16 + g * 4 + n
                            c.op("pe", lambda e: e.matmul(ps[bk][:, n * 128:(n + 1) * 128], lhsT=sel48[:, r * 128:(r + 1) * 128],
                                                          rhs=gates[:, tsl], start=True, stop=True), reads=[Dtb], writes=[Dps[bk]])
                        c.op("dve", lambda e: e.tensor_copy(out=gsb[pb][br][:], in_=ps[bk][:, :]), reads=[Dps[bk]], writes=[Dgsb[pb][br]])
                    bc = sc_next()
                    c.op("pe", lambda e: e.matmul(ps[bc][:, :], lhsT=kcomp[:, g, :], rhs=q_t, start=True, stop=False),
                         reads=[Dkcomp, Dgl[b]], writes=[Dps[bc]])
                    c.op("pe", lambda e: e.matmul(ps[bc][:, :], lhsT=ident[:, :], rhs=ncmask[:, i * 512:(i + 1) * 512], start=False, stop=True),
                         reads=[Dtb, Dconst], writes=[Dps[bc]])
                    c.op("act", lambda e: e.activation(out=Ec[pb][:], in_=ps[bc][:, :], func=AF.Exp, scale=SCALE),
                         reads=[Dps[bc]], writes=[DEc[pb]])

                    def stage_den():
                        bd = sc_next()
                        c.op("pe", lambda e: e.matmul(ps[bd][:, :], lhsT=ones[:, :], rhs=Ec[pb][:, :], start=True, stop=True),
                             reads=[Dconst, DEc[pb]], writes=[Dps[bd]])
                        neg_recip(ps[bd][:, :], Dps[bd], rden0[pb], Drden0[pb], tden, Dtden)
                        c.op("dve", lambda e: e.scalar_tensor_tensor(out=Pn[pb][:], in0=Ec[pb][:], scalar=-1.0, in1=rden0[pb][:],
                                                                     op0=ALU.mult, op1=ALU.mult),
                             reads=[DEc[pb], Drden0[pb]], writes=[DPn[pb]])

                    def stage_imp():
                        bi = sc_next()
                        for n in range(4):
                            c.op("pe", lambda e: e.matmul(ps[bi][:, 0:32], lhsT=Pn[pb][:, n * 128:(n + 1) * 128], rhs=overlap[:, :],
                                                          start=(n == 0), stop=(n == 3)), reads=[DPn[pb], Dtb], writes=[Dps[bi]])
                        bo = sc_next()
                        c.op("pe", lambda e: e.matmul(ps[bo][:, :], lhsT=vcomp[:, g, :], rhs=Pn[pb][:, :], start=True, stop=True),
                             reads=[Dvcomp, DPn[pb]], writes=[Dps[bo]])
                        c.op("dve", lambda e: e.tensor_tensor(out=imp2[:], in0=ps[bi][:, 0:32], in1=addm[:, i * 32:(i + 1) * 32], op=ALU.add),
                             reads=[Dps[bi], Dtb], writes=[Dimp2])
                        c.op("dve", lambda e: e.max(out=top8[:], in_=imp2[:]), reads=[Dimp2], writes=[Dtop8])
                        c.op("dve", lambda e: e.tensor_scalar_max(out=top8[:, 7:8], in0=top8[:, 7:8], scalar1=-1e29), reads=[Dtop8], writes=[Dtop8])
                        c.op("dve", lambda e: e.tensor_scalar(out=selF[:, 0:32], in0=imp2[:], scalar1=top8[:, 7:8], scalar2=None, op0=ALU.is_ge),
                             reads=[Dimp2, Dtop8], writes=[DselF])
                        c.op("dve", lambda e: e.tensor_tensor(out=oacc[pb][:], in0=ps[bo][:, :], in1=gsb[pb][0][:], op=ALU.mult),
                             reads=[Dps[bo], Dgsb[pb][0]], writes=[Doacc[pb]])
                        if debug and g == 0:
                            c.dma("sp", s_dbg[:, 1024 + i * 32:1024 + (i + 1) * 32], imp2[:], reads=[Dimp2], writes=[Dep()])

                    def stage_tr():
                        bt = sc_next()
                        c.op("pe", lambda e: e.transpose(out=ps[bt][:, 0:128], in_=selF[:, :], identity=identf[:, :]),
                             reads=[DselF, Dconst], writes=[Dps[bt]])
                        c.op("dve", lambda e: e.tensor_scalar(out=negsel[pb][:, :].rearrange("p (a b) -> p a b", b=128),
                                                              in0=ps[bt][:, 0:128].unsqueeze(1).to_broadcast([128, 4, 128]),
                                                              scalar1=-1.0, scalar2=NEGM, op0=ALU.add, op1=ALU.mult),
                             reads=[Dps[bt]], writes=[Dnegsel[pb]])

                    return [stage_den, None, None, None, stage_imp, None, None, stage_tr]

                tiles = [(g, i) for g in range(4) for i in range(8)]
                DEPTH = 3
                pending = []
                cur_stages = []

                def make_step(kt, extras, qsrc, ksrc, vsrc, Dg_, pO, pD, first, last, post):
                    def issue_S():
                        bk = sc_next()
                        e_ = E_next()
                        c.op("pe", lambda e: e.matmul(ps[bk][:, :], lhsT=ksrc[:, kt * 128:(kt + 1) * 128], rhs=qsrc, start=True,
                                                      stop=(len(extras) == 0)), reads=[Dg_], writes=[Dps[bk]])
                        for j, (l_ap, r_ap, rd) in enumerate(extras):
                            c.op("pe", lambda e: e.matmul(ps[bk][:, :], lhsT=l_ap, rhs=r_ap, start=False, stop=(j == len(extras) - 1)),
                                 reads=rd, writes=[Dps[bk]])
                        c.op("act", lambda e: e.activation(out=Et[e_][:], in_=ps[bk][:, :], func=AF.Exp, scale=SCALE),
                             reads=[Dps[bk]], writes=[DEt[e_]])
                        return e_

                    def issue_PV(e_):
                        c.op("pe", lambda e: e.matmul(ps[pO][:, :], lhsT=vsrc[:, kt, :], rhs=Et[e_][:, :], start=first, stop=last),
                             reads=[Dg_, DEt[e_]], writes=[Dps[pO]])
                        c.op("pe", lambda e: e.matmul(ps[pD][:, :], lhsT=ones[:, :], rhs=Et[e_][:, :], start=first, stop=last),
                             reads=[Dconst, DEt[e_]], writes=[Dps[pD]])
                        if post is not None:
                            post()
                    return issue_S, issue_PV

                def pop_one():
                    issue_PV, e_ = pending.pop(0)
                    issue_PV(e_)
                    if cur_stages:
                        (cur_stages.pop(0) or (lambda: None))()

                def push(step):
                    issue_S, issue_PV = step
                    pending.append((issue_PV, issue_S()))
                    if len(pending) > DEPTH:
                        pop_one()

                load_group(0)
                cur_stages.extend(prep(0, 0))
                while cur_stages:
                    (cur_stages.pop(0) or (lambda: None))()
                for ti, (g, i) in enumerate(tiles):
                    b = g % NG
                    p = i & 1
                    oa = ti % 3
                    tsl = slice(i * 128, (i + 1) * 128)
                    qr_t = qrg[b][:, :, tsl]
                    while cur_stages:
                        (cur_stages.pop(0) or (lambda: None))()
                    if i == 1 and g + 1 < 4:
                        load_group(g + 1)
                    if ti + 1 < len(tiles):
                        cur_stages.extend(prep(*tiles[ti + 1]))
                    wsteps = []
                    for o in range(6):
                        kt = 2 * i - 4 + o
                        if kt < 0:
                            continue
                        ex = [] if o in (2, 3) else [(ident[:, :], nwmask[:, (p * 6 + o) * 512:(p * 6 + o + 1) * 512], [Dtb, Dconst])]
                        wsteps.append((kt, ex))
                    for k, (kt, ex) in enumerate(wsteps):
                        lastw = (k == len(wsteps) - 1)
                        post = (lambda oa=oa: finish(4, 5, gsb[oa][2], Dgsb[oa][2], 0, oa)) if lastw else None
                        push(make_step(kt, ex, qr_t, kwg[b], vwg[b], Dgl[b], 4, 5, k == 0, lastw, post))
                    nsl = 2 * i + 2
                    for kt in range(nsl):
                        o = kt - 2 * i
                        ex = [(expand[:, kt * 128:(kt + 1) * 128], negsel[oa][:, :], [Dtb, Dnegsel[oa]])]
                        if o >= 0:
                            ex.append((ident[:, :], ndmask[:, (p * 2 + o) * 512:(p * 2 + o + 1) * 512], [Dtb, Dconst]))
                        lasts = (kt == nsl - 1)

                        def post_s(oa=oa, b=b, tsl=tsl, g=g, i=i):
                            finish(6, 7, gsb[oa][1], Dgsb[oa][1], 1, oa)
                            c.op("pool", lambda e: e.tensor_tensor(out=yst[b][:, :, tsl], in0=oacc[oa][:, :].rearrange("p (a b) -> p a b", b=128),
                                                                   in1=ngg[b][:, :, tsl], op=ALU.mult),
                                 reads=[Doacc[oa], Dgl[b]], writes=[Dyst[b]])
                            if i == 7:
                                c.dma("pool", s_y[:, 16 + 4 * g:16 + 4 * g + 4, :], yst[b][:], reads=[Dyst[b]], writes=[Dep()])
                        push(make_step(kt, ex, qr_t, ksg[b], vsg[b], Dgl[b], 6, 7, kt == 0, lasts, post_s if lasts else None))
                while pending:
                    pop_one()
                while cur_stages:
                    (cur_stages.pop(0) or (lambda: None))()
                ps.pop()
                Dps.pop()
            c.barrier()

    def out_proj(s_ysrc, w_out, x_src, dst, final_gain, prefix):
        with ExitStack() as st:
            wo = sbt(st, prefix + "wo", [128, 32, D], BF16)
            Dwo = [Dep() for _ in range(4)]
            wv = w_out.rearrange("(kc p) n -> p kc n", p=128)
            for nb in range(4):
                for h in range(8):
                    c.dma("pool", wo[:, 4 * h:4 * h + 4, nb * 512:(nb + 1) * 512], wv[:, 4 * h:4 * h + 4, nb * 512:(nb + 1) * 512],
                          writes=[Dwo[nb]])
            yT = [sbt(st, prefix + "yT%d" % i, [128, 32, 128], BF16) for i in range(2)]
            xt = [sbt(st, prefix + "xt%d" % i, [128, D], F32) for i in range(2)]
            xo = xt
            DyT = [Dep(), Dep()]
            Dxt = [Dep(), Dep()]
            Dxo = Dxt
            if final_gain is not None:
                gbc = sbt(st, prefix + "gbc", [128, D], F32)
                junk = sbt(st, prefix + "junk", [128, D], BF16)
                stat = sbt(st, prefix + "stat", [128, 4], F32)
                Dg, Dj, Dst = Dep(), Dep(), Dep()
                c.dma("sp", gbc[:], final_gain.to_broadcast([128, D]), writes=[Dg])
            for tt in range(8):
                b = tt % 2
                c.dma("sp", yT[b][:], s_ysrc[:, :, tt * 128:(tt + 1) * 128].rearrange("k p t -> p k t"), writes=[DyT[b]])
                c.dma("sp", xt[b][:], x_src[tt * 128:(tt + 1) * 128, :], writes=[Dxt[b]])
                for nb in range(4):
                    pi = next_ps()
                    mm_group(pi, 0, 512, lambda kc: yT[b][:, kc, :], lambda kc: wo[:, kc, nb * 512:(nb + 1) * 512], [DyT[b], Dwo[nb]], nk=32)
                    c.op("dve", lambda e: e.tensor_tensor(out=xo[b][:, nb * 512:(nb + 1) * 512], in0=ps[pi][:, :],
                                                          in1=xt[b][:, nb * 512:(nb + 1) * 512], op=ALU.add),
                         reads=[Dps[pi], Dxt[b]], writes=[Dxo[b]])
                if final_gain is not None:
                    c.op("act", lambda e: e.activation(out=junk[:], in_=xo[b][:], func=AF.Square, accum_out=stat[:, 0:1]),
                         reads=[Dxo[b]], writes=[Dj, Dst])
                    c.op("act", lambda e: e.activation(out=stat[:, 1:2], in_=stat[:, 0:1], func=AF.Sqrt, scale=1.0 / D, bias=epsc[:]),
                         reads=[Dst, Dconst], writes=[Dst])
                    c.op("dve", lambda e: e.reciprocal(out=stat[:, 2:3], in_=stat[:, 1:2]), reads=[Dst], writes=[Dst])
                    c.op("dve", lambda e: e.scalar_tensor_tensor(out=xo[b][:], in0=xo[b][:], scalar=stat[:, 2:3], in1=gbc[:],
                                                                 op0=ALU.mult, op1=ALU.mult), reads=[Dxo[b], Dst, Dg], writes=[Dxo[b]])
                c.dma("pool", dst[tt * 128:(tt + 1) * 128, :], xo[b][:], reads=[Dxo[b]], writes=[Dep()])
            c.barrier()

    def out_proj_nb(s_ysrc, w_out, x_src, dst, prefix):
        with ExitStack() as st:
            yT = sbt(st, prefix + "yT", [128, 32, TO], BF16)
            DyT = [Dep()] * 8
            for j in range(4):
                c.dma("sp", yT[:, 8 * j:8 * j + 8, :], s_ysrc[:, 8 * j:8 * j + 8, :], writes=[DyT[0]])
            wo = [sbt(st, prefix + "wo%d" % i, [128, 32, 512], BF16) for i in range(3)]
            Dwo = [Dep() for _ in range(3)]
            wv = w_out.rearrange("(kc p) n -> p kc n", p=128)
            xp = [sbt(st, prefix + "xp%d" % i, [128, 512], F32) for i in range(4)]
            Dxp = [Dep() for _ in range(4)]
            xi = 0
            for nb in range(4):
                k = nb % 3
                for h in range(8):
                    c.dma("pool", wo[k][:, 4 * h:4 * h + 4, :], wv[:, 4 * h:4 * h + 4, nb * 512:(nb + 1) * 512], writes=[Dwo[k]])
                for tt in range(8):
                    b = xi % 4
                    xi += 1
                    c.dma("sp", xp[b][:], x_src[tt * 128:(tt + 1) * 128, nb * 512:(nb + 1) * 512], writes=[Dxp[b]])
                    pi = next_ps()
                    mm_group(pi, 0, 512, lambda kc: yT[:, kc, tt * 128:(tt + 1) * 128], lambda kc: wo[k][:, kc, :], [DyT[tt], Dwo[k]], nk=32)
                    c.op("dve", lambda e: e.tensor_tensor(out=xp[b][:], in0=ps[pi][:, :], in1=xp[b][:], op=ALU.add),
                         reads=[Dps[pi], Dxp[b]], writes=[Dxp[b]])
                    c.dma("act", dst[tt * 128:(tt + 1) * 128, nb * 512:(nb + 1) * 512], xp[b][:], reads=[Dxp[b]], writes=[Dep()])
            c.barrier()

    if stop >= 5:
        out_proj_nb(s_y, w_out_even, x_own, s_x1, "d")

    if stop >= 6:
        with ExitStack() as st:
            hT = sbt(st, "hT1", [128, 16, TO], BF16)
            vsb = sbt(st, "vsb", [128, 8, 4096], BF16)
            Dv = [Dep() for _ in range(8)]
            lg = sbt(st, "lg", [128, 32], F32)
            lb = sbt(st, "lb", [128, 32], F32)
            wsT = sbt(st, "wsT", [128, 16, 128], BF16)
            rsbc = sbt(st, "rsbc", [128, 16, 128], F32)
            bsbc = sbt(st, "bsbc", [128, 16, 128], F32)
            ssum = sbt(st, "fssum", [128, 8, 8], F32)
            ssq = sbt(st, "fssq", [128, 8, 8], F32)
            sqj = sbt(st, "fsqj", [128, 512], BF16)
            stat = sbt(st, "fstat", [128, 8, 8], F32)
            st2 = ExitStack()
            lgr = sbt(st2, "lgr", [128, 128], F32)
            lbr = sbt(st2, "lbr", [128, 128], F32)
            wsr = sbt(st2, "wsr", [128, 16, 128], F32)
            tril = sbt(st2, "tril", [128, 128], F32)
            wsm = sbt(st2, "wsm", [128, 16, 128], BF16)
            Dlg, Dws, DwsT, Drs, Dbs = [Dep() for _ in range(5)]
            Dz = Dep()
            c.op("dve", lambda e: e.memset(lgr[:], 0.0), writes=[Dz])
            c.op("dve", lambda e: e.memset(lbr[:], 0.0), writes=[Dz])
            c.dma("sp", lgr[0:32, :], ln_g, reads=[Dz], writes=[Dlg])
            c.dma("sp", lbr[0:32, :], ln_b, reads=[Dz], writes=[Dlg])
            c.dma("sp", wsr[:], w_s.rearrange("g t s -> t g s"), writes=[Dws])
            c.dma("sp", tril[:], t_tril, writes=[Dws])
            c.dma("sp", bsbc[:].rearrange("p a b -> p (a b)"), b_s.to_broadcast([128, 2048]), writes=[Dbs])
            c.op("pe", lambda e: e.transpose(out=ps[6][:, 0:128], in_=lgr[:, :], identity=identf[:, :]), reads=[Dlg, Dconst], writes=[Dps[6]])
            c.op("pe", lambda e: e.transpose(out=ps[6][:, 128:256], in_=lbr[:, :], identity=identf[:, :]), reads=[Dlg, Dconst], writes=[Dps[6]])
            c.op("dve", lambda e: e.tensor_copy(out=lg[:], in_=ps[6][:, 0:32]), reads=[Dps[6]], writes=[Dlg])
            c.op("dve", lambda e: e.tensor_copy(out=lb[:], in_=ps[6][:, 128:160]), reads=[Dps[6]], writes=[Dlg])
            c.op("dve", lambda e: e.tensor_tensor(out=wsm[:], in0=wsr[:], in1=tril[:, :].unsqueeze(1).to_broadcast([128, 16, 128]), op=ALU.mult),
                 reads=[Dws], writes=[Dws])
            for a in range(2):
                for j in range(8):
                    c.op("pe", lambda e: e.transpose(out=psb[:, j * 128:(j + 1) * 128], in_=wsm[:, 8 * a + j, :], identity=ident[:, :]),
                         reads=[Dws, Dconst], writes=[Dpsb])
                c.op("dve", lambda e: e.tensor_copy(out=wsT[:, 8 * a:8 * a + 8, :], in_=psb[:, :].rearrange("p (a b) -> p a b", b=128)),
                     reads=[Dpsb], writes=[DwsT])
            for a in range(4):
                c.op("pe", lambda e: e.matmul(ps[6][:, :], lhsT=ones[:, :], rhs=wsT[:, 4 * a:4 * a + 4, :], start=True, stop=True),
                     reads=[DwsT, Dconst], writes=[Dps[6]])
                c.op("dve", lambda e: e.tensor_copy(out=rsbc[:, 4 * a:4 * a + 4, :], in_=ps[6][:, :].rearrange("p (a b) -> p a b", b=128)),
                     reads=[Dps[6]], writes=[Drs])
            DhT = [Dep() for _ in range(8)]
            pA, pB = norm_parts(st2, s_x1, norm_odd, hT, DhT, "e")
            w0 = sbt(st2, "fvw0", [128, 16, 512], BF16)
            Dw0 = Dep()
            Dss, Dsq, Dsqj, Dst = Dep(), Dep(), Dep(), Dep()
            wv_ = w_in_odd.rearrange("(kc p) n -> p kc n", p=128)
            for h in range(4):
                c.dma("pool", w0[:, 4 * h:4 * h + 4, :], wv_[:, 4 * h:4 * h + 4, 4096:4096 + 512], writes=[Dw0])

            def v_group(vb, tt, wt, Dw):
                pi = next_ps()
                mm_group(pi, 0, 512, lambda kc: hT[:, kc, tt * 128:(tt + 1) * 128], lambda kc: wt[:, kc, :], [Dw, DhT[tt]])
                c.op("dve", lambda e: e.tensor_scalar(out=vsb[:, tt, vb * 512:(vb + 1) * 512], in0=ps[pi][:, :], scalar1=1.0, scalar2=0.0,
                                                      op0=ALU.mult, op1=ALU.add, accum_out=ssum[:, tt, vb:vb + 1]),
                     reads=[Dps[pi]], writes=[Dv[tt], Dss])
                c.op("act", lambda e: e.activation(out=sqj[:], in_=ps[pi][:, :], func=AF.Square, accum_out=ssq[:, tt, vb:vb + 1]),
                     reads=[Dps[pi]], writes=[Dsqj, Dsq])

            pA(0)
            pB(0)
            pA(1)
            for tt in range(8):
                if tt + 1 < 8:
                    pB(tt + 1)
                if tt + 2 < 8:
                    pA(tt + 2)
                v_group(0, tt, w0, Dw0)
            c.barrier()
            st2.close()
            ws = WS(st, "fw", n=4)
            for vb in range(1, 8):
                wt, Dw = ws.load(w_in_odd, [(4096 + vb * 512, 512, 0)])
                for tt in range(8):
                    v_group(vb, tt, wt, Dw)
            for tt in range(8):
                s_ = stat[:, tt, :]
                c.op("dve", lambda e: e.reduce_sum(out=s_[:, 0:1], in_=ssum[:, tt, :], axis=AX.X), reads=[Dss], writes=[Dst])
                c.op("dve", lambda e: e.reduce_sum(out=s_[:, 1:2], in_=ssq[:, tt, :], axis=AX.X), reads=[Dsq], writes=[Dst])
                c.op("dve", lambda e: e.tensor_scalar(out=s_[:, 2:3], in0=s_[:, 0:1], scalar1=1.0 / 4096, scalar2=None, op0=ALU.mult),
                     reads=[Dst], writes=[Dst])
                c.op("dve", lambda e: e.tensor_tensor(out=s_[:, 3:4], in0=s_[:, 2:3], in1=s_[:, 2:3], op=ALU.mult), reads=[Dst], writes=[Dst])
                c.op("dve", lambda e: e.scalar_tensor_tensor(out=s_[:, 4:5], in0=s_[:, 1:2], scalar=1.0 / 4096, in1=s_[:, 3:4],
                                                             op0=ALU.mult, op1=ALU.subtract), reads=[Dst], writes=[Dst])
                c.op("act", lambda e: e.activation(out=s_[:, 5:6], in_=s_[:, 4:5], func=AF.Sqrt, scale=1.0, bias=epsc[:]),
                     reads=[Dst, Dconst], writes=[Dst])
                c.op("dve", lambda e: e.reciprocal(out=s_[:, 6:7], in_=s_[:, 5:6]), reads=[Dst], writes=[Dst])
            for tt in range(8):
                s_ = stat[:, tt, :]
                eng = "dve"
                c.op(eng, lambda e: e.tensor_scalar(out=vsb[:, tt, :], in0=vsb[:, tt, :], scalar1=s_[:, 2:3], scalar2=s_[:, 6:7],
                                                    op0=ALU.subtract, op1=ALU.mult), reads=[Dv[tt], Dst], writes=[Dv[tt]])
            B2 = sbt(st, "B2", [128, 128], F32)
            szS = sbt(st, "szS", [128, 512], F32)
            m1 = sbt(st, "m1", [128, 512], F32)
            y2 = [sbt(st, "y2S%d" % i, [128, TO], BF16) for i in range(2)]
            DB2, Dsz, Dm1 = Dep(), Dep(), Dep()
            Dy2 = [Dep(), Dep()]
            for cb4 in range(8):
                wt, Dw = ws.load(w_in_odd, [(cb4 * 512, 512, 0)])
                wt2, Dw2 = ws.load(w_in_odd, [(8192 + cb4 * 512, 512, 0)])
                for j in range(4):
                    ct = cb4 * 4 + j
                    g = ct // 2
                    b = ct % 2
                    c.op("dve", lambda e: e.scalar_tensor_tensor(out=B2[:], in0=rsbc[:, g, :], scalar=lb[:, ct:ct + 1], in1=bsbc[:, g, :],
                                                                 op0=ALU.mult, op1=ALU.add), reads=[Drs, Dbs, Dlg], writes=[DB2])
                    for th in range(2):
                        tsl = slice(th * 512, (th + 1) * 512)
                        pu = next_ps()
                        mm_group(pu, 0, 512, lambda kc: wt[:, kc, j * 128:(j + 1) * 128], lambda kc: hT[:, kc, tsl], [Dw] + DhT[4 * th:4 * th + 4])
                        pz = next_ps()
                        mm_group(pz, 0, 512, lambda kc: wt2[:, kc, j * 128:(j + 1) * 128], lambda kc: hT[:, kc, tsl], [Dw2] + DhT[4 * th:4 * th + 4])
                        c.op("act", lambda e: e.activation(out=szS[:], in_=ps[pz][:, :], func=AF.Silu), reads=[Dps[pz]], writes=[Dsz])
                        pm = next_ps()
                        for k4 in range(4):
                            tt = th * 4 + k4
                            c.op("pe", lambda e: e.matmul(ps[pm][:, k4 * 128:(k4 + 1) * 128], lhsT=vsb[:, tt, ct * 128:(ct + 1) * 128],
                                                          rhs=wsT[:, g, :], start=True, stop=True), reads=[Dv[tt], DwsT], writes=[Dps[pm]])
                        c.op("dve", lambda e: e.scalar_tensor_tensor(out=m1[:, :].rearrange("p (a b) -> p a b", b=128),
                                                                     in0=ps[pm][:, :].rearrange("p (a b) -> p a b", b=128),
                                                                     scalar=lg[:, ct:ct + 1],
                                                                     in1=B2[:, :].unsqueeze(1).to_broadcast([128, 4, 128]),
                                                                     op0=ALU.mult, op1=ALU.add), reads=[Dps[pm], DB2, Dlg], writes=[Dm1])
                        c.op("dve", lambda e: e.tensor_tensor(out=m1[:], in0=m1[:], in1=ps[pu][:, :], op=ALU.mult),
                             reads=[Dm1, Dps[pu]], writes=[Dm1])
                        c.op("dve", lambda e: e.tensor_tensor(out=y2[b][:, tsl], in0=m1[:], in1=szS[:], op=ALU.mult),
                             reads=[Dm1, Dsz], writes=[Dy2[b]])
                    c.dma("sp", s_y2[ct], y2[b][:], reads=[Dy2[b]], writes=[Dep()])
            c.barrier()

    if stop >= 7:
        out_proj(s_y2, w_out_odd, s_x1, out, norm_final, "g")

    c.barrier()
    top.close()
    c.close()
    return nc


def _bf(a):
    return np.asarray(a, dtype=np.float32).astype(ml_dtypes.bfloat16)


def own_tiles(hh):
    return [2 * i + (hh ^ (i & 1)) for i in range(8)]


def host_tables(hh):
    tiles = own_tiles(hh)
    own_pos = np.concatenate([np.arange(128 * t, 128 * t + 128) for t in tiles])
    half = 16
    inv_freq = np.power(np.float32(500000.0), -np.arange(half, dtype=np.float32) * np.float32(2.0) / np.float32(32)).astype(np.float32)

    def cs(pos):
        ang = pos.astype(np.float32)[:, None] * inv_freq[None, :]
        co = np.cos(ang).astype(np.float32).T
        si = np.sin(ang).astype(np.float32).T
        n = co.shape[1]
        return (np.ascontiguousarray(np.concatenate([co, co, np.ones((96, n), np.float32)], 0)),
                np.ascontiguousarray(np.concatenate([si, si, np.zeros((96, n), np.float32)], 0)))

    ca, sa = cs(np.arange(T))
    co, so = cs(own_pos)
    R = np.zeros((128, 128), np.float32)
    for m in range(16):
        R[m + 16, m] = -1.0
        R[m, m + 16] = 1.0
    tb = {"t_cos_all": ca, "t_sin_all": sa, "t_cos_own": co, "t_sin_own": so, "t_R": _bf(R),
          "t_ident": _bf(np.eye(128)), "t_identf": np.eye(128, dtype=np.float32), "t_ones": _bf(np.ones((128, 128)))}
    j = np.arange(32)
    am = np.zeros((1024, 32), np.float32)
    tpos = own_pos[:, None]
    forced = (j[None, :] == 0) | (j[None, :] == tpos // 64)
    causal = (64 * j[None, :]) <= tpos
    am[~causal] = -BIG
    am[forced] = BIG
    tb["t_addmask"] = np.ascontiguousarray(am.reshape(8, 128, 32).transpose(1, 0, 2).reshape(128, 256))
    cc = np.arange(128)[:, None]
    cm = ((cc < 127) & (16 * cc + 31 <= own_pos[None, :])).astype(np.float32)
    ncm = (cm.reshape(128, 8, 1, 128) - 1.0) * 30000.0
    tb["t_ncmask"] = _bf(np.broadcast_to(ncm, (128, 8, 4, 128)).reshape(128, 8 * 512))
    tb["t_overlap"] = _bf(((cc < 127) & (16 * cc < 64 * j[None, :] + 64) & (16 * cc + 32 > 64 * j[None, :])).astype(np.float32))
    tb["t_expand"] = _bf((np.arange(T)[None, :] // 64 == np.arange(128)[:, None]).astype(np.float32))
    tk = np.arange(128)[:, None]
    tq = np.arange(128)[None, :]
    dm = np.zeros((128, 4, 128), np.float32)
    for p in range(2):
        for o in range(2):
            r = o - (hh ^ p)
            dm[:, p * 2 + o, :] = (128 * r + tk <= tq)
    tb["t_ndmask"] = _bf(np.broadcast_to((dm.reshape(128, 4, 1, 128) - 1.0) * 30000.0, (128, 4, 4, 128)).reshape(128, 4 * 512))
    wm = np.zeros((128, 12, 128), np.float32)
    for p in range(2):
        for o in range(6):
            r = o - 4 - (hh ^ p)
            diff = tq - tk - 128 * r
            wm[:, p * 6 + o, :] = (diff >= 0) & (diff < 512)
    tb["t_nwmask"] = _bf(np.broadcast_to((wm.reshape(128, 12, 1, 128) - 1.0) * 30000.0, (128, 12, 4, 128)).reshape(128, 12 * 512))
    s48 = np.zeros((128, 48, 128), np.float32)
    for r in range(48):
        s48[r, r, :] = 1.0
    tb["t_sel48"] = _bf(s48.reshape(128, 48 * 128))
    tb["t_tril"] = np.tril(np.ones((128, 128), np.float32))
    return tb


def make_in_maps(inp):
    f = lambda a: np.ascontiguousarray(np.asarray(a, dtype=np.float32))
    x = f(inp["x"])
    shared = {
        "norm_even": f(inp["norm_even"]).reshape(1, D),
        "w_in_even": f(inp["w_in_even"]).reshape(D, L0),
        "conv_w": f(inp["conv_w"]).reshape(3, 16, 128).reshape(48, 128),
        "cmp_k_pos": f(inp["cmp_k_pos"]).reshape(32, 128), "cmp_k_w1": f(inp["cmp_k_w1"]).reshape(4096, 128),
        "cmp_k_b1": f(inp["cmp_k_b1"]).reshape(128, 1), "cmp_k_w2": f(inp["cmp_k_w2"]).reshape(128, 128),
        "cmp_v_pos": f(inp["cmp_v_pos"]).reshape(32, 128), "cmp_v_w1": f(inp["cmp_v_w1"]).reshape(4096, 128),
        "cmp_v_b1": f(inp["cmp_v_b1"]).reshape(128, 1), "cmp_v_w2": f(inp["cmp_v_w2"]).reshape(128, 128),
        "w_out_even": f(inp["w_out_even"]).reshape(4096, D),
        "norm_odd": f(inp["norm_odd"]).reshape(1, D),
        "w_in_odd": f(inp["w_in_odd"]).reshape(D, 12288),
        "sgu_ln_g": f(inp["sgu_ln_g"]).reshape(32, 128), "sgu_ln_b": f(inp["sgu_ln_b"]).reshape(32, 128),
        "sgu_w_s": f(inp["sgu_w_s"]).reshape(16, 128, 128), "sgu_b_s": f(inp["sgu_b_s"]).reshape(1, 2048),
        "w_out_odd": f(inp["w_out_odd"]).reshape(4096, D),
        "norm_final": f(inp["norm_final"]).reshape(1, D),
    }
    tabs = [host_tables(0), host_tables(1)]
    maps = []
    for cidx in range(8):
        b, hh = cidx // 2, cidx % 2
        tiles = own_tiles(hh)
        rows = np.concatenate([np.arange(128 * t, 128 * t + 128) for t in tiles])
        xh = np.zeros((128, D), np.float32)
        for i, t in enumerate(tiles):
            if t > 0:
                xh[2 * i:2 * i + 2] = x[b, 128 * t - 2:128 * t]
        m = dict(shared)
        m.update(tabs[hh])
        m["x_all"] = x[b]
        m["x_own"] = np.ascontiguousarray(x[b][rows])
        m["x_halo"] = xh
        maps.append(m)
    return maps


_NC = {}


def kernel(**inputs):
    if "nc" not in _NC:
        _NC["nc"] = build()
    maps = make_in_maps(inputs)
    res = run_bass_kernel_spmd(_NC["nc"], maps, core_ids=list(range(8)))
    outp = np.zeros((4, T, D), np.float32)
    for cidx in range(8):
        b, hh = cidx // 2, cidx % 2
        o = np.asarray(res.results[cidx]["out"], dtype=np.float32)
        for i, t in enumerate(own_tiles(hh)):
            outp[b, 128 * t:128 * t + 128] = o[128 * i:128 * i + 128]
    return outp
```

```python
from contextlib import ExitStack
import numpy as np
import ml_dtypes
import concourse.bass as bass
import concourse.mybir as mybir
from concourse.bass_utils import run_bass_kernel_spmd

F32 = mybir.dt.float32
BF16 = mybir.dt.bfloat16
ALU = mybir.AluOpType
AF = mybir.ActivationFunctionType
AX = mybir.AxisListType

D = 2048
T = 2048
TO = 1024
L0 = 15408
OFF = dict(cb=0, cc=2048, ch=4096, cg=6144, q=8192, kc=10240, vc=10752, ks=11264, vs=11776, kw=12288,
           vw=12800, gl=13312, ng=13360)
SCALE = 128 ** -0.5
EPS = 1e-6
BIG = 1e30


class Dep:
    __slots__ = ("w", "r", "pr", "excl")

    def __init__(self, excl=False):
        self.w = {}
        self.r = {}
        self.pr = {}
        self.excl = excl


class Ctx:
    DMA_POOL = 12

    def __init__(self, nc, same_engine_sync=True):
        self.nc = nc
        self.same = same_engine_sync
        self.eng = {"pe": nc.tensor, "act": nc.scalar, "dve": nc.vector, "pool": nc.gpsimd, "sp": nc.sync}
        self.sem = {}
        self.cnt = {}
        self.seen = {k: {} for k in self.eng}
        self._stack = []
        for k in self.eng:
            cm = nc.semaphore("s_" + k)
            self.sem[k] = cm.__enter__()
            self._stack.append(cm)
            self.cnt[k] = 0
        self.dpool = {}
        self.dpos = {}
        for q in ("sp", "pool", "act"):
            lst = []
            for i in range(self.DMA_POOL):
                cm = nc.semaphore("d_%s%d" % (q, i))
                lst.append([cm.__enter__(), 0])
                self._stack.append(cm)
            self.dpool[q] = lst
            self.dpos[q] = 0

    def close(self):
        for cm in reversed(self._stack):
            cm.__exit__(None, None, None)

    def _wait(self, e, tickets):
        E = self.eng[e]
        seen = self.seen[e]
        own = id(self.sem[e])
        for sid, (sem, val) in tickets.items():
            if seen.get(sid, 0) >= val:
                continue
            if sid == own and (e == "pe" or not self.same):
                continue
            E.wait_ge(sem, val)
            seen[sid] = val

    @staticmethod
    def _merge(dst, src):
        for sid, tv in src.items():
            if sid not in dst or dst[sid][1] < tv[1]:
                dst[sid] = tv

    def _collect(self, reads, writes, own=None):
        t = {}
        for d in reads:
            self._merge(t, d.w)
            if d.excl:
                self._merge(t, {k: v for k, v in d.r.items() if k != own})
        for d in writes:
            if d.r:
                self._merge(t, d.r)
            elif d.pr:
                self._merge(t, d.pr)
        return t

    def _record(self, reads, writes, ticket):
        tk = {id(ticket[0]): ticket}
        for d in writes:
            if d.r:
                d.w = dict(tk)
                d.pr = d.r
                d.r = {}
            else:
                self._merge(d.w, tk)
        for d in reads:
            self._merge(d.r, tk)

    def op(self, e, fn, reads=(), writes=()):
        self._wait(e, self._collect(reads, writes, id(self.sem[e])))
        inst = fn(self.eng[e])
        self.cnt[e] += 1
        inst.then_inc(self.sem[e], 1)
        self._record(reads, writes, (self.sem[e], self.cnt[e]))
        return inst

    def dma(self, q, out, in_, reads=(), writes=(), **kw):
        self._wait(q, self._collect(reads, writes))
        slot = self.dpool[q][self.dpos[q] % self.DMA_POOL]
        self.dpos[q] += 1
        E = self.eng[q]
        if slot[1] > 0 and self.seen[q].get(id(slot[0]), 0) < slot[1]:
            E.wait_ge(slot[0], slot[1])
            self.seen[q][id(slot[0])] = slot[1]
        inst = E.dma_start(out=out, in_=in_, **kw)
        slot[1] += 16
        inst.then_inc(slot[0], 16)
        self._record(reads, writes, (slot[0], slot[1]))
        return inst

    def barrier(self):
        t = {}
        for k in self.eng:
            if self.cnt[k]:
                t[id(self.sem[k])] = (self.sem[k], self.cnt[k])
        for q in self.dpool:
            for slot in self.dpool[q]:
                if slot[1]:
                    t[id(slot[0])] = (slot[0], slot[1])
        for e in self.eng:
            E = self.eng[e]
            seen = self.seen[e]
            for sid, (sem, val) in t.items():
                if sid == id(self.sem[e]) or seen.get(sid, 0) >= val:
                    continue
                E.wait_ge(sem, val)
                seen[sid] = val


def build(debug=False, stop=99):
    nc = bass.Bass("TRN2", target_bir_lowering=False)
    c = Ctx(nc)

    def din(name, shape, dt=F32):
        return nc.dram_tensor(name, list(shape), dt, kind="ExternalInput").ap()

    def dscr(name, shape, dt):
        return nc.dram_tensor(name, list(shape), dt, kind=("ExternalOutput" if debug else "Internal")).ap()

    x_all = din("x_all", [T, D])
    x_own = din("x_own", [TO, D])
    x_halo = din("x_halo", [128, D])
    norm_even = din("norm_even", [1, D])
    w_in_even = din("w_in_even", [D, L0])
    conv_w = din("conv_w", [48, 128])
    kpos = din("cmp_k_pos", [32, 128])
    kw1 = din("cmp_k_w1", [4096, 128])
    kb1 = din("cmp_k_b1", [128, 1])
    kw2 = din("cmp_k_w2", [128, 128])
    vpos = din("cmp_v_pos", [32, 128])
    vw1 = din("cmp_v_w1", [4096, 128])
    vb1 = din("cmp_v_b1", [128, 1])
    vw2 = din("cmp_v_w2", [128, 128])
    w_out_even = din("w_out_even", [4096, D])
    norm_odd = din("norm_odd", [1, D])
    w_in_odd = din("w_in_odd", [D, 12288])
    ln_g = din("sgu_ln_g", [32, 128])
    ln_b = din("sgu_ln_b", [32, 128])
    w_s = din("sgu_w_s", [16, 128, 128])
    b_s = din("sgu_b_s", [1, 2048])
    w_out_odd = din("w_out_odd", [4096, D])
    norm_final = din("norm_final", [1, D])
    t_cos_all = din("t_cos_all", [128, T])
    t_sin_all = din("t_sin_all", [128, T])
    t_cos_own = din("t_cos_own", [128, TO])
    t_sin_own = din("t_sin_own", [128, TO])
    t_R = din("t_R", [128, 128], BF16)
    t_ident = din("t_ident", [128, 128], BF16)
    t_identf = din("t_identf", [128, 128], F32)
    t_ones = din("t_ones", [128, 128], BF16)
    t_addmask = din("t_addmask", [128, 8 * 32])
    t_ncmask = din("t_ncmask", [128, 8 * 512], BF16)
    t_overlap = din("t_overlap", [128, 32], BF16)
    t_expand = din("t_expand", [128, T], BF16)
    t_ndmask = din("t_ndmask", [128, 4 * 512], BF16)
    t_nwmask = din("t_nwmask", [128, 12 * 512], BF16)
    t_sel48 = din("t_sel48", [128, 48 * 128], BF16)
    t_tril = din("t_tril", [128, 128])

    out = nc.dram_tensor("out", [TO, D], F32, kind="ExternalOutput").ap()

    s_kc = dscr("s_kc", [4, 128, T], BF16)
    s_vc = dscr("s_vc", [4, 128, T], BF16)
    s_ks = dscr("s_ks", [4, 128, T], BF16)
    s_kw = dscr("s_kw", [4, 128, T], BF16)
    s_vs = dscr("s_vs", [128, 16, 512], BF16)
    s_vw = dscr("s_vw", [128, 16, 512], BF16)
    s_q = dscr("s_q", [16, 128, TO], BF16)
    s_qr = dscr("s_qr", [16, 128, TO], BF16)
    s_ng = dscr("s_ng", [16, 128, TO], BF16)
    s_gt = dscr("s_gt", [48, TO], BF16)
    s_y = dscr("s_y", [128, 32, TO], BF16)
    s_x1 = dscr("s_x1", [TO, D], F32)
    s_y2 = dscr("s_y2", [32, 128, TO], BF16)
    s_dbg = dscr("s_dbg", [128, 2048], F32)
    s_dbgb = dscr("s_dbgb", [128, 1024], BF16)

    top = ExitStack()

    def sbt(st, name, shape, dt):
        return st.enter_context(nc.sbuf_tensor(name, list(shape), dt))

    ps = [top.enter_context(nc.psum_tensor("ps%d" % i, [128, 512], F32)) for i in range(7)]
    Dps = [Dep(excl=True) for _ in range(7)]
    psb = top.enter_context(nc.psum_tensor("psb", [128, 1024], BF16))
    Dpsb = Dep(excl=True)

    ident = sbt(top, "ident", [128, 128], BF16)
    identf = sbt(top, "identf", [128, 128], F32)
    ones = sbt(top, "ones", [128, 128], BF16)
    epsc = sbt(top, "epsc", [128, 1], F32)
    Dconst = Dep()
    c.dma("sp", ident[:], t_ident, writes=[Dconst])
    c.dma("sp", identf[:], t_identf, writes=[Dconst])
    c.dma("sp", ones[:], t_ones, writes=[Dconst])
    c.op("dve", lambda e: e.memset(epsc[:], EPS), writes=[Dconst])

    act_flip = [0]

    def evac(out_ap, in_ap, reads, writes):
        act_flip[0] ^= 1
        if act_flip[0]:
            c.op("act", lambda e: e.copy(out=out_ap, in_=in_ap), reads, writes)
        else:
            c.op("dve", lambda e: e.tensor_copy(out=out_ap, in_=in_ap), reads, writes)

    def norm_parts(st, x_src, gain_src, hT, DhT, prefix):
        gbc = sbt(st, prefix + "gbc", [128, D], F32)
        xt = [sbt(st, prefix + "xt%d" % i, [128, D], F32) for i in range(2)]
        hb = [sbt(st, prefix + "hb%d" % i, [128, D], BF16) for i in range(2)]
        stat = sbt(st, prefix + "stat", [128, 4], F32)
        Dg, Dst = Dep(), Dep()
        Dxt = [Dep(), Dep()]
        Dhb = [Dep(), Dep()]
        c.dma("sp", gbc[:], gain_src.to_broadcast([128, D]), writes=[Dg])
        nr = 128
        fifo = []
        cnt = [0]

        def partA(t, src=None):
            b = cnt[0] % 2
            cnt[0] += 1
            fifo.append(b)
            src_ap = src if src is not None else x_src[t * nr:(t + 1) * nr, :]
            c.dma("sp", xt[b][0:nr, :], src_ap, writes=[Dxt[b]])
            c.op("act", lambda e: e.activation(out=hb[b][0:nr, :], in_=xt[b][0:nr, :], func=AF.Square,
                                               accum_out=stat[0:nr, 0:1]), reads=[Dxt[b]], writes=[Dhb[b], Dst])
            c.op("act", lambda e: e.activation(out=stat[0:nr, 1:2], in_=stat[0:nr, 0:1], func=AF.Sqrt,
                                               scale=1.0 / D, bias=epsc[0:nr, :]), reads=[Dst, Dconst], writes=[Dst])
            c.op("dve", lambda e: e.reciprocal(out=stat[0:nr, 2:3], in_=stat[0:nr, 1:2]), reads=[Dst], writes=[Dst])
            c.op("dve", lambda e: e.scalar_tensor_tensor(out=hb[b][0:nr, :], in0=xt[b][0:nr, :], scalar=stat[0:nr, 2:3],
                                                         in1=gbc[0:nr, :], op0=ALU.mult, op1=ALU.mult),
                 reads=[Dxt[b], Dst, Dg, Dhb[b]], writes=[Dhb[b]])

        def partB(t, dst=None):
            b = fifo.pop(0)
            if dst is not None:
                hT_d, col0, Dt = dst
            else:
                hT_d, col0, Dt = hT, t * nr, (DhT[t] if isinstance(DhT, list) else DhT)
            for a in range(4):
                tgt, Dtgt = (psb[:], Dpsb) if a % 2 == 0 else (ps[6][:].bitcast(BF16), Dps[6])
                for j in range(4):
                    kc = 4 * a + j
                    c.op("pe", lambda e: e.transpose(out=tgt[:, j * nr:(j + 1) * nr], in_=hb[b][0:nr, kc * 128:(kc + 1) * 128],
                                                     identity=ident[0:nr, 0:nr]), reads=[Dhb[b], Dconst], writes=[Dtgt])
                evac(hT_d[:, 4 * a:4 * a + 4, col0:col0 + nr],
                     tgt[:, 0:4 * nr].rearrange("p (a b) -> p a b", b=nr), [Dtgt], [Dt])
        return partA, partB

    def norm_transpose(st, x_src, ntiles, gain_src, hT, DhT, prefix):
        pa, pb = norm_parts(st, x_src, gain_src, hT, DhT, prefix)
        for t in range(ntiles):
            pa(t)
            pb(t)

    class WS:
        def __init__(self, st, name, n=3, cols=512):
            self.t = [sbt(st, "%s%d" % (name, i), [128, 16, cols], BF16) for i in range(n)]
            self.d = [Dep() for _ in range(n)]
            self.i = 0

        def load(self, w_ap, pieces):
            k = self.i % len(self.t)
            self.i += 1
            wv = w_ap.rearrange("(kc p) n -> p kc n", p=128)
            for (co, ncol, dc) in pieces:
                for h in range(4):
                    c.dma("pool", self.t[k][:, 4 * h:4 * h + 4, dc:dc + ncol], wv[:, 4 * h:4 * h + 4, co:co + ncol],
                          writes=[self.d[k]])
            return self.t[k], self.d[k]

    psrot = [0]

    def next_ps(n=5):
        psrot[0] = (psrot[0] + 1) % n
        return psrot[0]

    def mm_group(pi, col0, ncols, lhs_fn, rhs_fn, reads, M=128, nk=16):
        for kc in range(nk):
            c.op("pe", lambda e: e.matmul(ps[pi][0:M, col0:col0 + ncols], lhsT=lhs_fn(kc), rhs=rhs_fn(kc),
                                          start=(kc == 0), stop=(kc == nk - 1)), reads=reads, writes=[Dps[pi]])

    def rope_epi(pi, ncols, plainS, Dplain, rotS, Drot, scol, cosT, sinT, tcol, Dtab, Rm, tmp, Dtmp, tmpb, Dtmpb):
        P = ps[pi]
        if plainS is not None:
            src, Dsrc, so = plainS, Dplain, scol
        else:
            src, Dsrc, so = tmpb, Dtmpb, 0
        c.op("act", lambda e: e.copy(out=src[:, so:so + ncols], in_=P[:, 0:ncols]), reads=[Dps[pi]], writes=[Dsrc])
        c.op("pe", lambda e: e.matmul(ps[5][:, 0:ncols], lhsT=Rm[:, :], rhs=src[:, so:so + ncols],
                                      start=True, stop=True), reads=[Dsrc, Dtab], writes=[Dps[5]])
        import os
        RL = int(os.environ.get("RL", "9"))
        if RL < 3:
            return
        c.op("dve", lambda e: e.tensor_tensor(out=tmp[:, 0:ncols], in0=P[:, 0:ncols], in1=cosT[:, tcol:tcol + ncols],
                                              op=ALU.mult), reads=[Dps[pi], Dtab], writes=[Dtmp])
        if RL < 4:
            return
        c.op("dve", lambda e: e.tensor_tensor(out=tmp[:, 512:512 + ncols], in0=ps[5][:, 0:ncols],
                                              in1=sinT[:, tcol:tcol + ncols], op=ALU.mult),
             reads=[Dps[5], Dtab, Dtmp], writes=[Dtmp])
        if RL < 5:
            return
        c.op("dve", lambda e: e.tensor_tensor(out=rotS[:, scol:scol + ncols], in0=tmp[:, 0:ncols],
                                              in1=tmp[:, 512:512 + ncols], op=ALU.add), reads=[Dtmp], writes=[Drot])

    if stop >= 1:
        with ExitStack() as st:
            hT = sbt(st, "hTall", [128, 16, T], BF16)
            DhT = [Dep() for _ in range(16)]
            pA, pB = norm_parts(st, x_all, norm_even, hT, DhT, "a1")
            ws = WS(st, "a1w")
            cosT = sbt(st, "a1cos", [128, T], F32)
            sinT = sbt(st, "a1sin", [128, T], F32)
            Rm = sbt(st, "a1R", [128, 128], BF16)
            tmp = sbt(st, "a1tmp", [128, 1024], F32)
            tmpb = sbt(st, "a1tmpb", [128, 512], BF16)
            Dtmpb = Dep()
            Dtab, Dtmp = Dep(), Dep()
            stg = [sbt(st, "a1stg%d" % i, [128, T], BF16) for i in range(4)]
            Dstg = [Dep() for _ in range(4)]
            vst = [sbt(st, "a1vst%d" % i, [128, 512], BF16) for i in range(4)]
            Dvst = [Dep() for _ in range(4)]
            for t in range(4):
                pA(t)
                pB(t)
            pA(4)
            c.dma("sp", cosT[:], t_cos_all, writes=[Dtab])
            c.dma("sp", sinT[:], t_sin_all, writes=[Dtab])
            c.dma("sp", Rm[:], t_R, writes=[Dtab])
            nxt = [4]

            def inject():
                n = nxt[0]
                if n <= 15:
                    pB(n)
                    if n + 1 <= 15:
                        pA(n + 1)
                    nxt[0] = n + 1

            wt, Dw = ws.load(w_in_even, [(OFF["kc"], 512, 0)])
            for tq in range(4):
                for g in range(4):
                    inject()
                    pi = next_ps()
                    mm_group(pi, 0, 512, lambda kc: wt[:, kc, g * 128:(g + 1) * 128],
                             lambda kc: hT[:, kc, tq * 512:(tq + 1) * 512], [Dw] + DhT[4 * tq:4 * tq + 4])
                    evac(stg[g][:, tq * 512:(tq + 1) * 512], ps[pi][:, :], [Dps[pi]], [Dstg[g]])
            while nxt[0] <= 15:
                inject()
            for g in range(4):
                c.dma("sp", s_kc[g], stg[g][:], reads=[Dstg[g]], writes=[Dep()])
            si = 0
            for name, dst, rope in (("vc", s_vc, False), ("ks", s_ks, True), ("kw", s_kw, True)):
                wt, Dw = ws.load(w_in_even, [(OFF[name], 512, 0)])
                for g in range(4):
                    S, DS = stg[si % 4], Dstg[si % 4]
                    si += 1
                    for tq in range(4):
                        pi = next_ps()
                        mm_group(pi, 0, 512, lambda kc: wt[:, kc, g * 128:(g + 1) * 128],
                                 lambda kc: hT[:, kc, tq * 512:(tq + 1) * 512], [Dw] + DhT[4 * tq:4 * tq + 4])
                        if rope:
                            rope_epi(pi, 512, None, None, S, DS, tq * 512, cosT, sinT, tq * 512, Dtab, Rm, tmp, Dtmp, tmpb, Dtmpb)
                        else:
                            evac(S[:, tq * 512:(tq + 1) * 512], ps[pi][:, :], [Dps[pi]], [DS])
                    c.dma("sp", dst[g], S[:], reads=[DS], writes=[Dep()])
            vi = 0
            for name, dst in (("vs", s_vs), ("vw", s_vw)):
                wt, Dw = ws.load(w_in_even, [(OFF[name], 512, 0)])
                for tt in range(16):
                    pi = next_ps()
                    mm_group(pi, 0, 512, lambda kc: hT[:, kc, tt * 128:(tt + 1) * 128], lambda kc: wt[:, kc, :], [Dw, DhT[tt]])
                    b = vi % 4
                    vi += 1
                    evac(vst[b][:], ps[pi][:, :], [Dps[pi]], [Dvst[b]])
                    c.dma("sp", dst[:, tt, :], vst[b][:], reads=[Dvst[b]], writes=[Dep()])
            c.barrier()

    if stop >= 2:
        with ExitStack() as st:
            hT = sbt(st, "hTown", [128, 16, TO], BF16)
            hTh = sbt(st, "hThalo", [128, 16, 128], BF16)
            DhT, DhTh = [Dep() for _ in range(8)], Dep()
            pA, pB = norm_parts(st, x_own, norm_even, hT, DhT, "a2")
            pA(0, src=x_halo[0:128, :])
            pB(0, dst=(hTh, 0, DhTh))
            for t in range(4):
                pA(t)
                pB(t)
            pA(4)
            nxt = [4]

            def inject():
                n = nxt[0]
                if n <= 7:
                    pB(n)
                    if n + 1 <= 7:
                        pA(n + 1)
                    nxt[0] = n + 1

            ws = WS(st, "a2w", n=3)
            cosT = sbt(st, "a2cos", [128, TO], F32)
            sinT = sbt(st, "a2sin", [128, TO], F32)
            Rm = sbt(st, "a2R", [128, 128], BF16)
            tmp = sbt(st, "a2tmp", [128, 1024], F32)
            tmpb = sbt(st, "a2tmpb", [128, 512], BF16)
            Dtmpb = Dep()
            cwr = sbt(st, "a2cwr", [128, 128], F32)
            cw = sbt(st, "a2cw", [128, 48], F32)
            Dtab, Dtmp, Dcw = Dep(), Dep(), Dep()
            c.dma("sp", cosT[:], t_cos_own, writes=[Dtab])
            c.dma("sp", sinT[:], t_sin_own, writes=[Dtab])
            c.dma("sp", Rm[:], t_R, writes=[Dtab])
            Dz = Dep()
            c.op("dve", lambda e: e.memset(cwr[:], 0.0), writes=[Dcw, Dz])
            c.dma("sp", cwr[0:48, :], conv_w, reads=[Dz], writes=[Dcw])
            c.op("pe", lambda e: e.transpose(out=ps[6][:, 0:128], in_=cwr[:, :], identity=identf[:, :]),
                 reads=[Dcw, Dconst], writes=[Dps[6]])
            c.op("dve", lambda e: e.tensor_copy(out=cw[:], in_=ps[6][:, 0:48]), reads=[Dps[6]], writes=[Dcw])
            NB = 2
            ccS = [sbt(st, "ccS%d" % i, [128, TO], F32) for i in range(NB)]
            cbS = [sbt(st, "cbS%d" % i, [128, TO], F32) for i in range(NB)]
            sgS = [sbt(st, "sgS%d" % i, [128, TO], F32) for i in range(NB)]
            uS = [sbt(st, "uS%d" % i, [128, 8, 130], F32) for i in range(NB)]
            acc = [sbt(st, "acc%d" % i, [128, TO], F32) for i in range(NB)]
            yS = [sbt(st, "yS%d" % i, [128, TO], BF16) for i in range(NB)]
            hcc = sbt(st, "hcc", [128, 16], F32)
            Dcc, Dcb, Dsg, Du, Dacc, DyS = [[Dep() for _ in range(NB)] for _ in range(6)]
            Dhcc = Dep()
            for ct in range(16):
                b = ct % NB
                wt, Dw = ws.load(w_in_even, [(OFF["cc"] + ct * 128, 128, 0), (OFF["ch"] + ct * 128, 128, 128),
                                             (OFF["cb"] + ct * 128, 128, 256), (OFF["cg"] + ct * 128, 128, 384)])
                for f in range(2):
                    mm_group(6, 16 * f, 16, lambda kc: wt[:, kc, f * 128:(f + 1) * 128], lambda kc: hTh[:, kc, 0:16], [Dw, DhTh])
                c.op("act", lambda e: e.copy(out=hcc[:], in_=ps[6][:, 0:16]), reads=[Dps[6]], writes=[Dhcc])
                c.op("dve", lambda e: e.tensor_tensor(out=uS[b][:, :, 0:2], in0=hcc[:].rearrange("p (a b) -> p a b", b=2),
                                                      in1=ps[6][:, 16:32].rearrange("p (a b) -> p a b", b=2), op=ALU.mult),
                     reads=[Dhcc, Dps[6]], writes=[Du[b]])
                for th in range(2):
                    tsl = slice(th * 512, (th + 1) * 512)
                    rhs = lambda kc: hT[:, kc, tsl]
                    pi = next_ps()
                    if ct == 0 and th == 0:
                        inject()
                    mm_group(pi, 0, 512, lambda kc: wt[:, kc, 0:128], rhs, [Dw] + DhT[4 * th:4 * th + 4])
                    c.op("act", lambda e: e.copy(out=ccS[b][:, tsl], in_=ps[pi][:, :]), reads=[Dps[pi]], writes=[Dcc[b]])
                    pi = next_ps()
                    if ct == 0 and th == 0:
                        inject()
                    mm_group(pi, 0, 512, lambda kc: wt[:, kc, 128:256], rhs, [Dw] + DhT[4 * th:4 * th + 4])
                    c.op("dve", lambda e: e.tensor_tensor(out=uS[b][:, 4 * th:4 * th + 4, 2:130],
                                                          in0=ccS[b][:, tsl].rearrange("p (a b) -> p a b", b=128),
                                                          in1=ps[pi][:, :].rearrange("p (a b) -> p a b", b=128), op=ALU.mult),
                         reads=[Dcc[b], Dps[pi]], writes=[Du[b]])
                    pi = next_ps()
                    if ct == 0 and th == 0:
                        inject()
                    mm_group(pi, 0, 512, lambda kc: wt[:, kc, 256:384], rhs, [Dw] + DhT[4 * th:4 * th + 4])
                    c.op("act", lambda e: e.copy(out=cbS[b][:, tsl], in_=ps[pi][:, :]), reads=[Dps[pi]], writes=[Dcb[b]])
                    pi = next_ps()
                    if ct == 0 and th == 0:
                        inject()
                    mm_group(pi, 0, 512, lambda kc: wt[:, kc, 384:512], rhs, [Dw] + DhT[4 * th:4 * th + 4])
                    c.op("act", lambda e: e.activation(out=sgS[b][:, tsl], in_=ps[pi][:, :], func=AF.Silu),
                         reads=[Dps[pi]], writes=[Dsg[b]])
                a3 = acc[b][:, :].rearrange("p (a b) -> p a b", b=128)
                c.op("dve", lambda e: e.tensor_scalar(out=a3, in0=uS[b][:, :, 2:130], scalar1=cw[:, 32 + ct:33 + ct], scalar2=None,
                                                      op0=ALU.mult), reads=[Du[b], Dcw], writes=[Dacc[b]])
                c.op("dve", lambda e: e.scalar_tensor_tensor(out=a3, in0=uS[b][:, :, 1:129], scalar=cw[:, 16 + ct:17 + ct], in1=a3,
                                                             op0=ALU.mult, op1=ALU.add), reads=[Du[b], Dcw, Dacc[b]], writes=[Dacc[b]])
                c.op("dve", lambda e: e.scalar_tensor_tensor(out=a3, in0=uS[b][:, :, 0:128], scalar=cw[:, ct:ct + 1], in1=a3,
                                                             op0=ALU.mult, op1=ALU.add), reads=[Du[b], Dcw, Dacc[b]], writes=[Dacc[b]])
                c.op("dve", lambda e: e.tensor_tensor(out=acc[b][:, :], in0=acc[b][:, :], in1=cbS[b][:, :], op=ALU.mult),
                     reads=[Dacc[b], Dcb[b]], writes=[Dacc[b]])
                c.op("dve", lambda e: e.tensor_tensor(out=yS[b][:, :], in0=acc[b][:, :], in1=sgS[b][:, :], op=ALU.mult),
                     reads=[Dacc[b], Dsg[b]], writes=[DyS[b]])
                c.dma("sp", s_y[:, ct, :], yS[b][:], reads=[DyS[b]], writes=[Dep()])
            qS = [sbt(st, "qS%d" % i, [128, TO], BF16) for i in range(2)]
            qrS = [sbt(st, "qrS%d" % i, [128, TO], BF16) for i in range(2)]
            DqS = [Dep(), Dep()]
            DqrS = [Dep(), Dep()]
            si = 0
            for a in range(4):
                wt, Dw = ws.load(w_in_even, [(OFF["q"] + a * 512, 512, 0)])
                for j in range(4):
                    b = si % 2
                    si += 1
                    for th in range(2):
                        pi = next_ps()
                        mm_group(pi, 0, 512, lambda kc: wt[:, kc, j * 128:(j + 1) * 128],
                                 lambda kc: hT[:, kc, th * 512:(th + 1) * 512], [Dw] + DhT[4 * th:4 * th + 4])
                        rope_epi(pi, 512, qS[b], DqS[b], qrS[b], DqrS[b], th * 512, cosT, sinT, th * 512, Dtab, Rm, tmp, Dtmp, tmpb, Dtmpb)
                    c.dma("sp", s_q[4 * a + j], qS[b][:], reads=[DqS[b]], writes=[Dep()])
                    c.dma("sp", s_qr[4 * a + j], qrS[b][:], reads=[DqrS[b]], writes=[Dep()])
            for a in range(4):
                wt, Dw = ws.load(w_in_even, [(OFF["ng"] + a * 512, 512, 0)])
                for j in range(4):
                    b = si % 2
                    si += 1
                    for th in range(2):
                        pi = next_ps()
                        mm_group(pi, 0, 512, lambda kc: wt[:, kc, j * 128:(j + 1) * 128],
                                 lambda kc: hT[:, kc, th * 512:(th + 1) * 512], [Dw] + DhT[4 * th:4 * th + 4])
                        c.op("act", lambda e: e.activation(out=qS[b][:, th * 512:(th + 1) * 512], in_=ps[pi][:, :], func=AF.Silu),
                             reads=[Dps[pi]], writes=[DqS[b]])
                    c.dma("sp", s_ng[4 * a + j], qS[b][:], reads=[DqS[b]], writes=[Dep()])
            wt, Dw = ws.load(w_in_even, [(OFF["gl"], 48, 0)])
            b = si % 2
            for th in range(2):
                pi = next_ps()
                mm_group(pi, 0, 512, lambda kc: wt[:, kc, 0:128], lambda kc: hT[:, kc, th * 512:(th + 1) * 512], [Dw] + DhT[4 * th:4 * th + 4])
                c.op("act", lambda e: e.activation(out=qS[b][0:48, th * 512:(th + 1) * 512], in_=ps[pi][0:48, :], func=AF.Sigmoid),
                     reads=[Dps[pi]], writes=[DqS[b]])
            c.dma("sp", s_gt, qS[b][0:48, :], reads=[DqS[b]], writes=[Dep()])
            c.barrier()

    if stop >= 3:
        with ExitStack() as st:
            kcomp = sbt(st, "kcomp", [128, 4, 128], BF16)
            vcomp = sbt(st, "vcomp", [128, 4, 128], BF16)
            Dkcomp, Dvcomp = Dep(), Dep()
            c.op("dve", lambda e: e.memset(kcomp[:], 0.0), writes=[Dkcomp])
            c.op("dve", lambda e: e.memset(vcomp[:], 0.0), writes=[Dvcomp])
            with ExitStack() as st2:
                w1 = sbt(st2, "bw1", [128, 32, 128], BF16)
                w2 = sbt(st2, "bw2", [128, 128], BF16)
                posr = sbt(st2, "bposr", [128, 128], BF16)
                posT = sbt(st2, "bposT", [128, 128], BF16)
                b1 = sbt(st2, "bb1", [128, 1], F32)
                bias = sbt(st2, "bbias", [128, 1], F32)
                xin = [sbt(st2, "bxin%d" % i, [128, T], BF16) for i in range(2)]
                hid = sbt(st2, "bhid", [128, 128], BF16)
                Dw1, Dw2, Dpos, Db1, Dbias, Dhid = [Dep() for _ in range(6)]
                Dxin = [Dep(), Dep()]
                xi = 0
                c.op("dve", lambda e: e.memset(hid[:], 0.0), writes=[Dhid])
                for kv, (pos_d, w1_d, b1_d, w2_d, src) in enumerate(((kpos, kw1, kb1, kw2, s_kc), (vpos, vw1, vb1, vw2, s_vc))):
                    w1v = w1_d.rearrange("(l d) j -> d l j", d=128)
                    for h in range(8):
                        c.dma("pool", w1[:, 4 * h:4 * h + 4, :], w1v[:, 4 * h:4 * h + 4, :], writes=[Dw1])
                    c.dma("pool", w2[:], w2_d, writes=[Dw2])
                    Dz = Dep()
                    c.op("dve", lambda e: e.memset(posr[:], 0.0), writes=[Dpos, Dz])
                    c.dma("pool", posr[0:32, :], pos_d, reads=[Dz], writes=[Dpos])
                    c.dma("sp", b1[:], b1_d, writes=[Db1])
                    c.op("pe", lambda e: e.transpose(out=psb[:, 0:128], in_=posr[:, :], identity=ident[:, :]),
                         reads=[Dpos, Dconst], writes=[Dpsb])
                    c.op("dve", lambda e: e.tensor_copy(out=posT[:], in_=psb[:, 0:128]), reads=[Dpsb], writes=[Dpos])
                    for l in range(32):
                        c.op("pe", lambda e: e.matmul(ps[6][:, 0:1], lhsT=w1[:, l, :], rhs=posT[:, l:l + 1], start=(l == 0), stop=(l == 31)),
                             reads=[Dw1, Dpos], writes=[Dps[6]])
                    c.op("dve", lambda e: e.tensor_tensor(out=bias[:], in0=ps[6][:, 0:1], in1=b1[:], op=ALU.add),
                         reads=[Dps[6], Db1], writes=[Dbias])
                    for g in range(4):
                        X, DX = xin[xi % 2], Dxin[xi % 2]
                        xi += 1
                        c.dma("sp", X[:], src[g], writes=[DX])
                        pi = next_ps()
                        for l in range(32):
                            c.op("pe", lambda e: e.matmul(ps[pi][:, 0:127], lhsT=w1[:, l, :], rhs=X[:, l:l + 16 * 126 + 1:16],
                                                          start=(l == 0), stop=(l == 31)), reads=[Dw1, DX], writes=[Dps[pi]])
                        c.op("act", lambda e: e.activation(out=hid[:, 0:127], in_=ps[pi][:, 0:127], func=AF.Silu, bias=bias[:]),
                             reads=[Dps[pi], Dbias], writes=[Dhid])
                        pi = next_ps()
                        if kv == 0:
                            c.op("pe", lambda e: e.matmul(ps[pi][:, 0:127], lhsT=w2[:, :], rhs=hid[:, 0:127], start=True, stop=True),
                                 reads=[Dw2, Dhid], writes=[Dps[pi]])
                            c.op("dve", lambda e: e.tensor_copy(out=kcomp[:, g, 0:127], in_=ps[pi][:, 0:127]),
                                 reads=[Dps[pi]], writes=[Dkcomp])
                        else:
                            c.op("pe", lambda e: e.matmul(ps[pi][:, 0:128], lhsT=hid[:, :], rhs=w2[:, :], start=True, stop=True),
                                 reads=[Dw2, Dhid], writes=[Dps[pi]])
                            c.op("dve", lambda e: e.tensor_copy(out=vcomp[0:127, g, :], in_=ps[pi][0:127, 0:128]),
                                 reads=[Dps[pi]], writes=[Dvcomp])
                c.barrier()
            if debug:
                c.dma("sp", s_dbgb[:, 0:512], kcomp[:].rearrange("p a b -> p (a b)"), reads=[Dkcomp], writes=[Dep()])
                c.dma("sp", s_dbgb[:, 512:1024], vcomp[:].rearrange("p a b -> p (a b)"), reads=[Dvcomp], writes=[Dep()])
            if stop >= 4:
                NEGM = 30000.0
                ps.append(psb[:].bitcast(F32))
                Dps.append(Dpsb)
                gates = sbt(st, "gates", [128, TO], BF16)
                sel48 = sbt(st, "sel48", [128, 48 * 128], BF16)
                ncmask = sbt(st, "ncmask", [128, 8 * 512], BF16)
                overlap = sbt(st, "overlap", [128, 32], BF16)
                expand = sbt(st, "expand", [128, T], BF16)
                ndmask = sbt(st, "ndmask", [128, 4 * 512], BF16)
                nwmask = sbt(st, "nwmask", [128, 12 * 512], BF16)
                addm = sbt(st, "addm", [128, 256], F32)
                Dtb = Dep()
                Dz = Dep()
                c.op("dve", lambda e: e.memset(gates[:], 0.0), writes=[Dz])
                c.dma("sp", gates[0:48, :], s_gt, reads=[Dz], writes=[Dtb])
                for dst_, src_ in ((sel48, t_sel48), (ncmask, t_ncmask), (overlap, t_overlap), (expand, t_expand),
                                   (ndmask, t_ndmask), (nwmask, t_nwmask), (addm, t_addmask)):
                    c.dma("sp", dst_[:], src_, writes=[Dtb])
                NG = 2
                qg = [sbt(st, "qg%d" % i, [128, 4, TO], BF16) for i in range(NG)]
                qrg = [sbt(st, "qrg%d" % i, [128, 4, TO], BF16) for i in range(NG)]
                ngg = [sbt(st, "ngg%d" % i, [128, 4, TO], BF16) for i in range(NG)]
                ksg = [sbt(st, "ksg%d" % i, [128, T], BF16) for i in range(NG)]
                kwg = [sbt(st, "kwg%d" % i, [128, T], BF16) for i in range(NG)]
                vsg = [sbt(st, "vsg%d" % i, [128, 16, 128], BF16) for i in range(NG)]
                vwg = [sbt(st, "vwg%d" % i, [128, 16, 128], BF16) for i in range(NG)]
                yst = [sbt(st, "yst%d" % i, [128, 4, TO], BF16) for i in range(NG)]
                Dgl = [Dep() for _ in range(NG)]
                Dyst = [Dep() for _ in range(NG)]
                NE = 6
                Et = [sbt(st, "Et%d" % i, [128, 512], BF16) for i in range(NE)]
                DEt = [Dep() for _ in range(NE)]
                Ec = [sbt(st, "Ec%d" % i, [128, 512], BF16) for i in range(3)]
                Pn = [sbt(st, "Pn%d" % i, [128, 512], BF16) for i in range(3)]
                oacc = [sbt(st, "oacc%d" % i, [128, 512], F32) for i in range(3)]
                otmp = [sbt(st, "otmp%d" % i, [128, 512], F32) for i in range(2)]
                rden0 = [sbt(st, "rden0%d" % i, [128, 512], F32) for i in range(3)]
                rdenB = [sbt(st, "rdenB%d" % i, [128, 512], F32) for i in range(2)]
                gsb = [[sbt(st, "gsb%d_%d" % (i, j), [128, 512], F32) for j in range(3)] for i in range(3)]
                negsel = [sbt(st, "negsel%d" % i, [128, 512], BF16) for i in range(3)]
                imp2 = sbt(st, "imp2", [128, 32], F32)
                top8 = sbt(st, "top8", [128, 8], F32)
                selF = sbt(st, "selF", [128, 128], F32)
                tden = sbt(st, "tden", [128, 512], F32)
                Dimp2, Dtop8, DselF, Dtden = [Dep() for _ in range(4)]
                DEc = [Dep() for _ in range(3)]
                DPn = [Dep() for _ in range(3)]
                Doacc = [Dep() for _ in range(3)]
                Dotmp = [Dep(), Dep()]
                Drden0 = [Dep() for _ in range(3)]
                DrdenB = [Dep(), Dep()]
                Dgsb = [[Dep() for _ in range(3)] for _ in range(3)]
                Dnegsel = [Dep() for _ in range(3)]
                c.op("dve", lambda e: e.memset(selF[:], 0.0), writes=[DselF])
                rot = [0, 0]

                def sc_next():
                    rot[0] = (rot[0] + 1) % 4
                    return rot[0]

                def E_next():
                    rot[1] = (rot[1] + 1) % NE
                    return rot[1]

                def run_steps(steps, qsrc, ksrc, vsrc, Dg_, pO, pD, stages, depth=3):
                    n = len(steps)
                    slots = []

                    def issue_S(k):
                        kt, extras = steps[k]
                        bk = sc_next()
                        e_ = E_next()
                        c.op("pe", lambda e: e.matmul(ps[bk][:, :], lhsT=ksrc[:, kt * 128:(kt + 1) * 128], rhs=qsrc, start=True,
                                                      stop=(len(extras) == 0)), reads=[Dg_], writes=[Dps[bk]])
                        for j, (l_ap, r_ap, rd) in enumerate(extras):
                            c.op("pe", lambda e: e.matmul(ps[bk][:, :], lhsT=l_ap, rhs=r_ap, start=False, stop=(j == len(extras) - 1)),
                                 reads=rd, writes=[Dps[bk]])
                        c.op("act", lambda e: e.activation(out=Et[e_][:], in_=ps[bk][:, :], func=AF.Exp, scale=SCALE),
                             reads=[Dps[bk]], writes=[DEt[e_]])
                        slots.append(e_)

                    for k in range(min(depth, n)):
                        issue_S(k)
                    for k in range(n):
                        if k + depth < n:
                            issue_S(k + depth)
                        e_ = slots[k]
                        kt = steps[k][0]
                        c.op("pe", lambda e: e.matmul(ps[pO][:, :], lhsT=vsrc[:, kt, :], rhs=Et[e_][:, :], start=(k == 0), stop=(k == n - 1)),
                             reads=[Dg_, DEt[e_]], writes=[Dps[pO]])
                        c.op("pe", lambda e: e.matmul(ps[pD][:, :], lhsT=ones[:, :], rhs=Et[e_][:, :], start=(k == 0), stop=(k == n - 1)),
                             reads=[Dconst, DEt[e_]], writes=[Dps[pD]])
                        if stages:
                            stages.pop(0)()

                def neg_recip(src_ap, Dsrc, rb, Drb, tA, DtA):
                    c.op("dve", lambda e: e.tensor_scalar_max(out=tA[:], in0=src_ap, scalar1=1e-30), reads=[Dsrc], writes=[DtA])
                    c.op("act", lambda e: e.activation(out=rb[:], in_=tA[:], func=AF.Ln), reads=[DtA], writes=[Drb])
                    c.op("act", lambda e: e.activation(out=rb[:], in_=rb[:], func=AF.Exp, scale=-1.0), reads=[Drb], writes=[Drb])
                    c.op("dve", lambda e: e.tensor_tensor(out=tA[:], in0=tA[:], in1=rb[:], op=ALU.mult), reads=[DtA, Drb], writes=[DtA])
                    c.op("dve", lambda e: e.scalar_tensor_tensor(out=rb[:], in0=tA[:], scalar=2.0, in1=rb[:], op0=ALU.subtract, op1=ALU.mult),
                         reads=[DtA, Drb], writes=[Drb])

                def finish(pO, pD, gs, Dgs, ob, oa):
                    rb, Drb = rdenB[ob], DrdenB[ob]
                    neg_recip(ps[pD][:, :], Dps[pD], rb, Drb, otmp[ob], Dotmp[ob])
                    c.op("dve", lambda e: e.scalar_tensor_tensor(out=rb[:], in0=rb[:], scalar=-1.0, in1=gs[:], op0=ALU.mult, op1=ALU.mult),
                         reads=[Drb, Dgs], writes=[Drb])
                    c.op("dve", lambda e: e.tensor_tensor(out=otmp[ob][:], in0=ps[pO][:, :], in1=rb[:], op=ALU.mult),
                         reads=[Dps[pO], Drb], writes=[Dotmp[ob]])
                    c.op("pool", lambda e: e.tensor_tensor(out=oacc[oa][:], in0=oacc[oa][:], in1=otmp[ob][:], op=ALU.add),
                         reads=[Doacc[oa], Dotmp[ob]], writes=[Doacc[oa]])

                def load_group(g):
                    b = g % NG
                    c.dma("sp", qg[b][:], s_q[4 * g:4 * g + 4].rearrange("n p t -> p n t"), writes=[Dgl[b]])
                    c.dma("sp", qrg[b][:], s_qr[4 * g:4 * g + 4].rearrange("n p t -> p n t"), writes=[Dgl[b]])
                    c.dma("sp", ngg[b][:], s_ng[4 * g:4 * g + 4].rearrange("n p t -> p n t"), writes=[Dgl[b]])
                    c.dma("sp", ksg[b][:], s_ks[g], writes=[Dgl[b]])
                    c.dma("sp", kwg[b][:], s_kw[g], writes=[Dgl[b]])
                    c.dma("sp", vsg[b][:], s_vs[:, :, g * 128:(g + 1) * 128], writes=[Dgl[b]])
                    c.dma("sp", vwg[b][:], s_vw[:, :, g * 128:(g + 1) * 128], writes=[Dgl[b]])

                def prep(g, i):
                    b = g % NG
                    pb = (g * 8 + i) % 3
                    tsl = slice(i * 128, (i + 1) * 128)
                    q_t = qg[b][:, :, tsl]
                    for br in range(3):
                        bk = sc_next()
                        for n in range(4):
                            r = br * 16 + g * 4 + n
                            c.op("pe", lambda e: e.matmul(ps[bk][:, n * 128:(n + 1) * 128], lhsT=sel48[:, r * 128:(r + 1) * 128],
                                                          rhs=gates[:, tsl], start=True, stop=True), reads=[Dtb], writes=[Dps[bk]])
                        c.op("dve", lambda e: e.tensor_copy(out=gsb[pb][br][:], in_=ps[bk][:, :]), reads=[Dps[bk]], writes=[Dgsb[pb][br]])
                    bc = sc_next()
                    c.op("pe", lambda e: e.matmul(ps[bc][:, :], lhsT=kcomp[:, g, :], rhs=q_t, start=True, stop=False),
                         reads=[Dkcomp, Dgl[b]], writes=[Dps[bc]])
                    c.op("pe", lambda e: e.matmul(ps[bc][:, :], lhsT=ident[:, :], rhs=ncmask[:, i * 512:(i + 1) * 512], start=False, stop=True),
                         reads=[Dtb, Dconst], writes=[Dps[bc]])
                    c.op("act", lambda e: e.activation(out=Ec[pb][:], in_=ps[bc][:, :], func=AF.Exp, scale=SCALE),
                         reads=[Dps[bc]], writes=[DEc[pb]])

                    def stage_den():
                        bd = sc_next()
                        c.op("pe", lambda e: e.matmul(ps[bd][:, :], lhsT=ones[:, :], rhs=Ec[pb][:, :], start=True, stop=True),
                             reads=[Dconst, DEc[pb]], writes=[Dps[bd]])
                        neg_recip(ps[bd][:, :], Dps[bd], rden0[pb], Drden0[pb], tden, Dtden)
                        c.op("dve", lambda e: e.scalar_tensor_tensor(out=Pn[pb][:], in0=Ec[pb][:], scalar=-1.0, in1=rden0[pb][:],
                                                                     op0=ALU.mult, op1=ALU.mult),
                             reads=[DEc[pb], Drden0[pb]], writes=[DPn[pb]])

                    def stage_imp():
                        bi = sc_next()
                        for n in range(4):
                            c.op("pe", lambda e: e.matmul(ps[bi][:, 0:32], lhsT=Pn[pb][:, n * 128:(n + 1) * 128], rhs=overlap[:, :],
                                                          start=(n == 0), stop=(n == 3)), reads=[DPn[pb], Dtb], writes=[Dps[bi]])
                        bo = sc_next()
                        c.op("pe", lambda e: e.matmul(ps[bo][:, :], lhsT=vcomp[:, g, :], rhs=Pn[pb][:, :], start=True, stop=True),
                             reads=[Dvcomp, DPn[pb]], writes=[Dps[bo]])
                        c.op("dve", lambda e: e.tensor_tensor(out=imp2[:], in0=ps[bi][:, 0:32], in1=addm[:, i * 32:(i + 1) * 32], op=ALU.add),
                             reads=[Dps[bi], Dtb], writes=[Dimp2])
                        c.op("dve", lambda e: e.max(out=top8[:], in_=imp2[:]), reads=[Dimp2], writes=[Dtop8])
                        c.op("dve", lambda e: e.tensor_scalar_max(out=top8[:, 7:8], in0=top8[:, 7:8], scalar1=-1e29), reads=[Dtop8], writes=[Dtop8])
                        c.op("dve", lambda e: e.tensor_scalar(out=selF[:, 0:32], in0=imp2[:], scalar1=top8[:, 7:8], scalar2=None, op0=ALU.is_ge),
                             reads=[Dimp2, Dtop8], writes=[DselF])
                        c.op("dve", lambda e: e.tensor_tensor(out=oacc[pb][:], in0=ps[bo][:, :], in1=gsb[pb][0][:], op=ALU.mult),
                             reads=[Dps[bo], Dgsb[pb][0]], writes=[Doacc[pb]])
                        if debug and g == 0:
                            c.dma("sp", s_dbg[:, 1024 + i * 32:1024 + (i + 1) * 32], imp2[:], reads=[Dimp2], writes=[Dep()])

                    def stage_tr():
                        bt = sc_next()
                        c.op("pe", lambda e: e.transpose(out=ps[bt][:, 0:128], in_=selF[:, :], identity=identf[:, :]),
                             reads=[DselF, Dconst], writes=[Dps[bt]])
                        c.op("dve", lambda e: e.tensor_scalar(out=negsel[pb][:, :].rearrange("p (a b) -> p a b", b=128),
                                                              in0=ps[bt][:, 0:128].unsqueeze(1).to_broadcast([128, 4, 128]),
                                                              scalar1=-1.0, scalar2=NEGM, op0=ALU.add, op1=ALU.mult),
                             reads=[Dps[bt]], writes=[Dnegsel[pb]])

                    return [stage_den, None, None, None, stage_imp, None, None, stage_tr]

                tiles = [(g, i) for g in range(4) for i in range(8)]
                DEPTH = 3
                pending = []
                cur_stages = []

                def make_step(kt, extras, qsrc, ksrc, vsrc, Dg_, pO, pD, first, last, post):
                    def issue_S():
                        bk = sc_next()
                        e_ = E_next()
                        c.op("pe", lambda e: e.matmul(ps[bk][:, :], lhsT=ksrc[:, kt * 128:(kt + 1) * 128], rhs=qsrc, start=True,
                                                      stop=(len(extras) == 0)), reads=[Dg_], writes=[Dps[bk]])
                        for j, (l_ap, r_ap, rd) in enumerate(extras):
                            c.op("pe", lambda e: e.matmul(ps[bk][:, :], lhsT=l_ap, rhs=r_ap, start=False, stop=(j == len(extras) - 1)),
                                 reads=rd, writes=[Dps[bk]])
                        c.op("act", lambda e: e.activation(out=Et[e_][:], in_=ps[bk][:, :], func=AF.Exp, scale=SCALE),
                             reads=[Dps[bk]], writes=[DEt[e_]])
                        return e_

                    def issue_PV(e_):
                        c.op("pe", lambda e: e.matmul(ps[pO][:, :], lhsT=vsrc[:, kt, :], rhs=Et[e_][:, :], start=first, stop=last),
                             reads=[Dg_, DEt[e_]], writes=[Dps[pO]])
                        c.op("pe", lambda e: e.matmul(ps[pD][:, :], lhsT=ones[:, :], rhs=Et[e_][:, :], start=first, stop=last),
                             reads=[Dconst, DEt[e_]], writes=[Dps[pD]])
                        if post is not None:
                            post()
                    return issue_S, issue_PV

                def pop_one():
                    issue_PV, e_ = pending.pop(0)
                    issue_PV(e_)
                    if cur_stages:
                        (cur_stages.pop(0) or (lambda: None))()

                def push(step):
                    issue_S, issue_PV = step
                    pending.append((issue_PV, issue_S()))
                    if len(pending) > DEPTH:
                        pop_one()

                load_group(0)
                cur_stages.extend(prep(0, 0))
                while cur_stages:
                    (cur_stages.pop(0) or (lambda: None))()
                for ti, (g, i) in enumerate(tiles):
                    b = g % NG
                    p = i & 1
                    oa = ti % 3
                    tsl = slice(i * 128, (i + 1) * 128)
                    qr_t = qrg[b][:, :, tsl]
                    while cur_stages:
                        (cur_stages.pop(0) or (lambda: None))()
                    if i == 1 and g + 1 < 4:
                        load_group(g + 1)
                    if ti + 1 < len(tiles):
                        cur_stages.extend(prep(*tiles[ti + 1]))
                    wsteps = []
                    for o in range(6):
                        kt = 2 * i - 4 + o
                        if kt < 0:
                            continue
                        ex = [] if o in (2, 3) else [(ident[:, :], nwmask[:, (p * 6 + o) * 512:(p * 6 + o + 1) * 512], [Dtb, Dconst])]
                        wsteps.append((kt, ex))
                    for k, (kt, ex) in enumerate(wsteps):
                        lastw = (k == len(wsteps) - 1)
                        post = (lambda oa=oa: finish(4, 5, gsb[oa][2], Dgsb[oa][2], 0, oa)) if lastw else None
                        push(make_step(kt, ex, qr_t, kwg[b], vwg[b], Dgl[b], 4, 5, k == 0, lastw, post))
                    nsl = 2 * i + 2
                    for kt in range(nsl):
                        o = kt - 2 * i
                        ex = [(expand[:, kt * 128:(kt + 1) * 128], negsel[oa][:, :], [Dtb, Dnegsel[oa]])]
                        if o >= 0:
                            ex.append((ident[:, :], ndmask[:, (p * 2 + o) * 512:(p * 2 + o + 1) * 512], [Dtb, Dconst]))
                        lasts = (kt == nsl - 1)

                        def post_s(oa=oa, b=b, tsl=tsl, g=g, i=i):
                            finish(6, 7, gsb[oa][1], Dgsb[oa][1], 1, oa)
                            c.op("pool", lambda e: e.tensor_tensor(out=yst[b][:, :, tsl], in0=oacc[oa][:, :].rearrange("p (a b) -> p a b", b=128),
                                                                   in1=ngg[b][:, :, tsl], op=ALU.mult),
                                 reads=[Doacc[oa], Dgl[b]], writes=[Dyst[b]])
                            if i == 7:
                                c.dma("pool", s_y[:, 16 + 4 * g:16 + 4 * g + 4, :], yst[b][:], reads=[Dyst[b]], writes=[Dep()])
                        push(make_step(kt, ex, qr_t, ksg[b], vsg[b], Dgl[b], 6, 7, kt == 0, lasts, post_s if lasts else None))
                while pending:
                    pop_one()
                while cur_stages:
                    (cur_stages.pop(0) or (lambda: None))()
                ps.pop()
                Dps.pop()
            c.barrier()

    def out_proj(s_ysrc, w_out, x_src, dst, final_gain, prefix):
        with ExitStack() as st:
            wo = sbt(st, prefix + "wo", [128, 32, D], BF16)
            Dwo = [Dep() for _ in range(4)]
            wv = w_out.rearrange("(kc p) n -> p kc n", p=128)
            for nb in range(4):
                for h in range(8):
                    c.dma("pool", wo[:, 4 * h:4 * h + 4, nb * 512:(nb + 1) * 512], wv[:, 4 * h:4 * h + 4, nb * 512:(nb + 1) * 512],
                          writes=[Dwo[nb]])
            yT = [sbt(st, prefix + "yT%d" % i, [128, 32, 128], BF16) for i in range(2)]
            xt = [sbt(st, prefix + "xt%d" % i, [128, D], F32) for i in range(2)]
            xo = xt
            DyT = [Dep(), Dep()]
            Dxt = [Dep(), Dep()]
            Dxo = Dxt
            if final_gain is not None:
                gbc = sbt(st, prefix + "gbc", [128, D], F32)
                junk = sbt(st, prefix + "junk", [128, D], BF16)
                stat = sbt(st, prefix + "stat", [128, 4], F32)
                Dg, Dj, Dst = Dep(), Dep(), Dep()
                c.dma("sp", gbc[:], final_gain.to_broadcast([128, D]), writes=[Dg])
            def load_t(tt):
                b = tt % 2
                c.dma("sp", yT[b][:], s_ysrc[:, :, tt * 128:(tt + 1) * 128].rearrange("k p t -> p k t"), writes=[DyT[b]])
                c.dma("sp", xt[b][:], x_src[tt * 128:(tt + 1) * 128, :], writes=[Dxt[b]])

            def group_t(tt, nb):
                b = tt % 2
                pi = next_ps()
                mm_group(pi, 0, 512, lambda kc: yT[b][:, kc, :], lambda kc: wo[:, kc, nb * 512:(nb + 1) * 512], [DyT[b], Dwo[nb]], nk=32)
                c.op("dve", lambda e: e.tensor_tensor(out=xo[b][:, nb * 512:(nb + 1) * 512], in0=ps[pi][:, :],
                                                      in1=xt[b][:, nb * 512:(nb + 1) * 512], op=ALU.add),
                     reads=[Dps[pi], Dxt[b]], writes=[Dxo[b]])

            def finish_t(tt):
                b = tt % 2
                if final_gain is not None:
                    c.op("act", lambda e: e.activation(out=junk[:], in_=xo[b][:], func=AF.Square, accum_out=stat[:, 0:1]),
                         reads=[Dxo[b]], writes=[Dj, Dst])
                    c.op("act", lambda e: e.activation(out=stat[:, 1:2], in_=stat[:, 0:1], func=AF.Sqrt, scale=1.0 / D, bias=epsc[:]),
                         reads=[Dst, Dconst], writes=[Dst])
                    c.op("dve", lambda e: e.reciprocal(out=stat[:, 2:3], in_=stat[:, 1:2]), reads=[Dst], writes=[Dst])
                    c.op("dve", lambda e: e.scalar_tensor_tensor(out=xo[b][:], in0=xo[b][:], scalar=stat[:, 2:3], in1=gbc[:],
                                                                 op0=ALU.mult, op1=ALU.mult), reads=[Dxo[b], Dst, Dg], writes=[Dxo[b]])
                c.dma("pool", dst[tt * 128:(tt + 1) * 128, :], xo[b][:], reads=[Dxo[b]], writes=[Dep()])

            load_t(0)
            load_t(1)
            for nb in range(4):
                group_t(0, nb)
                group_t(1, nb)
            finish_t(0)
            finish_t(1)
            for tt in range(2, 8):
                load_t(tt)
                for nb in range(4):
                    group_t(tt, nb)
                finish_t(tt)
            c.barrier()

    def out_proj_nb(s_ysrc, w_out, x_src, dst, prefix):
        with ExitStack() as st:
            yT = sbt(st, prefix + "yT", [128, 32, TO], BF16)
            DyT = [Dep()] * 8
            for j in range(4):
                c.dma("sp", yT[:, 8 * j:8 * j + 8, :], s_ysrc[:, 8 * j:8 * j + 8, :], writes=[DyT[0]])
            wo = [sbt(st, prefix + "wo%d" % i, [128, 32, 512], BF16) for i in range(3)]
            Dwo = [Dep() for _ in range(3)]
            wv = w_out.rearrange("(kc p) n -> p kc n", p=128)
            xp = [sbt(st, prefix + "xp%d" % i, [128, 512], F32) for i in range(4)]
            Dxp = [Dep() for _ in range(4)]
            xi = 0
            for nb in range(4):
                k = nb % 3
                for h in range(8):
                    c.dma("pool", wo[k][:, 4 * h:4 * h + 4, :], wv[:, 4 * h:4 * h + 4, nb * 512:(nb + 1) * 512], writes=[Dwo[k]])
                for tt in range(8):
                    b = xi % 4
                    xi += 1
                    c.dma("sp", xp[b][:], x_src[tt * 128:(tt + 1) * 128, nb * 512:(nb + 1) * 512], writes=[Dxp[b]])
                    pi = next_ps()
                    mm_group(pi, 0, 512, lambda kc: yT[:, kc, tt * 128:(tt + 1) * 128], lambda kc: wo[k][:, kc, :], [DyT[tt], Dwo[k]], nk=32)
                    c.op("dve", lambda e: e.tensor_tensor(out=xp[b][:], in0=ps[pi][:, :], in1=xp[b][:], op=ALU.add),
                         reads=[Dps[pi], Dxp[b]], writes=[Dxp[b]])
                    c.dma("act", dst[tt * 128:(tt + 1) * 128, nb * 512:(nb + 1) * 512], xp[b][:], reads=[Dxp[b]], writes=[Dep()])
            c.barrier()

    if stop >= 5:
        out_proj_nb(s_y, w_out_even, x_own, s_x1, "d")

    if stop >= 6:
        with ExitStack() as st:
            hT = sbt(st, "hT1", [128, 16, TO], BF16)
            vsb = sbt(st, "vsb", [128, 8, 4096], BF16)
            Dv = [Dep() for _ in range(8)]
            lg = sbt(st, "lg", [128, 32], F32)
            lb = sbt(st, "lb", [128, 32], F32)
            wsT = sbt(st, "wsT", [128, 16, 128], BF16)
            rsbc = sbt(st, "rsbc", [128, 16, 128], F32)
            bsbc = sbt(st, "bsbc", [128, 16, 128], F32)
            ssum = sbt(st, "fssum", [128, 8, 8], F32)
            ssq = sbt(st, "fssq", [128, 8, 8], F32)
            sqj = sbt(st, "fsqj", [128, 512], BF16)
            stat = sbt(st, "fstat", [128, 8, 8], F32)
            st2 = ExitStack()
            lgr = sbt(st2, "lgr", [128, 128], F32)
            lbr = sbt(st2, "lbr", [128, 128], F32)
            wsr = sbt(st2, "wsr", [128, 16, 128], F32)
            tril = sbt(st2, "tril", [128, 128], F32)
            wsm = sbt(st2, "wsm", [128, 16, 128], BF16)
            Dlg, Dws, DwsT, Drs, Dbs = [Dep() for _ in range(5)]
            Dz = Dep()
            c.op("dve", lambda e: e.memset(lgr[:], 0.0), writes=[Dz])
            c.op("dve", lambda e: e.memset(lbr[:], 0.0), writes=[Dz])
            c.dma("sp", lgr[0:32, :], ln_g, reads=[Dz], writes=[Dlg])
            c.dma("sp", lbr[0:32, :], ln_b, reads=[Dz], writes=[Dlg])
            c.dma("sp", wsr[:], w_s.rearrange("g t s -> t g s"), writes=[Dws])
            c.dma("sp", tril[:], t_tril, writes=[Dws])
            c.dma("sp", bsbc[:].rearrange("p a b -> p (a b)"), b_s.to_broadcast([128, 2048]), writes=[Dbs])
            c.op("pe", lambda e: e.transpose(out=ps[6][:, 0:128], in_=lgr[:, :], identity=identf[:, :]), reads=[Dlg, Dconst], writes=[Dps[6]])
            c.op("pe", lambda e: e.transpose(out=ps[6][:, 128:256], in_=lbr[:, :], identity=identf[:, :]), reads=[Dlg, Dconst], writes=[Dps[6]])
            c.op("dve", lambda e: e.tensor_copy(out=lg[:], in_=ps[6][:, 0:32]), reads=[Dps[6]], writes=[Dlg])
            c.op("dve", lambda e: e.tensor_copy(out=lb[:], in_=ps[6][:, 128:160]), reads=[Dps[6]], writes=[Dlg])
            c.op("dve", lambda e: e.tensor_tensor(out=wsm[:], in0=wsr[:], in1=tril[:, :].unsqueeze(1).to_broadcast([128, 16, 128]), op=ALU.mult),
                 reads=[Dws], writes=[Dws])
            for a in range(2):
                for j in range(8):
                    c.op("pe", lambda e: e.transpose(out=psb[:, j * 128:(j + 1) * 128], in_=wsm[:, 8 * a + j, :], identity=ident[:, :]),
                         reads=[Dws, Dconst], writes=[Dpsb])
                c.op("dve", lambda e: e.tensor_copy(out=wsT[:, 8 * a:8 * a + 8, :], in_=psb[:, :].rearrange("p (a b) -> p a b", b=128)),
                     reads=[Dpsb], writes=[DwsT])
            for a in range(4):
                c.op("pe", lambda e: e.matmul(ps[6][:, :], lhsT=ones[:, :], rhs=wsT[:, 4 * a:4 * a + 4, :], start=True, stop=True),
                     reads=[DwsT, Dconst], writes=[Dps[6]])
                c.op("dve", lambda e: e.tensor_copy(out=rsbc[:, 4 * a:4 * a + 4, :], in_=ps[6][:, :].rearrange("p (a b) -> p a b", b=128)),
                     reads=[Dps[6]], writes=[Drs])
            DhT = [Dep() for _ in range(8)]
            pA, pB = norm_parts(st2, s_x1, norm_odd, hT, DhT, "e")
            w0 = sbt(st2, "fvw0", [128, 16, 512], BF16)
            Dw0 = Dep()
            Dss, Dsq, Dsqj, Dst = Dep(), Dep(), Dep(), Dep()
            wv_ = w_in_odd.rearrange("(kc p) n -> p kc n", p=128)
            for h in range(4):
                c.dma("pool", w0[:, 4 * h:4 * h + 4, :], wv_[:, 4 * h:4 * h + 4, 4096:4096 + 512], writes=[Dw0])

            def v_group(vb, tt, wt, Dw):
                pi = next_ps()
                mm_group(pi, 0, 512, lambda kc: hT[:, kc, tt * 128:(tt + 1) * 128], lambda kc: wt[:, kc, :], [Dw, DhT[tt]])
                c.op("dve", lambda e: e.tensor_scalar(out=vsb[:, tt, vb * 512:(vb + 1) * 512], in0=ps[pi][:, :], scalar1=1.0, scalar2=0.0,
                                                      op0=ALU.mult, op1=ALU.add, accum_out=ssum[:, tt, vb:vb + 1]),
                     reads=[Dps[pi]], writes=[Dv[tt], Dss])
                c.op("act", lambda e: e.activation(out=sqj[:], in_=ps[pi][:, :], func=AF.Square, accum_out=ssq[:, tt, vb:vb + 1]),
                     reads=[Dps[pi]], writes=[Dsqj, Dsq])

            pA(0)
            pB(0)
            pA(1)
            for tt in range(8):
                if tt + 1 < 8:
                    pB(tt + 1)
                if tt + 2 < 8:
                    pA(tt + 2)
                v_group(0, tt, w0, Dw0)
            c.barrier()
            st2.close()
            ws = WS(st, "fw", n=4)
            for vb in range(1, 8):
                wt, Dw = ws.load(w_in_odd, [(4096 + vb * 512, 512, 0)])
                for tt in range(8):
                    v_group(vb, tt, wt, Dw)
            for tt in range(8):
                s_ = stat[:, tt, :]
                c.op("dve", lambda e: e.reduce_sum(out=s_[:, 0:1], in_=ssum[:, tt, :], axis=AX.X), reads=[Dss], writes=[Dst])
                c.op("dve", lambda e: e.reduce_sum(out=s_[:, 1:2], in_=ssq[:, tt, :], axis=AX.X), reads=[Dsq], writes=[Dst])
                c.op("dve", lambda e: e.tensor_scalar(out=s_[:, 2:3], in0=s_[:, 0:1], scalar1=1.0 / 4096, scalar2=None, op0=ALU.mult),
                     reads=[Dst], writes=[Dst])
                c.op("dve", lambda e: e.tensor_tensor(out=s_[:, 3:4], in0=s_[:, 2:3], in1=s_[:, 2:3], op=ALU.mult), reads=[Dst], writes=[Dst])
                c.op("dve", lambda e: e.scalar_tensor_tensor(out=s_[:, 4:5], in0=s_[:, 1:2], scalar=1.0 / 4096, in1=s_[:, 3:4],
                                                             op0=ALU.mult, op1=ALU.subtract), reads=[Dst], writes=[Dst])
                c.op("act", lambda e: e.activation(out=s_[:, 5:6], in_=s_[:, 4:5], func=AF.Sqrt, scale=1.0, bias=epsc[:]),
                     reads=[Dst, Dconst], writes=[Dst])
                c.op("dve", lambda e: e.reciprocal(out=s_[:, 6:7], in_=s_[:, 5:6]), reads=[Dst], writes=[Dst])
            for tt in range(8):
                s_ = stat[:, tt, :]
                eng = "dve"
                c.op(eng, lambda e: e.tensor_scalar(out=vsb[:, tt, :], in0=vsb[:, tt, :], scalar1=s_[:, 2:3], scalar2=s_[:, 6:7],
                                                    op0=ALU.subtract, op1=ALU.mult), reads=[Dv[tt], Dst], writes=[Dv[tt]])
            B2 = sbt(st, "B2", [128, 128], F32)
            szS = sbt(st, "szS", [128, 512], F32)
            m1 = sbt(st, "m1", [128, 512], F32)
            y2 = [sbt(st, "y2S%d" % i, [128, TO], BF16) for i in range(2)]
            DB2, Dsz, Dm1 = Dep(), Dep(), Dep()
            Dy2 = [Dep(), Dep()]
            for cb4 in range(8):
                wt, Dw = ws.load(w_in_odd, [(cb4 * 512, 512, 0)])
                wt2, Dw2 = ws.load(w_in_odd, [(8192 + cb4 * 512, 512, 0)])
                for j in range(4):
                    ct = cb4 * 4 + j
                    g = ct // 2
                    b = ct % 2
                    c.op("dve", lambda e: e.scalar_tensor_tensor(out=B2[:], in0=rsbc[:, g, :], scalar=lb[:, ct:ct + 1], in1=bsbc[:, g, :],
                                                                 op0=ALU.mult, op1=ALU.add), reads=[Drs, Dbs, Dlg], writes=[DB2])
                    for th in range(2):
                        tsl = slice(th * 512, (th + 1) * 512)
                        pu = next_ps()
                        mm_group(pu, 0, 512, lambda kc: wt[:, kc, j * 128:(j + 1) * 128], lambda kc: hT[:, kc, tsl], [Dw] + DhT[4 * th:4 * th + 4])
                        pz = next_ps()
                        mm_group(pz, 0, 512, lambda kc: wt2[:, kc, j * 128:(j + 1) * 128], lambda kc: hT[:, kc, tsl], [Dw2] + DhT[4 * th:4 * th + 4])
                        c.op("act", lambda e: e.activation(out=szS[:], in_=ps[pz][:, :], func=AF.Silu), reads=[Dps[pz]], writes=[Dsz])
                        pm = next_ps()
                        for k4 in range(4):
                            tt = th * 4 + k4
                            c.op("pe", lambda e: e.matmul(ps[pm][:, k4 * 128:(k4 + 1) * 128], lhsT=vsb[:, tt, ct * 128:(ct + 1) * 128],
                                                          rhs=wsT[:, g, :], start=True, stop=True), reads=[Dv[tt], DwsT], writes=[Dps[pm]])
                        c.op("dve", lambda e: e.scalar_tensor_tensor(out=m1[:, :].rearrange("p (a b) -> p a b", b=128),
                                                                     in0=ps[pm][:, :].rearrange("p (a b) -> p a b", b=128),
                                                                     scalar=lg[:, ct:ct + 1],
                                                                     in1=B2[:, :].unsqueeze(1).to_broadcast([128, 4, 128]),
                                                                     op0=ALU.mult, op1=ALU.add), reads=[Dps[pm], DB2, Dlg], writes=[Dm1])
                        c.op("dve", lambda e: e.tensor_tensor(out=m1[:], in0=m1[:], in1=ps[pu][:, :], op=ALU.mult),
                             reads=[Dm1, Dps[pu]], writes=[Dm1])
                        c.op("dve", lambda e: e.tensor_tensor(out=y2[b][:, tsl], in0=m1[:], in1=szS[:], op=ALU.mult),
                             reads=[Dm1, Dsz], writes=[Dy2[b]])
                    c.dma("sp", s_y2[ct], y2[b][:], reads=[Dy2[b]], writes=[Dep()])
            c.barrier()

    if stop >= 7:
        out_proj(s_y2, w_out_odd, s_x1, out, norm_final, "g")

    c.barrier()
    top.close()
    c.close()
    return nc


def _bf(a):
    return np.asarray(a, dtype=np.float32).astype(ml_dtypes.bfloat16)


def own_tiles(hh):
    return [2 * i + (hh ^ (i & 1)) for i in range(8)]


def host_tables(hh):
    tiles = own_tiles(hh)
    own_pos = np.concatenate([np.arange(128 * t, 128 * t + 128) for t in tiles])
    half = 16
    inv_freq = np.power(np.float32(500000.0), -np.arange(half, dtype=np.float32) * np.float32(2.0) / np.float32(32)).astype(np.float32)

    def cs(pos):
        ang = pos.astype(np.float32)[:, None] * inv_freq[None, :]
        co = np.cos(ang).astype(np.float32).T
        si = np.sin(ang).astype(np.float32).T
        n = co.shape[1]
        return (np.ascontiguousarray(np.concatenate([co, co, np.ones((96, n), np.float32)], 0)),
                np.ascontiguousarray(np.concatenate([si, si, np.zeros((96, n), np.float32)], 0)))

    ca, sa = cs(np.arange(T))
    co, so = cs(own_pos)
    R = np.zeros((128, 128), np.float32)
    for m in range(16):
        R[m + 16, m] = -1.0
        R[m, m + 16] = 1.0
    tb = {"t_cos_all": ca, "t_sin_all": sa, "t_cos_own": co, "t_sin_own": so, "t_R": _bf(R),
          "t_ident": _bf(np.eye(128)), "t_identf": np.eye(128, dtype=np.float32), "t_ones": _bf(np.ones((128, 128)))}
    j = np.arange(32)
    am = np.zeros((1024, 32), np.float32)
    tpos = own_pos[:, None]
    forced = (j[None, :] == 0) | (j[None, :] == tpos // 64)
    causal = (64 * j[None, :]) <= tpos
    am[~causal] = -BIG
    am[forced] = BIG
    tb["t_addmask"] = np.ascontiguousarray(am.reshape(8, 128, 32).transpose(1, 0, 2).reshape(128, 256))
    cc = np.arange(128)[:, None]
    cm = ((cc < 127) & (16 * cc + 31 <= own_pos[None, :])).astype(np.float32)
    ncm = (cm.reshape(128, 8, 1, 128) - 1.0) * 30000.0
    tb["t_ncmask"] = _bf(np.broadcast_to(ncm, (128, 8, 4, 128)).reshape(128, 8 * 512))
    tb["t_overlap"] = _bf(((cc < 127) & (16 * cc < 64 * j[None, :] + 64) & (16 * cc + 32 > 64 * j[None, :])).astype(np.float32))
    tb["t_expand"] = _bf((np.arange(T)[None, :] // 64 == np.arange(128)[:, None]).astype(np.float32))
    tk = np.arange(128)[:, None]
    tq = np.arange(128)[None, :]
    dm = np.zeros((128, 4, 128), np.float32)
    for p in range(2):
        for o in range(2):
            r = o - (hh ^ p)
            dm[:, p * 2 + o, :] = (128 * r + tk <= tq)
    tb["t_ndmask"] = _bf(np.broadcast_to((dm.reshape(128, 4, 1, 128) - 1.0) * 30000.0, (128, 4, 4, 128)).reshape(128, 4 * 512))
    wm = np.zeros((128, 12, 128), np.float32)
    for p in range(2):
        for o in range(6):
            r = o - 4 - (hh ^ p)
            diff = tq - tk - 128 * r
            wm[:, p * 6 + o, :] = (diff >= 0) & (diff < 512)
    tb["t_nwmask"] = _bf(np.broadcast_to((wm.reshape(128, 12, 1, 128) - 1.0) * 30000.0, (128, 12, 4, 128)).reshape(128, 12 * 512))
    s48 = np.zeros((128, 48, 128), np.float32)
    for r in range(48):
        s48[r, r, :] = 1.0
    tb["t_sel48"] = _bf(s48.reshape(128, 48 * 128))
    tb["t_tril"] = np.tril(np.ones((128, 128), np.float32))
    return tb


def make_in_maps(inp):
    f = lambda a: np.ascontiguousarray(np.asarray(a, dtype=np.float32))
    x = f(inp["x"])
    shared = {
        "norm_even": f(inp["norm_even"]).reshape(1, D),
        "w_in_even": f(inp["w_in_even"]).reshape(D, L0),
        "conv_w": f(inp["conv_w"]).reshape(3, 16, 128).reshape(48, 128),
        "cmp_k_pos": f(inp["cmp_k_pos"]).reshape(32, 128), "cmp_k_w1": f(inp["cmp_k_w1"]).reshape(4096, 128),
        "cmp_k_b1": f(inp["cmp_k_b1"]).reshape(128, 1), "cmp_k_w2": f(inp["cmp_k_w2"]).reshape(128, 128),
        "cmp_v_pos": f(inp["cmp_v_pos"]).reshape(32, 128), "cmp_v_w1": f(inp["cmp_v_w1"]).reshape(4096, 128),
        "cmp_v_b1": f(inp["cmp_v_b1"]).reshape(128, 1), "cmp_v_w2": f(inp["cmp_v_w2"]).reshape(128, 128),
        "w_out_even": f(inp["w_out_even"]).reshape(4096, D),
        "norm_odd": f(inp["norm_odd"]).reshape(1, D),
        "w_in_odd": f(inp["w_in_odd"]).reshape(D, 12288),
        "sgu_ln_g": f(inp["sgu_ln_g"]).reshape(32, 128), "sgu_ln_b": f(inp["sgu_ln_b"]).reshape(32, 128),
        "sgu_w_s": f(inp["sgu_w_s"]).reshape(16, 128, 128), "sgu_b_s": f(inp["sgu_b_s"]).reshape(1, 2048),
        "w_out_odd": f(inp["w_out_odd"]).reshape(4096, D),
        "norm_final": f(inp["norm_final"]).reshape(1, D),
    }
    tabs = [host_tables(0), host_tables(1)]
    maps = []
    for cidx in range(8):
        b, hh = cidx // 2, cidx % 2
        tiles = own_tiles(hh)
        rows = np.concatenate([np.arange(128 * t, 128 * t + 128) for t in tiles])
        xh = np.zeros((128, D), np.float32)
        for i, t in enumerate(tiles):
            if t > 0:
                xh[2 * i:2 * i + 2] = x[b, 128 * t - 2:128 * t]
        m = dict(shared)
        m.update(tabs[hh])
        m["x_all"] = x[b]
        m["x_own"] = np.ascontiguousarray(x[b][rows])
        m["x_halo"] = xh
        maps.append(m)
    return maps


_NC = {}


def kernel(**inputs):
    if "nc" not in _NC:
        _NC["nc"] = build()
    maps = make_in_maps(inputs)
    res = run_bass_kernel_spmd(_NC["nc"], maps, core_ids=list(range(8)))
    outp = np.zeros((4, T, D), np.float32)
    for cidx in range(8):
        b, hh = cidx // 2, cidx % 2
        o = np.asarray(res.results[cidx]["out"], dtype=np.float32)
        for i, t in enumerate(own_tiles(hh)):
            outp[b, 128 * t:128 * t + 128] = o[128 * i:128 * i + 128]
    return outp
```
